# Optimizing a Trainium2 kernel written in Bass

```python
import jax, jax.numpy as jnp
from jax import lax
import numpy as np

D_MODEL = 2048
BATCH = 4
SEQ = 2048
DEPTH = 4
DEC_BATCH = 128
DEC_SEQ = 1
PAST_LEN = 16384
PAGE_SIZE = 128

N_EVEN = (DEPTH + 1) // 2
N_ODD = DEPTH // 2
D_A = D_MODEL // 2
CONV_A_W = 3
D_B = D_MODEL // 2
POOL_WINDOWS = (2, 4, 8, 16)
B_GROUP = D_B // len(POOL_WINDOWS)
POOL_HIST = max(POOL_WINDOWS) - 1
D_C = D_MODEL // 2
CHUNK = 128
C_HEADS = 8
C_HEAD_DIM = D_C // C_HEADS
D_D = D_MODEL // 2
CONF_W = 31
D_FF = -(-8 * D_MODEL // (3 * 256)) * 256
EPS = 1e-6

kernel_name = "hybrid_conv_pool_gmlp_conformer_decoder_step"


def rmsnorm(x, g):
    xf = x.astype(jnp.float32)
    y = xf * lax.rsqrt(jnp.mean(xf * xf, axis=-1, keepdims=True) + EPS)
    return (y * g.astype(jnp.float32)).astype(x.dtype)


def layernorm(x, g, b):
    xf = x.astype(jnp.float32)
    mu = jnp.mean(xf, axis=-1, keepdims=True)
    var = jnp.mean(jnp.square(xf - mu), axis=-1, keepdims=True)
    y = (xf - mu) * lax.rsqrt(var + EPS) * g.astype(jnp.float32) + b.astype(jnp.float32)
    return y.astype(x.dtype)


def causal_dwconv(hist, x, w):
    width = w.shape[0]
    full = jnp.concatenate([hist.astype(x.dtype), x], axis=1)
    y = lax.conv_general_dilated(full, w[:, None, :].astype(x.dtype), window_strides=(1,), padding='VALID',
                                 dimension_numbers=('NWC', 'WIO', 'NWC'), feature_group_count=x.shape[-1])
    return y, full[:, full.shape[1] - (width - 1):]


def multiscale_pool(hist, p, start_pos):
    T = p.shape[1]
    full_raw = jnp.concatenate([hist.astype(p.dtype), p], axis=1)
    full = full_raw.astype(jnp.float32)
    cs = jnp.concatenate([jnp.zeros_like(full[:, :1]), jnp.cumsum(full, axis=1)], axis=1)
    pos = start_pos + jnp.arange(T, dtype=jnp.int32)
    outs = []
    for g, w in enumerate(POOL_WINDOWS):
        sl = slice(g * B_GROUP, (g + 1) * B_GROUP)
        s = cs[:, POOL_HIST + 1:POOL_HIST + 1 + T, sl] - cs[:, POOL_HIST + 1 - w:POOL_HIST + 1 - w + T, sl]
        cnt = jnp.minimum(w, pos + 1).astype(jnp.float32)
        outs.append(s / cnt[None, :, None])
    pooled = jnp.concatenate(outs, axis=-1) - full[:, POOL_HIST:]
    return pooled.astype(p.dtype), full_raw[:, full_raw.shape[1] - POOL_HIST:]


def chunk_spatial_gate(u, v, w_s, b_s):
    B, T, _ = v.shape
    n_chunks = -(-T // CHUNK)
    pad = n_chunks * CHUNK - T
    vp = jnp.pad(v, ((0, 0), (0, pad), (0, 0))).reshape(B, n_chunks, CHUNK, C_HEADS, C_HEAD_DIM)
    mask = jnp.tril(jnp.ones((CHUNK, CHUNK), dtype=bool))
    ws = jnp.where(mask[None], w_s, 0.0).astype(v.dtype)
    mixed = jnp.einsum('hts,bnshd->bnthd', ws, vp) + b_s.T.astype(v.dtype)[None, None, :, :, None]
    mixed = mixed.reshape(B, n_chunks * CHUNK, D_C)[:, :T]
    return u * mixed, v[:, (n_chunks - 1) * CHUNK:]


def even_mixer(h, conv_hist, pool_hist, start_pos, w_in, w_conv, w_pool, pool_scale, w_out):
    proj = h @ w_in
    xin, g_pre, g_post, p = jnp.split(proj, [D_A, 2 * D_A, 3 * D_A], axis=-1)
    conv_y, new_conv = causal_dwconv(conv_hist, g_pre * xin, w_conv)
    y_a = g_post * conv_y
    pooled, new_pool = multiscale_pool(pool_hist, p, start_pos)
    B, T, _ = pooled.shape
    y_b = jnp.einsum('btgc,gcd->btgd', pooled.reshape(B, T, len(POOL_WINDOWS), B_GROUP), w_pool)
    y_b = y_b.reshape(B, T, D_B) * pool_scale
    return jnp.concatenate([y_a, y_b], axis=-1) @ w_out, new_conv, new_pool


def odd_mixer(h, conf_hist, w_in, v_norm_g, v_norm_b, w_spatial, b_spatial,
              w_conv, b_conv, conf_norm_g, conf_norm_b, w_out):
    proj = h @ w_in
    u, v, a, g = jnp.split(proj, [D_C, 2 * D_C, 2 * D_C + D_D], axis=-1)
    u = jax.nn.gelu(u)
    v = layernorm(jax.nn.gelu(v), v_norm_g, v_norm_b)
    y_c, new_v = chunk_spatial_gate(u, v, w_spatial, b_spatial)
    glu = a * jax.nn.sigmoid(g)
    conv_y, new_conf = causal_dwconv(conf_hist, glu, w_conv)
    y_d = jax.nn.silu(layernorm(conv_y + b_conv, conf_norm_g, conf_norm_b))
    return jnp.concatenate([y_c, y_d], axis=-1) @ w_out, new_v, new_conf


def swiglu(h, w_gate, w_up, w_down):
    return (jax.nn.silu(h @ w_gate) * (h @ w_up)) @ w_down


def run_trunk(x, conv_hist, pool_hist, conf_hist, start_pos,
              norm_mix, norm_ffn, w_in_even, w_conv_a, w_pool, pool_scale, w_out_even,
              w_in_odd, v_norm_g, v_norm_b, w_spatial, b_spatial, w_conv_d, b_conv_d,
              conf_norm_g, conf_norm_b, w_out_odd, w_ffn_gate, w_ffn_up, w_ffn_down, norm_final):
    conv_new, pool_new, chunk_new, conf_new = [], [], [], []
    for l in range(DEPTH):
        i = l // 2
        h = rmsnorm(x, norm_mix[l])
        if l % 2 == 0:
            y, nc, npl = even_mixer(h, conv_hist[i], pool_hist[i], start_pos,
                                    w_in_even[i], w_conv_a[i], w_pool[i], pool_scale[i], w_out_even[i])
            conv_new.append(nc)
            pool_new.append(npl)
        else:
            y, nv, ncf = odd_mixer(h, conf_hist[i], w_in_odd[i], v_norm_g[i], v_norm_b[i],
                                   w_spatial[i], b_spatial[i], w_conv_d[i], b_conv_d[i],
                                   conf_norm_g[i], conf_norm_b[i], w_out_odd[i])
            chunk_new.append(nv)
            conf_new.append(ncf)
        x = x + y
        x = x + swiglu(rmsnorm(x, norm_ffn[l]), w_ffn_gate[l], w_ffn_up[l], w_ffn_down[l])
    y = rmsnorm(x, norm_final)
    return y, jnp.stack(conv_new), jnp.stack(pool_new), jnp.stack(chunk_new), jnp.stack(conf_new)


def setup_inputs(seed: int = 0) -> dict:
    key = jax.random.key(seed)
    ks = jax.random.split(key, 32)
    f32 = jnp.float32

    def nrm(k, shape, scale):
        return jax.random.normal(k, shape, f32) * scale

    return {
        "x_prompt": nrm(ks[0], (BATCH, SEQ, D_MODEL), 1.0),
        "x_sample": nrm(ks[1], (DEC_BATCH, DEC_SEQ, D_MODEL), 1.0),
        "state_conv_a": nrm(ks[2], (N_EVEN, DEC_BATCH, CONV_A_W - 1, D_A), 0.5),
        "state_pool": nrm(ks[3], (N_EVEN, DEC_BATCH, POOL_HIST, D_B), 0.5),
        "state_conformer": nrm(ks[4], (N_ODD, DEC_BATCH, CONF_W - 1, D_D), 0.5),
        "norm_mix": 1.0 + nrm(ks[5], (DEPTH, D_MODEL), 0.05),
        "norm_ffn": 1.0 + nrm(ks[6], (DEPTH, D_MODEL), 0.05),
        "w_in_even": nrm(ks[7], (N_EVEN, D_MODEL, 3 * D_A + D_B), D_MODEL ** -0.5),
        "w_conv_a": nrm(ks[8], (N_EVEN, CONV_A_W, D_A), CONV_A_W ** -0.5),
        "w_pool": nrm(ks[9], (N_EVEN, len(POOL_WINDOWS), B_GROUP, B_GROUP), B_GROUP ** -0.5),
        "pool_scale": 1.0 + nrm(ks[10], (N_EVEN, D_B), 0.1),
        "w_out_even": nrm(ks[11], (N_EVEN, D_A + D_B, D_MODEL), (D_A + D_B) ** -0.5),
        "w_in_odd": nrm(ks[12], (N_ODD, D_MODEL, 2 * D_C + 2 * D_D), D_MODEL ** -0.5),
        "v_norm_g": 1.0 + nrm(ks[13], (N_ODD, D_C), 0.05),
        "v_norm_b": nrm(ks[14], (N_ODD, D_C), 0.02),
        "w_spatial": nrm(ks[15], (N_ODD, C_HEADS, CHUNK, CHUNK), CHUNK ** -0.5),
        "b_spatial": 1.0 + nrm(ks[16], (N_ODD, C_HEADS, CHUNK), 0.1),
        "w_conv_d": nrm(ks[17], (N_ODD, CONF_W, D_D), CONF_W ** -0.5),
        "b_conv_d": nrm(ks[18], (N_ODD, D_D), 0.02),
        "conf_norm_g": 1.0 + nrm(ks[19], (N_ODD, D_D), 0.05),
        "conf_norm_b": nrm(ks[20], (N_ODD, D_D), 0.02),
        "w_out_odd": nrm(ks[21], (N_ODD, D_C + D_D, D_MODEL), (D_C + D_D) ** -0.5),
        "w_ffn_gate": nrm(ks[22], (DEPTH, D_MODEL, D_FF), D_MODEL ** -0.5),
        "w_ffn_up": nrm(ks[23], (DEPTH, D_MODEL, D_FF), D_MODEL ** -0.5),
        "w_ffn_down": nrm(ks[24], (DEPTH, D_FF, D_MODEL), D_FF ** -0.5),
        "norm_final": 1.0 + nrm(ks[25], (D_MODEL,), 0.05),
    }


def reference(x_prompt, x_sample, state_conv_a, state_pool, state_conformer,
              norm_mix, norm_ffn, w_in_even, w_conv_a, w_pool, pool_scale, w_out_even,
              w_in_odd, v_norm_g, v_norm_b, w_spatial, b_spatial, w_conv_d, b_conv_d,
              conf_norm_g, conf_norm_b, w_out_odd, w_ffn_gate, w_ffn_up, w_ffn_down, norm_final):
    dt = x_prompt.dtype
    zero_conv = jnp.zeros((N_EVEN, BATCH, CONV_A_W - 1, D_A), dt)
    zero_pool = jnp.zeros((N_EVEN, BATCH, POOL_HIST, D_B), dt)
    zero_conf = jnp.zeros((N_ODD, BATCH, CONF_W - 1, D_D), dt)
    y_prompt, conv_a_prompt, pool_prompt, chunk_v_prompt, conformer_prompt = run_trunk(
        x_prompt, zero_conv, zero_pool, zero_conf, 0,
        norm_mix, norm_ffn, w_in_even, w_conv_a, w_pool, pool_scale, w_out_even,
        w_in_odd, v_norm_g, v_norm_b, w_spatial, b_spatial, w_conv_d, b_conv_d,
        conf_norm_g, conf_norm_b, w_out_odd, w_ffn_gate, w_ffn_up, w_ffn_down, norm_final)
    y_sample, conv_a_sample, pool_sample, chunk_v_sample, conformer_sample = run_trunk(
        x_sample, state_conv_a, state_pool, state_conformer, PAST_LEN,
        norm_mix, norm_ffn, w_in_even, w_conv_a, w_pool, pool_scale, w_out_even,
        w_in_odd, v_norm_g, v_norm_b, w_spatial, b_spatial, w_conv_d, b_conv_d,
        conf_norm_g, conf_norm_b, w_out_odd, w_ffn_gate, w_ffn_up, w_ffn_down, norm_final)
    return (y_prompt, y_sample, conv_a_prompt, conv_a_sample, pool_prompt, pool_sample,
            chunk_v_prompt, chunk_v_sample, conformer_prompt, conformer_sample)
```

```python
import contextlib
import numpy as np
import concourse.bass as bass
import concourse.mybir as mybir
from concourse.bass_utils import run_bass_kernel_spmd

F32 = mybir.dt.float32
BF16 = mybir.dt.bfloat16
AF = mybir.ActivationFunctionType
ALU = mybir.AluOpType

D = 2048
KC = 16
HALO = 144
MAIN = 1024
NS = 16
T = HALO + MAIN + NS
TP = 1188
NT = 396
M0 = HALO
S0 = HALO + MAIN
G0 = 16
NCH = 9
DFF = 5632
NFB = DFF // 512
EPS = 1e-6
UNIT = 2048
NOUT = MAIN + NS
DEPTH = 4


def _pv_layout():
    off = {}
    n = 0

    def add(name, cols):
        nonlocal n
        off[name] = n
        n += cols
    for l in range(4):
        add(f"nmix{l}", 16)
    for l in range(4):
        add(f"nffn{l}", 16)
    add("nfin", 16)
    for i in range(2):
        add(f"wca{i}", 24)
        add(f"psc{i}", 8)
        add(f"vg{i}", 8)
        add(f"vb{i}", 8)
        add(f"bcd{i}", 8)
        add(f"cg{i}", 8)
        add(f"cb{i}", 8)
        add(f"wcd{i}", 248)
        add(f"ws00{i}", 8)
        add(f"bs0{i}", 8)
    return off, n


PVO, NPV = _pv_layout()
C_ID = 0
C_MASK = 128
C_HM = 256
C_CORR = 257
NCST = 257 + 64


class Sched:
    def __init__(self):
        self.prog = {e: [] for e in ("pe", "act", "dve", "pool", "sp")}
        self.cnt = {}
        self.known = {e: {} for e in self.prog}
        self.lw = {}
        self.rd = {}
        self.dry = False
        self.alias = {}

    def op(self, e, fn, reads=(), writes=(), dma=None, extra_wait=()):
        if self.dry:
            return
        deps = {}
        extra_wait = list(extra_wait)
        for b in writes:
            extra_wait.extend(self.alias.get(b, ()))

        def add(tok):
            if tok is None:
                return
            s, v = tok
            if deps.get(s, 0) < v:
                deps[s] = v
        for b in reads:
            add(self.lw.get(b))
        for b in list(writes) + list(extra_wait):
            add(self.lw.get(b))
            for s, v in self.rd.get(b, {}).items():
                add((s, v))
        for s, v in deps.items():
            if e == "pe" and s == "pe":
                continue
            if self.known[e].get(s, 0) >= v:
                continue
            self.known[e][s] = v
            self.prog[e].append(("w", s, v))
        if dma is None:
            s, amt = e, 1
        else:
            s, amt = "dma_" + dma, 16
        self.cnt[s] = self.cnt.get(s, 0) + amt
        tok = (s, self.cnt[s])
        self.prog[e].append(("o", fn, s, amt))
        for b in reads:
            r = self.rd.setdefault(b, {})
            if r.get(tok[0], 0) < tok[1]:
                r[tok[0]] = tok[1]
        for b in writes:
            self.lw[b] = tok
            self.rd[b] = {}

    def emit(self, nc):
        sems = {s: nc.alloc_semaphore(name=s) for s in self.cnt}
        for s, v in self.cnt.items():
            self.prog["sp"].append(("w", s, v))

        def run(e, eng):
            for it in self.prog[e]:
                if it[0] == "w":
                    eng.wait_ge(sems[it[1]], it[2])
                else:
                    ins = it[1](eng)
                    ins.then_inc(sems[it[2]], it[3])
        with nc.Block() as block:
            @block.tensor
            def _(eng):
                run("pe", eng)

            @block.scalar
            def _(eng):
                run("act", eng)

            @block.vector
            def _(eng):
                run("dve", eng)

            @block.gpsimd
            def _(eng):
                run("pool", eng)

            @block.sync
            def _(eng):
                run("sp", eng)


def build_program(n_layers=DEPTH):
    nc = bass.Bass("TRN2", target_bir_lowering=False)
    S = Sched()

    def din(name, shape):
        return nc.dram_tensor(name, list(shape), F32, kind="ExternalInput").ap()

    def dout(name, shape):
        return nc.dram_tensor(name, list(shape), F32, kind="ExternalOutput").ap()

    xT_in = din("xT", [D, T])
    pv_in = din("pv", [128, NPV])
    cst_in = din("cst", [128, NCST])
    sconv_in = din("sconv", [2, 1024, NS, 2])
    spool_in = din("spool", [2, 1024, NS, 15])
    sconf_in = din("sconf", [2, 1024, NS, 30])
    wsT_in = din("wsT", [2, 128, 8, 128])
    bsb_in = din("bsb", [2, 128, 1024])
    w_in_even = din("w_in_even", [2, D, 4096])
    w_pool = din("w_pool", [2, 4, 256, 256])
    w_out_even = din("w_out_even", [2, D, D])
    w_in_odd = din("w_in_odd", [2, D, 4096])
    w_out_odd = din("w_out_odd", [2, D, D])
    w_gate = din("w_ffn_gate", [4, D, DFF])
    w_up = din("w_ffn_up", [4, D, DFF])
    w_down = din("w_ffn_down", [4, DFF, D])

    yT_out = dout("yT", [D, NOUT])
    conva_out = dout("conva", [2, 1024, 2 + NS * 2])
    pool_out = dout("pool", [2, 1024, 15 + NS * 15])
    chunkv_out = dout("chunkv", [2, 1024, 128 + NS])
    conf_out = dout("conf", [2, 1024, 30 + NS * 30])

    es = contextlib.ExitStack()

    def sb(name, shape, dt=F32):
        return es.enter_context(nc.sbuf_tensor("s_" + name, list(shape), dt))

    def ps(name, shape, dt=F32):
        return es.enter_context(nc.psum_tensor("p_" + name, list(shape), dt))

    with es:
        xT = sb("xT", [128, KC * TP])
        hT = sb("hT", [128, KC * TP], BF16)
        AT = sb("AT", [128, 4 * TP], BF16)
        RG = sb("RG", [128, 4 * TP])
        LNB = RG[:].bitcast(BF16)
        sa = RG[:, 0:1204]
        sbf = RG[:, 1204:2408]
        plb = RG[:, 2408:2408 + TP].bitcast(BF16)
        wpb = RG[:, 3596:3596 + 1024].bitcast(BF16)
        stg = [sb(f"stg{i}", [128, UNIT]) for i in range(2)]
        ring = [sb(f"ring{i}", [128, UNIT], BF16) for i in range(2)]
        PV = sb("PV", [128, NPV])
        CST = sb("CST", [128, NCST])
        identb = sb("identb", [128, 128], BF16)
        ones32 = sb("ones32", [128, 128])
        onesb = sb("onesb", [128, 128], BF16)
        epsc = sb("epsc", [128, 1])
        wsTb = sb("wsTb", [128, 8 * 128], BF16)
        bsbh = sb("bsbh", [128, 256])
        t1 = sb("t1", [128, TP])
        t2 = sb("t2", [128, TP])
        t3 = sb("t3", [128, TP])
        acc1 = sb("acc1", [128, TP])
        acc2 = sb("acc2", [128, TP])
        rstd = acc2
        hx = sb("hx", [128, 32 + TP])
        vT = hx[:, 0:576].bitcast(BF16)
        hxb = hx[:, 0:612].bitcast(BF16)
        gtail = hx[:, 700:746]
        MISC = sb("MISC", [128, 1760])
        hsA = MISC[:, 0:256]
        hsP = [MISC[:, 256:496], MISC[:, 496:736]]
        oA = MISC[:, 736:1008]
        oP = [MISC[:, 1008:1263]]
        hsC = [MISC[:, 0:480], MISC[:, 480:960]]
        oC = [MISC[:, 960:1470]]
        oVh = [MISC[:, 1470:1614], MISC[:, 1614:1758]]
        oY = [t1[:, 0:NOUT], t3[:, 0:NOUT], t2[:, 0:NOUT], hx[:, 0:NOUT]]
        oYk = ["t1", "t3", "t2", "hx"]
        st16 = sb("st16", [128, 64])
        PS = [ps("psA", [128, 1536]), ps("psB", [128, 1536])]
        AUXB = ps("auxb", [128, 2048], BF16)

        def x_(k, a=0, b=TP):
            return xT[:, k * TP + a:k * TP + b]

        def h_(k, a=0, b=TP):
            return hT[:, k * TP + a:k * TP + b]

        def at_(k, a=0, b=TP):
            return AT[:, k * TP + a:k * TP + b]

        def ln_(k, a=0, b=TP):
            return LNB[:, k * TP + a:k * TP + b]

        def v3(ap):
            return ap.rearrange("p (n c) -> p n c", c=NT)

        def ps3(s):
            return PS[s].rearrange("p (n c) -> p n c", c=512)[:, :, 0:NT]

        def pvc(name, j=0):
            o = PVO[name] + j
            return PV[:, o:o + 1]

        class WStream:
            def __init__(self):
                self.units = []
                self.nd = 0
                self.ncast = 0
                self.nu = 0

            def get(self, ap, dest=None):
                if S.dry:
                    self.units.append((ap, dest))
                    return ring[0], ("wr", 0)
                u = self.nu
                self.nu += 1
                n = len(self.units)
                tgt = min(u + 1, n - 1)
                while self.ncast <= tgt:
                    while self.nd <= self.ncast:
                        self._dma(self.nd)
                    self._cast(self.ncast)
                while self.nd < n and self.nd - 2 < self.ncast:
                    self._dma(self.nd)
                if self.units[u][1] is not None:
                    return self.units[u][1], "wpb"
                return ring[u % 2], ("wr", u % 2)

            def _dma(self, v):
                ap = self.units[v][0]
                sl = v % 2
                shp = ap.shape
                if len(shp) == 3:
                    o = stg[sl].rearrange("p (a b) -> p a b", b=shp[2])
                else:
                    o = stg[sl]
                S.op("sp", lambda e, o=o, ap=ap: e.dma_start(out=o, in_=ap),
                     writes=[("stg", sl)], dma=f"w{sl}")
                self.nd += 1

            def _cast(self, v):
                sl = v % 2
                dest = self.units[v][1]
                if dest is not None:
                    S.op("act", lambda e, sl=sl, dest=dest: e.activation(out=dest[:], in_=stg[sl][:], func=AF.Copy),
                         reads=[("stg", sl)], writes=["wpb"])
                else:
                    S.op("act", lambda e, sl=sl: e.activation(out=ring[sl][:], in_=stg[sl][:], func=AF.Copy),
                         reads=[("stg", sl)], writes=[("wr", sl)])
                self.ncast += 1

        _even_misc = ["hsA", ("hsP", 0), ("hsP", 1), "oA", ("oP", 0)]
        _odd_misc = [("hsC", 0), ("hsC", 1), ("oC", 0), ("oV", 0), ("oV", 1)]
        for k_ in _even_misc:
            S.alias[k_] = _odd_misc
        for k_ in _odd_misc:
            S.alias[k_] = _even_misc
        _even_rg = ["sa", "sbf", "plb", "wpb"]
        for k_ in _even_rg:
            S.alias[k_] = ["LNB"]
        S.alias["LNB"] = _even_rg
        S.alias["hx"] = ["vT", "gtail"]
        S.alias["vT"] = ["hx"]
        S.alias["gtail"] = ["hx"]
        W = WStream()
        C0 = [0]
        CIN = [0, 0, 96, 112]
        COUT = [0, 96, 112, 144]
        slot_ctr = [0]
        deferred = []

        def defer(fn, **kw):
            deferred.append((fn, kw))

        def flush():
            while deferred:
                fn, kw = deferred.pop(0)
                S.op("sp", fn, **kw)

        def next_slot():
            s = slot_ctr[0] % 2
            slot_ctr[0] += 1
            return s

        def mm_group(lhs, rhs, reads):
            s = next_slot()

            def fn(pe, lhs=lhs, rhs=rhs, s=s, c0=C0[0]):
                last = None
                nk = len(lhs)
                for k in range(nk):
                    for n in range(3):
                        a = c0 if n == 0 else 0
                        last = pe.matmul(PS[s][:, n * 512 + a:n * 512 + NT], lhsT=lhs[k],
                                         rhs=rhs[k][:, n * NT + a:(n + 1) * NT],
                                         start=(k == 0), stop=(k == nk - 1))
                return last
            S.op("pe", fn, reads=reads, writes=[("ps", s)])
            return s

        first_after_norm = [False]

        def panel_mm(wap, col0):
            u, key = W.get(wap[:, col0:col0 + 128].rearrange("(k p) c -> p k c", p=128))
            uv = u.rearrange("p (k c) -> p k c", c=128)
            if first_after_norm[0]:
                first_after_norm[0] = False
                s = next_slot()
                for k in range(KC):
                    def fn(pe, k=k, s=s, uv=uv, c0=C0[0]):
                        last = None
                        for n in range(3):
                            a = c0 if n == 0 else 0
                            last = pe.matmul(PS[s][:, n * 512 + a:n * 512 + NT], lhsT=uv[:, k, :],
                                             rhs=h_(k, n * NT + a, (n + 1) * NT), start=(k == 0), stop=(k == KC - 1))
                        return last
                    S.op("pe", fn, reads=[key, ("hT", k)], writes=[("ps", s)])
                return s
            return mm_group([uv[:, k, :] for k in range(KC)], [h_(k) for k in range(KC)],
                            reads=[key] + [("hT", k) for k in range(KC)])

        cur_layer = [0]

        def rowproj(wap, row0, nk=4):
            saved = C0[0]
            C0[0] = COUT[cur_layer[0]]
            _rowproj(wap, row0, nk)
            C0[0] = saved

        def _rowproj(wap, row0, nk=4):
            for q in range(4):
                u, key = W.get(wap[row0:row0 + nk * 128, q * 512:(q + 1) * 512]
                               .rearrange("(j p) c -> p j c", p=128))
                uv = u.rearrange("p (j c) -> p j c", c=512)
                for mm in range(4):
                    m = q * 4 + mm
                    s = mm_group([uv[:, j, mm * 128:(mm + 1) * 128] for j in range(nk)],
                                 [at_(j) for j in range(nk)], reads=[key, "AT"])
                    S.op("dve", lambda e, m=m, s=s: e.tensor_tensor(
                        out=v3(x_(m)), in0=ps3(s), in1=v3(x_(m)), op=ALU.add),
                        reads=[("ps", s), ("x", m)], writes=[("x", m)])

        S.op("sp", lambda e: e.dma_start(out=PV[:], in_=pv_in), writes=["PV"], dma="pv")
        S.op("sp", lambda e: e.dma_start(out=CST[:], in_=cst_in), writes=["CST"], dma="cst")
        for q in range(4):
            S.op("sp", lambda e, q=q: e.dma_start(
                out=xT.rearrange("p (k t) -> p k t", t=TP)[:, 4 * q:4 * q + 4, 0:T],
                in_=xT_in[512 * q:512 * (q + 1), :].rearrange("(k p) t -> p k t", p=128)),
                writes=[("x", k) for k in range(4 * q, 4 * q + 4)], dma=f"x{q}")
        S.op("dve", lambda e: e.tensor_copy(out=identb[:], in_=CST[:, C_ID:C_ID + 128]),
             reads=["CST"], writes=["identb"])
        S.op("dve", lambda e: e.memset(ones32[:], 1.0), writes=["ones32"])
        S.op("dve", lambda e: e.memset(onesb[:], 1.0), writes=["onesb"])
        S.op("dve", lambda e: e.memset(epsc[:], EPS), writes=["epsc"])
        S.op("pool", lambda e: e.memset(xT.rearrange("p (k t) -> p k t", t=TP)[:, :, T:TP], 0.0),
             writes=[("x", k) for k in range(KC)])
        S.op("pool", lambda e: e.memset(AT[:], 0.0), writes=["AT"])
        S.op("pool", lambda e: e.memset(hx[:], 0.0), writes=["hx"])
        S.op("pool", lambda e: e.memset(RG[:], 0.0), writes=["sa", "sbf", "LNB", "plb", "wpb"])

        def stats_broadcast(acc_ap, acc_key):
            s = next_slot()

            def fn(pe, s=s):
                last = None
                for n in range(3):
                    last = pe.matmul(PS[s][:, n * 512:n * 512 + NT], lhsT=ones32[:],
                                     rhs=acc_ap[:, n * NT:(n + 1) * NT], start=True, stop=True)
                return last
            S.op("pe", fn, reads=[acc_key, "ones32"], writes=[("ps", s)])
            return s

        def rmsnorm(gname, final=False):
            sqb = [t2[:, 0:TP // 2].bitcast(BF16), t2[:, TP // 2:TP].bitcast(BF16),
                   t3[:, 0:TP // 2].bitcast(BF16), t3[:, TP // 2:TP].bitcast(BF16)]
            s = next_slot()
            for k in range(KC):
                b = k % 4
                if k % 2 == 0:
                    S.op("act", lambda e, k=k, b=b: e.activation(out=sqb[b], in_=x_(k), func=AF.Square),
                         reads=[("x", k)], writes=[("sq", b)], extra_wait=["t2" if b < 2 else "t3"])
                else:
                    S.op("dve", lambda e, k=k, b=b: e.tensor_tensor(out=sqb[b], in0=x_(k), in1=x_(k), op=ALU.mult),
                         reads=[("x", k)], writes=[("sq", b)], extra_wait=["t2" if b < 2 else "t3"])

                def sfn(pe, k=k, b=b, s=s):
                    last = None
                    for n in range(3):
                        last = pe.matmul(PS[s][:, n * 512:n * 512 + NT], lhsT=onesb[:],
                                         rhs=sqb[b][:, n * NT:(n + 1) * NT], start=(k == 0), stop=(k == KC - 1))
                    return last
                S.op("pe", sfn, reads=[("sq", b), "onesb"], writes=[("ps", s)])
            S.op("act", lambda e, s=s: e.activation(out=v3(rstd[:]), in_=ps3(s), func=AF.Ln, scale=1.0 / D, bias=epsc[:]),
                 reads=[("ps", s), "epsc"], writes=["acc2", "t2", "t3"])
            S.op("act", lambda e: e.activation(out=rstd[:], in_=rstd[:], func=AF.Exp, scale=-0.5),
                 reads=["acc2"], writes=["acc2"])
            if not final:
                for k in range(KC):
                    S.op("dve", lambda e, k=k: e.scalar_tensor_tensor(
                        out=h_(k), in0=x_(k), scalar=pvc(gname, k), in1=rstd[:],
                        op0=ALU.mult, op1=ALU.mult),
                        reads=[("x", k), "acc2", "PV"], writes=[("hT", k)])
                first_after_norm[0] = True
            else:
                for k in range(KC):
                    b = k % 4
                    S.op("dve", lambda e, k=k, b=b: e.scalar_tensor_tensor(
                        out=oY[b][:], in0=x_(k, M0, T), scalar=pvc(gname, k), in1=rstd[:, M0:T],
                        op0=ALU.mult, op1=ALU.mult),
                        reads=[("x", k), "acc2", "PV"], writes=[oYk[b]])
                    S.op("sp", lambda e, k=k, b=b: e.dma_start(out=yT_out[k * 128:(k + 1) * 128, :], in_=oY[b][:]),
                         reads=[oYk[b]], dma=f"oY{b}")

        def ffn(l):
            C0[0] = COUT[l]
            rmsnorm(f"nffn{l}")
            for blk in range(NFB):
                for c in range(4):
                    f = blk * 4 + c
                    sg = panel_mm(w_gate[l], f * 128)
                    S.op("act", lambda e, sg=sg: e.activation(out=v3(t1[:]), in_=ps3(sg), func=AF.Silu),
                         reads=[("ps", sg)], writes=["t1"])
                    su = panel_mm(w_up[l], f * 128)
                    S.op("dve", lambda e, su=su, c=c: e.tensor_tensor(
                        out=v3(at_(c)), in0=ps3(su), in1=v3(t1[:]), op=ALU.mult),
                        reads=[("ps", su), "t1"], writes=["AT"])
                rowproj(w_down[l], blk * 512)

        def even_mixer(l):
            i = l // 2
            rmsnorm(f"nmix{l}")
            wi = w_in_even[i]
            S.op("act", lambda e: e.dma_start(
                out=hsA.rearrange("p (c s) -> p c s", s=NS * 2),
                in_=sconv_in[i].rearrange("(c p) b r -> p c (b r)", p=128)),
                writes=["hsA"], dma="hsA")
            oAv = oA.rearrange("p (c s) -> p c s", s=2 + NS * 2)
            S.op("dve", lambda e: e.memset(hx[:, 0:2], 0.0), writes=["hx"])
            for c in range(8):
                sx = panel_mm(wi, c * 128)
                S.op("act", lambda e, sx=sx: e.activation(out=v3(t1[:]), in_=ps3(sx), func=AF.Copy),
                     reads=[("ps", sx)], writes=["t1"])
                sp_ = panel_mm(wi, 1024 + c * 128)
                S.op("dve", lambda e, sp_=sp_: e.tensor_tensor(
                    out=v3(hx[:, 2:2 + TP]), in0=ps3(sp_), in1=v3(t1[:]), op=ALU.mult),
                    reads=[("ps", sp_), "t1"], writes=["hx"])
                spo = panel_mm(wi, 2048 + c * 128)
                w0, w1, w2 = (pvc(f"wca{i}", c * 3 + j) for j in range(3))
                S.op("dve", lambda e, w2=w2: e.tensor_scalar(
                    out=t2[:], in0=hx[:, 2:2 + TP], scalar1=w2, scalar2=None, op0=ALU.mult),
                    reads=["hx", "PV"], writes=["t2"])
                S.op("dve", lambda e, w1=w1: e.scalar_tensor_tensor(
                    out=t2[:], in0=hx[:, 1:1 + TP], scalar=w1, in1=t2[:], op0=ALU.mult, op1=ALU.add),
                    reads=["hx", "t2"], writes=["t2"])
                S.op("dve", lambda e, w0=w0: e.scalar_tensor_tensor(
                    out=t2[:], in0=hx[:, 0:TP], scalar=w0, in1=t2[:], op0=ALU.mult, op1=ALU.add),
                    reads=["hx", "t2"], writes=["t2"])
                hs = hsA[:, c * NS * 2:(c + 1) * NS * 2].rearrange("p (b r) -> p b r", r=2)
                gs = hx[:, 2 + S0:2 + S0 + NS]
                S.op("dve", lambda e, w2=w2, gs=gs: e.tensor_scalar(
                    out=t2[:, S0:S0 + NS], in0=gs, scalar1=w2, scalar2=None, op0=ALU.mult),
                    reads=["hx", "t2"], writes=["t2"])
                S.op("dve", lambda e, w1=w1, hs=hs: e.scalar_tensor_tensor(
                    out=t2[:, S0:S0 + NS], in0=hs[:, :, 1], scalar=w1, in1=t2[:, S0:S0 + NS],
                    op0=ALU.mult, op1=ALU.add), reads=["hsA", "t2"], writes=["t2"])
                S.op("dve", lambda e, w0=w0, hs=hs: e.scalar_tensor_tensor(
                    out=t2[:, S0:S0 + NS], in0=hs[:, :, 0], scalar=w0, in1=t2[:, S0:S0 + NS],
                    op0=ALU.mult, op1=ALU.add), reads=["hsA", "t2"], writes=["t2"])
                S.op("dve", lambda e, spo=spo, c=c: e.tensor_tensor(
                    out=v3(at_(c % 4)), in0=ps3(spo), in1=v3(t2[:]), op=ALU.mult),
                    reads=[("ps", spo), "t2"], writes=["AT"])
                S.op("act", lambda e, c=c: e.activation(out=oAv[:, c, 0:2], in_=hx[:, 2 + S0 - 2:2 + S0], func=AF.Copy),
                     reads=["hx"], writes=["oA"])
                osv = oAv[:, c, 2:2 + NS * 2].rearrange("p (b r) -> p b r", r=2)
                S.op("act", lambda e, osv=osv, hs=hs: e.activation(out=osv[:, :, 0], in_=hs[:, :, 1], func=AF.Copy),
                     reads=["hsA"], writes=["oA"])
                S.op("act", lambda e, osv=osv, gs=gs: e.activation(out=osv[:, :, 1], in_=gs, func=AF.Copy),
                     reads=["hx"], writes=["oA"])
                if c % 4 == 3:
                    rowproj(w_out_even[i], (c // 4) * 512)
            S.op("act", lambda e: e.dma_start(out=conva_out[i].rearrange("(c p) s -> p c s", p=128), in_=oAv),
                 reads=["oA"], dma="oA")
            wpu = [None]
            S.op("dve", lambda e: e.memset(hx[:, 0:15], 0.0), writes=["hx"])

            def load_hsP(c):
                S.op("act", lambda e, c=c: e.dma_start(
                    out=hsP[c % 2], in_=spool_in[i, c * 128:(c + 1) * 128].rearrange("p b r -> p (b r)")),
                    writes=[("hsP", c % 2)], dma=f"hsP{c % 2}")
            load_hsP(0)
            for g in range(4):
                w = 2 << g
                for cc in range(2):
                    c = 2 * g + cc
                    pb_ = c % 2
                    if c + 1 < 8:
                        load_hsP(c + 1)
                    spp = panel_mm(wi, 3072 + c * 128)
                    S.op("act", lambda e, spp=spp: e.activation(out=v3(hx[:, 15:15 + TP]), in_=ps3(spp), func=AF.Copy),
                         reads=[("ps", spp)], writes=["hx"])
                    L = 15 + TP
                    src, skey = hx, "hx"
                    bufs = [(sa, "sa"), (sbf, "sbf")]
                    step = 1
                    bi = 0
                    while step < w:
                        dst, dkey = bufs[bi]
                        lo = 2 * step - 1
                        S.op("dve", lambda e, src=src, dst=dst, lo=lo, step=step: e.tensor_tensor(
                            out=dst[:, lo:L], in0=src[:, lo:L], in1=src[:, lo - step:L - step], op=ALU.add),
                            reads=[skey], writes=[dkey])
                        src, skey = dst, dkey
                        step *= 2
                        bi ^= 1
                    S.op("dve", lambda e, src=src, w=w: e.scalar_tensor_tensor(
                        out=t2[:], in0=src[:, 15:15 + TP], scalar=1.0 / w, in1=hx[:, 15:15 + TP],
                        op0=ALU.mult, op1=ALU.subtract), reads=[skey, "hx"], writes=["t2"])
                    S.op("dve", lambda e, src=src, g=g: e.tensor_tensor(
                        out=t2[:, M0:M0 + 16], in0=src[:, 15 + M0:15 + M0 + 16],
                        in1=CST[:, C_CORR + g * 16:C_CORR + (g + 1) * 16], op=ALU.mult),
                        reads=[skey, "CST", "t2"], writes=["t2"])
                    S.op("dve", lambda e: e.tensor_tensor(
                        out=t2[:, M0:M0 + 16], in0=t2[:, M0:M0 + 16], in1=hx[:, 15 + M0:15 + M0 + 16],
                        op=ALU.subtract), reads=["hx", "t2"], writes=["t2"])
                    hp = hsP[pb_].rearrange("p (b r) -> p b r", r=15)
                    ps_s = hx[:, 15 + S0:15 + S0 + NS]
                    S.op("dve", lambda e, hp=hp, w=w: e.tensor_reduce(
                        out=st16[:, 0:NS], in_=hp[:, :, 15 - (w - 1):15], op=ALU.add,
                        axis=mybir.AxisListType.X), reads=[("hsP", pb_)], writes=["st16"])
                    S.op("dve", lambda e, ps_s=ps_s: e.tensor_tensor(
                        out=st16[:, 0:NS], in0=st16[:, 0:NS], in1=ps_s, op=ALU.add),
                        reads=["hx", "st16"], writes=["st16"])
                    S.op("dve", lambda e, ps_s=ps_s, w=w: e.scalar_tensor_tensor(
                        out=t2[:, S0:S0 + NS], in0=st16[:, 0:NS], scalar=1.0 / w, in1=ps_s,
                        op0=ALU.mult, op1=ALU.subtract), reads=["st16", "hx", "t2"], writes=["t2"])
                    S.op("act", lambda e, cc=cc: e.activation(out=plb[:, cc * TP:(cc + 1) * TP], in_=t2[:], func=AF.Copy),
                         reads=["t2"], writes=["plb"])
                    ob = 0
                    S.op("act", lambda e, ob=ob: e.activation(out=oP[ob][:, 0:15], in_=hx[:, 15 + S0 - 15:15 + S0], func=AF.Copy),
                         reads=["hx"], writes=[("oP", ob)])
                    opv = oP[ob][:, 15:15 + NS * 15].rearrange("p (b r) -> p b r", r=15)
                    S.op("act", lambda e, opv=opv, hp=hp: e.activation(out=opv[:, :, 0:14], in_=hp[:, :, 1:15], func=AF.Copy),
                         reads=[("hsP", pb_)], writes=[("oP", ob)])
                    S.op("act", lambda e, opv=opv, ps_s=ps_s: e.activation(out=opv[:, :, 14], in_=ps_s, func=AF.Copy),
                         reads=["hx"], writes=[("oP", ob)])
                    S.op("act", lambda e, ob=ob, c=c: e.dma_start(out=pool_out[i, c * 128:(c + 1) * 128, :], in_=oP[ob]),
                         reads=[("oP", ob)], dma=f"oP{ob}")
                if g == 0:
                    wpu[0] = W.get(w_pool[i].rearrange("g (k p) d -> p (g k) d", p=128), dest=wpb)
                if g == 2:
                    pass
                u, key = wpu[0]
                uv = u.rearrange("p (a d) -> p a d", d=256)
                for mo in range(2):
                    c = 2 * g + mo
                    s = mm_group([uv[:, g * 2 + kk, mo * 128:(mo + 1) * 128] for kk in range(2)],
                                 [plb[:, kk * TP:(kk + 1) * TP] for kk in range(2)], reads=[key, "plb"])
                    S.op("act", lambda e, s=s, c=c: e.activation(
                        out=v3(at_(c % 4)), in_=ps3(s), func=AF.Identity, scale=pvc(f"psc{i}", c)),
                        reads=[("ps", s), "PV"], writes=["AT"])
                if g % 2 == 1:
                    rowproj(w_out_even[i], 1024 + (g // 2) * 512)

        def ln_accum(c):
            if c == 0:
                S.op("dve", lambda e: e.tensor_copy(out=acc1[:], in_=ln_(0)), reads=["LNB"], writes=["acc1"])
                S.op("dve", lambda e: e.tensor_tensor(out=acc2[:], in0=ln_(0), in1=ln_(0), op=ALU.mult),
                     reads=["LNB"], writes=["acc2"])
            else:
                S.op("pool", lambda e, c=c: e.tensor_tensor(out=acc1[:], in0=acc1[:], in1=ln_(c), op=ALU.add),
                     reads=["LNB", "acc1"], writes=["acc1"])
                S.op("dve", lambda e, c=c: e.tensor_tensor(out=t3[:], in0=ln_(c), in1=ln_(c), op=ALU.mult),
                     reads=["LNB"], writes=["t3"])
                S.op("dve", lambda e: e.tensor_tensor(out=acc2[:], in0=acc2[:], in1=t3[:], op=ALU.add),
                     reads=["t3", "acc2"], writes=["acc2"])

        def ln_stats():
            s1 = stats_broadcast(acc1, "acc1")
            S.op("dve", lambda e, s1=s1: e.tensor_scalar(out=v3(acc1[:]), in0=ps3(s1), scalar1=1.0 / 1024,
                                                         scalar2=None, op0=ALU.mult),
                 reads=[("ps", s1)], writes=["acc1"])
            s2 = stats_broadcast(acc2, "acc2")
            S.op("dve", lambda e: e.tensor_tensor(out=t3[:], in0=acc1[:], in1=acc1[:], op=ALU.mult),
                 reads=["acc1"], writes=["t3"])
            S.op("dve", lambda e, s2=s2: e.scalar_tensor_tensor(
                out=v3(rstd[:]), in0=ps3(s2), scalar=1.0 / 1024, in1=v3(t3[:]),
                op0=ALU.mult, op1=ALU.subtract), reads=[("ps", s2), "t3"], writes=["acc2"])
            S.op("dve", lambda e: e.tensor_scalar(out=rstd[:], in0=rstd[:], scalar1=0.0, scalar2=EPS,
                                                  op0=ALU.max, op1=ALU.add), reads=["acc2"], writes=["acc2"])
            S.op("act", lambda e: e.activation(out=rstd[:], in_=rstd[:], func=AF.Ln),
                 reads=["acc2"], writes=["acc2"])
            S.op("act", lambda e: e.activation(out=rstd[:], in_=rstd[:], func=AF.Exp, scale=-0.5),
                 reads=["acc2"], writes=["acc2"])

        def odd_mixer(l):
            i = l // 2
            rmsnorm(f"nmix{l}")
            wi = w_in_odd[i]
            S.op("sp", lambda e: e.dma_start(out=t3[:, 0:1024], in_=wsT_in[i].rearrange("s h t -> s (h t)")),
                 writes=["t3"], dma="t3")
            for h in range(8):
                S.op("dve", lambda e, h=h: e.tensor_tensor(
                    out=wsTb[:, h * 128:(h + 1) * 128], in0=t3[:, h * 128:(h + 1) * 128],
                    in1=CST[:, C_MASK:C_MASK + 128], op=ALU.mult),
                    reads=["t3", "CST"], writes=["wsTb"])
            for c in range(8):
                sv = panel_mm(wi, 1024 + c * 128)
                S.op("act", lambda e, sv=sv, c=c: e.activation(out=v3(ln_(c)), in_=ps3(sv), func=AF.Gelu),
                     reads=[("ps", sv)], writes=["LNB"])
                ln_accum(c)
            ln_stats()

            def load_bsb(h):
                S.op("act", lambda e, h=h: e.dma_start(out=bsbh[:, (h % 2) * 128:(h % 2 + 1) * 128],
                                                       in_=bsb_in[i, :, h * 128:(h + 1) * 128]),
                     writes=[("bsbh", h % 2)], dma=f"bsbh{h % 2}")
            load_bsb(0)
            def v_normalize(h):
                S.op("dve", lambda e, h=h: e.tensor_tensor(out=t2[:], in0=ln_(h), in1=acc1[:], op=ALU.subtract),
                     reads=["LNB", "acc1"], writes=["t2"])
                S.op("dve", lambda e: e.tensor_tensor(out=t2[:], in0=t2[:], in1=rstd[:], op=ALU.mult),
                     reads=["t2", "acc2"], writes=["t2"])
                S.op("act", lambda e, h=h: e.activation(out=ln_(h), in_=t2[:], func=AF.Identity,
                                                        scale=pvc(f"vg{i}", h), bias=pvc(f"vb{i}", h)),
                     reads=["t2", "PV"], writes=["LNB"])
                S.op("act", lambda e, h=h: e.activation(out=oVh[h % 2], in_=t2[:, S0 - 128:S0 + NS], func=AF.Identity,
                                                        scale=pvc(f"vg{i}", h), bias=pvc(f"vb{i}", h)),
                     reads=["t2", "PV"], writes=[("oV", h % 2)])
                S.op("act", lambda e, h=h: e.dma_start(out=chunkv_out[i, h * 128:(h + 1) * 128, :], in_=oVh[h % 2]),
                     reads=[("oV", h % 2)], dma=f"oV{h % 2}")

            v_normalize(0)
            for h in range(8):
                if h + 1 < 8:
                    load_bsb(h + 1)
                def tfn(pe, h=h):
                    last = None
                    for j in range(NCH):
                        last = pe.transpose(AUXB[:, j * 128:(j + 1) * 128],
                                            ln_(h, G0 + j * 128, G0 + (j + 1) * 128), identb[:])
                    return last
                S.op("pe", tfn, reads=["LNB", "identb"], writes=["auxb"])
                S.op("act", lambda e: e.activation(out=vT[:], in_=AUXB[:, 0:NCH * 128], func=AF.Copy),
                     reads=["auxb"], writes=["vT"])
                su = panel_mm(wi, h * 128)
                S.op("act", lambda e, su=su: e.activation(out=v3(t1[:]), in_=ps3(su), func=AF.Gelu),
                     reads=[("ps", su)], writes=["t1"])
                sgt = next_slot()

                def gfn(pe, h=h, sgt=sgt):
                    last = None
                    for j in range(NCH):
                        o = PS[sgt][:, (j // 3) * 512 + (j % 3) * 128:(j // 3) * 512 + (j % 3) * 128 + 128]
                        last = pe.matmul(o, lhsT=vT[:, j * 128:(j + 1) * 128],
                                         rhs=wsTb[:, h * 128:(h + 1) * 128],
                                         start=True, stop=True)
                    return last
                S.op("pe", gfn, reads=["vT", "wsTb"], writes=[("ps", sgt)])
                if h + 1 < 8:
                    v_normalize(h + 1)
                bs_h = bsbh[:, (h % 2) * 128:(h % 2 + 1) * 128]
                for jb in range(3):
                    gv_ = PS[sgt][:, jb * 512:jb * 512 + 384].rearrange("p (j t) -> p j t", t=128)
                    for jj in range(3):
                        j = jb * 3 + jj
                        c0 = G0 + j * 128
                        S.op("dve", lambda e, gv_=gv_, jj=jj, bs_h=bs_h, c0=c0: e.tensor_tensor(
                            out=t3[:, c0:c0 + 128], in0=gv_[:, jj, :], in1=bs_h, op=ALU.add),
                            reads=[("ps", sgt), ("bsbh", h % 2)], writes=["t3"])
                S.op("dve", lambda e, h=h: e.tensor_tensor(
                    out=at_(h % 4, G0, S0), in0=t3[:, G0:S0], in1=t1[:, G0:S0], op=ALU.mult),
                    reads=["t3", "t1"], writes=["AT"])
                S.op("dve", lambda e, h=h: e.tensor_scalar(
                    out=st16[:, 16:16 + NS], in0=ln_(h, S0, S0 + NS), scalar1=pvc(f"ws00{i}", h),
                    scalar2=pvc(f"bs0{i}", h), op0=ALU.mult, op1=ALU.add),
                    reads=["LNB", "PV"], writes=["st16"])
                S.op("dve", lambda e, h=h: e.tensor_tensor(
                    out=at_(h % 4, S0, S0 + NS), in0=st16[:, 16:16 + NS], in1=t1[:, S0:S0 + NS], op=ALU.mult),
                    reads=["st16", "t1"], writes=["AT"])
                if h % 4 == 3:
                    rowproj(w_out_odd[i], (h // 4) * 512)
            S.op("dve", lambda e: e.memset(hxb[:, 0:30], 0.0), writes=["hx"])

            def load_hsC(c):
                S.op("act", lambda e, c=c: e.dma_start(
                    out=hsC[c % 2], in_=sconf_in[i, c * 128:(c + 1) * 128].rearrange("p b r -> p (b r)")),
                    writes=[("hsC", c % 2)], dma=f"hsC{c % 2}")
            load_hsC(0)
            T2C0 = 2 * NT

            def d_gate(c):
                sgg = panel_mm(wi, 3072 + c * 128)
                S.op("act", lambda e, sgg=sgg: e.activation(out=v3(t1[:]), in_=ps3(sgg), func=AF.Sigmoid),
                     reads=[("ps", sgg)], writes=["t1"])

            def d_glu(c):
                sa_ = panel_mm(wi, 2048 + c * 128)
                S.op("dve", lambda e, sa_=sa_: e.tensor_tensor(
                    out=v3(hxb[:, 30:30 + TP]), in0=ps3(sa_), in1=v3(t1[:]), op=ALU.mult),
                    reads=[("ps", sa_), "t1"], writes=["hx"])
                S.op("dve", lambda e, sa_=sa_: e.tensor_tensor(
                    out=gtail, in0=PS[sa_][:, 1024 + (S0 - 30 - T2C0):1024 + (S0 + NS - T2C0)],
                    in1=t1[:, S0 - 30:S0 + NS], op=ALU.mult),
                    reads=[("ps", sa_), "t1"], writes=["gtail"])

            d_gate(0)
            d_glu(0)
            for c in range(8):
                hb = c % 2
                if c + 1 < 8:
                    load_hsC(c + 1)
                    d_gate(c + 1)
                wc = lambda j, c=c: pvc(f"wcd{i}", c * 31 + j)
                for j in range(31):
                    S.op("dve", lambda e, j=j, wc=wc: e.tensor_scalar(
                        out=AT[:, j * 128:(j + 1) * 128], in0=identb[:], scalar1=wc(j), scalar2=None,
                        op0=ALU.mult), reads=["identb", "PV"], writes=[("dg", j)], extra_wait=["AT"])
                sc = next_slot()

                def cfn(pe, sc=sc, c0=C0[0]):
                    last = None
                    for j in range(31):
                        for n in range(3):
                            a = c0 if n == 0 else 0
                            last = pe.matmul(PS[sc][:, n * 512 + a:n * 512 + NT], lhsT=AT[:, j * 128:(j + 1) * 128],
                                             rhs=hxb[:, j + n * NT + a:j + (n + 1) * NT],
                                             start=(j == 0), stop=(j == 30))
                    return last
                S.op("pe", cfn, reads=["AT", "hx"] + [("dg", j) for j in range(31)], writes=[("ps", sc)])
                S.op("act", lambda e, sc=sc, c=c: e.activation(out=v3(ln_(c)), in_=ps3(sc), func=AF.Identity,
                                                              bias=pvc(f"bcd{i}", c)),
                     reads=[("ps", sc), "PV"], writes=["LNB"])
                hc = hsC[hb].rearrange("p (b r) -> p b r", r=30)
                gl_s = gtail[:, 30:30 + NS]
                wrow = PV[:, PVO[f"wcd{i}"] + c * 31:PVO[f"wcd{i}"] + c * 31 + 30]
                for b in range(NS):
                    S.op("dve", lambda e, b=b, hc=hc, wrow=wrow: e.tensor_tensor(
                        out=t3[:, b * 30:(b + 1) * 30], in0=hc[:, b, :], in1=wrow, op=ALU.mult),
                        reads=[("hsC", hb), "PV"], writes=["t3"])
                S.op("dve", lambda e: e.tensor_reduce(
                    out=st16[:, 32:32 + NS], in_=t3[:, 0:NS * 30].rearrange("p (b r) -> p b r", r=30),
                    op=ALU.add, axis=mybir.AxisListType.X), reads=["t3"], writes=["st16"])
                S.op("dve", lambda e, c=c, wc=wc, gl_s=gl_s: e.tensor_scalar(
                    out=st16[:, 48:48 + NS], in0=gl_s, scalar1=wc(30), scalar2=pvc(f"bcd{i}", c),
                    op0=ALU.mult, op1=ALU.add), reads=["gtail", "PV"], writes=["st16b"])
                S.op("dve", lambda e, c=c: e.tensor_tensor(
                    out=ln_(c, S0, S0 + NS), in0=st16[:, 48:48 + NS], in1=st16[:, 32:32 + NS], op=ALU.add),
                    reads=["st16", "st16b"], writes=["LNB"])
                ln_accum(c)
                ob = 0
                S.op("act", lambda e, ob=ob: e.activation(out=oC[ob][:, 0:30], in_=gtail[:, 0:30], func=AF.Copy),
                     reads=["gtail"], writes=[("oC", ob)])
                ocv = oC[ob][:, 30:30 + NS * 30].rearrange("p (b r) -> p b r", r=30)
                S.op("act", lambda e, ocv=ocv, hc=hc: e.activation(out=ocv[:, :, 0:29], in_=hc[:, :, 1:30], func=AF.Copy),
                     reads=[("hsC", hb)], writes=[("oC", ob)])
                S.op("act", lambda e, ocv=ocv, gl_s=gl_s: e.activation(out=ocv[:, :, 29], in_=gl_s, func=AF.Copy),
                     reads=["gtail"], writes=[("oC", ob)])
                S.op("act", lambda e, ob=ob, c=c: e.dma_start(out=conf_out[i, c * 128:(c + 1) * 128, :], in_=oC[ob]),
                     reads=[("oC", ob)], dma=f"oC{ob}")
                if c + 1 < 8:
                    d_glu(c + 1)
            ln_stats()
            tb = [(t2, "t2"), (t3, "t3"), (t2, "t2"), (t3, "t3"), (t1, "t1"), (hx[:, 0:TP], "hx"), (t2, "t2"), (t3, "t3")]

            def yd_dve(c):
                tt, tk = tb[c]
                S.op("dve", lambda e, c=c, tt=tt: e.tensor_tensor(out=tt[:], in0=ln_(c), in1=acc1[:], op=ALU.subtract),
                     reads=["LNB", "acc1"], writes=[tk])
                S.op("dve", lambda e, tt=tt: e.tensor_tensor(out=tt[:], in0=tt[:], in1=rstd[:], op=ALU.mult),
                     reads=[tk, "acc2"], writes=[tk])

            def yd_act(c):
                tt, tk = tb[c]
                S.op("act", lambda e, c=c, tt=tt: e.activation(out=at_(c % 4), in_=tt[:], func=AF.Silu,
                                                               scale=pvc(f"cg{i}", c), bias=pvc(f"cb{i}", c)),
                     reads=[tk, "PV"], writes=["AT"])
            for c in range(4):
                yd_dve(c)
                yd_act(c)
            for c in range(4, 8):
                yd_dve(c)
            rowproj(w_out_odd[i], 1024)
            for c in range(4, 8):
                yd_act(c)
            rowproj(w_out_odd[i], 1024 + 512)
            for k in range(KC):
                S.op("dve", lambda e, k=k: e.tensor_scalar(
                    out=x_(k, 0, HALO), in0=x_(k, 0, HALO), scalar1=CST[:, C_HM:C_HM + 1], scalar2=None,
                    op0=ALU.mult), reads=[("x", k), "CST"], writes=[("x", k)])

        def program():
            for l in range(n_layers):
                cur_layer[0] = l
                C0[0] = CIN[l]
                if l % 2 == 0:
                    even_mixer(l)
                else:
                    odd_mixer(l)
                ffn(l)
            rmsnorm("nfin", final=True)

        S.dry = True
        program()
        S.dry = False
        slot_ctr[0] = 0
        program()
        S.emit(nc)
    return nc


def _host_inputs(inp):
    f = np.float32
    g = {k: np.asarray(v) for k, v in inp.items()}
    pv = np.zeros((128, NPV), f)

    def put(name, rows):
        rows = np.asarray(rows, f)
        pv[:, PVO[name]:PVO[name] + rows.shape[0]] = rows.T
    for l in range(4):
        put(f"nmix{l}", g["norm_mix"][l].reshape(16, 128))
        put(f"nffn{l}", g["norm_ffn"][l].reshape(16, 128))
    put("nfin", g["norm_final"].reshape(16, 128))
    for i in range(2):
        put(f"wca{i}", g["w_conv_a"][i].reshape(3, 8, 128).transpose(1, 0, 2).reshape(24, 128))
        put(f"psc{i}", g["pool_scale"][i].reshape(8, 128))
        put(f"vg{i}", g["v_norm_g"][i].reshape(8, 128))
        put(f"vb{i}", g["v_norm_b"][i].reshape(8, 128))
        put(f"bcd{i}", g["b_conv_d"][i].reshape(8, 128))
        put(f"cg{i}", g["conf_norm_g"][i].reshape(8, 128))
        put(f"cb{i}", g["conf_norm_b"][i].reshape(8, 128))
        put(f"wcd{i}", g["w_conv_d"][i].reshape(31, 8, 128).transpose(1, 0, 2).reshape(248, 128))
        put(f"ws00{i}", np.broadcast_to(g["w_spatial"][i, :, 0, 0][:, None], (8, 128)))
        put(f"bs0{i}", np.broadcast_to(g["b_spatial"][i, :, 0][:, None], (8, 128)))
    wsT = np.ascontiguousarray(g["w_spatial"].transpose(0, 3, 1, 2)).astype(f)
    bsb = np.ascontiguousarray(np.broadcast_to(g["b_spatial"].reshape(2, 1, 1024), (2, 128, 1024))).astype(f)
    shared = dict(pv=pv, wsT=wsT, bsb=bsb)
    for k in ("w_in_even", "w_pool", "w_out_even", "w_in_odd", "w_out_odd",
              "w_ffn_gate", "w_ffn_up", "w_ffn_down"):
        shared[k] = np.ascontiguousarray(g[k], dtype=f)
    xp, xs = g["x_prompt"], g["x_sample"]
    maps = []
    ss = np.arange(128)
    for c in range(8):
        b, half = c // 2, c % 2
        xT = np.zeros((D, T), f)
        if half:
            xT[:, 0:HALO] = xp[b, MAIN - HALO:MAIN].T
        xT[:, M0:S0] = xp[b, half * MAIN:(half + 1) * MAIN].T
        xT[:, S0:T] = xs[c * NS:(c + 1) * NS, 0].T
        cst = np.zeros((128, NCST), f)
        cst[:, C_ID:C_ID + 128] = np.eye(128, dtype=f)
        cst[:, C_MASK:C_MASK + 128] = (ss[:, None] <= ss[None, :]).astype(f)
        cst[:, C_HM] = float(half)
        for gi, w in enumerate((2, 4, 8, 16)):
            pos = half * MAIN + np.arange(16)
            cst[:, C_CORR + gi * 16:C_CORR + (gi + 1) * 16] = (1.0 / np.minimum(w, pos + 1))[None, :]
        m = dict(shared)
        m["xT"] = xT
        m["cst"] = cst
        sl = slice(c * NS, (c + 1) * NS)
        m["sconv"] = np.ascontiguousarray(g["state_conv_a"][:, sl].transpose(0, 3, 1, 2)).astype(f)
        m["spool"] = np.ascontiguousarray(g["state_pool"][:, sl].transpose(0, 3, 1, 2)).astype(f)
        m["sconf"] = np.ascontiguousarray(g["state_conformer"][:, sl].transpose(0, 3, 1, 2)).astype(f)
        maps.append(m)
    return maps


def _assemble(res):
    f = np.float32
    y_prompt = np.zeros((4, 2048, D), f)
    y_sample = np.zeros((128, 1, D), f)
    conv_a_prompt = np.zeros((2, 4, 2, 1024), f)
    conv_a_sample = np.zeros((2, 128, 2, 1024), f)
    pool_prompt = np.zeros((2, 4, 15, 1024), f)
    pool_sample = np.zeros((2, 128, 15, 1024), f)
    chunk_v_prompt = np.zeros((2, 4, 128, 1024), f)
    chunk_v_sample = np.zeros((2, 128, 1, 1024), f)
    conformer_prompt = np.zeros((2, 4, 30, 1024), f)
    conformer_sample = np.zeros((2, 128, 30, 1024), f)
    for c in range(8):
        r = res[c]
        b, half = c // 2, c % 2
        sl = slice(c * NS, (c + 1) * NS)
        yT = np.asarray(r["yT"])
        y_prompt[b, half * MAIN:(half + 1) * MAIN] = yT[:, :MAIN].T
        y_sample[sl, 0] = yT[:, MAIN:].T
        ca, po, cv, cf = (np.asarray(r[k]) for k in ("conva", "pool", "chunkv", "conf"))
        conv_a_sample[:, sl] = ca[:, :, 2:].reshape(2, 1024, NS, 2).transpose(0, 2, 3, 1)
        pool_sample[:, sl] = po[:, :, 15:].reshape(2, 1024, NS, 15).transpose(0, 2, 3, 1)
        chunk_v_sample[:, sl, 0] = cv[:, :, 128:].transpose(0, 2, 1)
        conformer_sample[:, sl] = cf[:, :, 30:].reshape(2, 1024, NS, 30).transpose(0, 2, 3, 1)
        if half:
            conv_a_prompt[:, b] = ca[:, :, 0:2].transpose(0, 2, 1)
            pool_prompt[:, b] = po[:, :, 0:15].transpose(0, 2, 1)
            chunk_v_prompt[:, b] = cv[:, :, 0:128].transpose(0, 2, 1)
            conformer_prompt[:, b] = cf[:, :, 0:30].transpose(0, 2, 1)
    return (y_prompt, y_sample, conv_a_prompt, conv_a_sample, pool_prompt, pool_sample,
            chunk_v_prompt, chunk_v_sample, conformer_prompt, conformer_sample)


_NC_CACHE = {}


def kernel(**inputs):
    maps = _host_inputs(inputs)
    if "nc" not in _NC_CACHE:
        _NC_CACHE["nc"] = build_program()
    nc = _NC_CACHE["nc"]
    res = run_bass_kernel_spmd(nc, maps, core_ids=list(range(8)))
    return _assemble(res.results)
```

```python
import contextlib
import numpy as np
import concourse.bass as bass
import concourse.mybir as mybir
from concourse.bass_utils import run_bass_kernel_spmd

F32 = mybir.dt.float32
BF16 = mybir.dt.bfloat16
AF = mybir.ActivationFunctionType
ALU = mybir.AluOpType

D = 2048
KC = 16
HALO = 144
MAIN = 1024
NS = 16
T = HALO + MAIN + NS
TP = 1188
NT = 396
M0 = HALO
S0 = HALO + MAIN
G0 = 16
NCH = 9
DFF = 5632
NFB = DFF // 512
EPS = 1e-6
UNIT = 2048
NOUT = MAIN + NS
DEPTH = 4


def _pv_layout():
    off = {}
    n = 0

    def add(name, cols):
        nonlocal n
        off[name] = n
        n += cols
    for l in range(4):
        add(f"nmix{l}", 16)
    for l in range(4):
        add(f"nffn{l}", 16)
    add("nfin", 16)
    for i in range(2):
        add(f"wca{i}", 24)
        add(f"psc{i}", 8)
        add(f"vg{i}", 8)
        add(f"vb{i}", 8)
        add(f"bcd{i}", 8)
        add(f"cg{i}", 8)
        add(f"cb{i}", 8)
        add(f"wcd{i}", 248)
        add(f"ws00{i}", 8)
        add(f"bs0{i}", 8)
    return off, n


PVO, NPV = _pv_layout()
C_ID = 0
C_MASK = 128
C_HM = 256
C_CORR = 257
NCST = 257 + 64


class Sched:
    def __init__(self):
        self.prog = {e: [] for e in ("pe", "act", "dve", "pool", "sp")}
        self.cnt = {}
        self.known = {e: {} for e in self.prog}
        self.lw = {}
        self.rd = {}
        self.dry = False
        self.alias = {}

    def op(self, e, fn, reads=(), writes=(), dma=None, extra_wait=()):
        if self.dry:
            return
        deps = {}
        extra_wait = list(extra_wait)
        for b in writes:
            extra_wait.extend(self.alias.get(b, ()))

        def add(tok):
            if tok is None:
                return
            s, v = tok
            if deps.get(s, 0) < v:
                deps[s] = v
        for b in reads:
            add(self.lw.get(b))
        for b in list(writes) + list(extra_wait):
            add(self.lw.get(b))
            for s, v in self.rd.get(b, {}).items():
                add((s, v))
        for s, v in deps.items():
            if e == "pe" and s == "pe":
                continue
            if self.known[e].get(s, 0) >= v:
                continue
            self.known[e][s] = v
            self.prog[e].append(("w", s, v))
        if dma is None:
            s, amt = e, 1
        else:
            s, amt = "dma_" + dma, 16
        self.cnt[s] = self.cnt.get(s, 0) + amt
        tok = (s, self.cnt[s])
        self.prog[e].append(("o", fn, s, amt))
        for b in reads:
            r = self.rd.setdefault(b, {})
            if r.get(tok[0], 0) < tok[1]:
                r[tok[0]] = tok[1]
        for b in writes:
            self.lw[b] = tok
            self.rd[b] = {}

    def emit(self, nc):
        sems = {s: nc.alloc_semaphore(name=s) for s in self.cnt}
        for s, v in self.cnt.items():
            self.prog["sp"].append(("w", s, v))

        def run(e, eng):
            for it in self.prog[e]:
                if it[0] == "w":
                    eng.wait_ge(sems[it[1]], it[2])
                else:
                    ins = it[1](eng)
                    ins.then_inc(sems[it[2]], it[3])
        with nc.Block() as block:
            @block.tensor
            def _(eng):
                run("pe", eng)

            @block.scalar
            def _(eng):
                run("act", eng)

            @block.vector
            def _(eng):
                run("dve", eng)

            @block.gpsimd
            def _(eng):
                run("pool", eng)

            @block.sync
            def _(eng):
                run("sp", eng)


def build_program(n_layers=DEPTH):
    nc = bass.Bass("TRN2", target_bir_lowering=False)
    S = Sched()

    def din(name, shape):
        return nc.dram_tensor(name, list(shape), F32, kind="ExternalInput").ap()

    def dout(name, shape):
        return nc.dram_tensor(name, list(shape), F32, kind="ExternalOutput").ap()

    xT_in = din("xT", [D, T])
    pv_in = din("pv", [128, NPV])
    cst_in = din("cst", [128, NCST])
    sconv_in = din("sconv", [2, 1024, NS, 2])
    spool_in = din("spool", [2, 1024, NS, 15])
    sconf_in = din("sconf", [2, 1024, NS, 30])
    wsT_in = din("wsT", [2, 128, 8, 128])
    bsb_in = din("bsb", [2, 128, 1024])
    w_in_even = din("w_in_even", [2, D, 4096])
    w_pool = din("w_pool", [2, 4, 256, 256])
    w_out_even = din("w_out_even", [2, D, D])
    w_in_odd = din("w_in_odd", [2, D, 4096])
    w_out_odd = din("w_out_odd", [2, D, D])
    w_gate = din("w_ffn_gate", [4, D, DFF])
    w_up = din("w_ffn_up", [4, D, DFF])
    w_down = din("w_ffn_down", [4, DFF, D])

    yT_out = dout("yT", [D, NOUT])
    conva_out = dout("conva", [2, 1024, 2 + NS * 2])
    pool_out = dout("pool", [2, 1024, 15 + NS * 15])
    chunkv_out = dout("chunkv", [2, 1024, 128 + NS])
    conf_out = dout("conf", [2, 1024, 30 + NS * 30])

    es = contextlib.ExitStack()

    def sb(name, shape, dt=F32):
        return es.enter_context(nc.sbuf_tensor("s_" + name, list(shape), dt))

    def ps(name, shape, dt=F32):
        return es.enter_context(nc.psum_tensor("p_" + name, list(shape), dt))

    with es:
        xT = sb("xT", [128, KC * TP])
        hT = sb("hT", [128, KC * TP], BF16)
        AT = sb("AT", [128, 4 * TP], BF16)
        RG = sb("RG", [128, 4 * TP])
        LNB = RG[:].bitcast(BF16)
        sa = RG[:, 0:1204]
        sbf = RG[:, 1204:2408]
        plb = RG[:, 2408:2408 + TP].bitcast(BF16)
        wpb = RG[:, 3596:3596 + 1024].bitcast(BF16)
        stg = [sb(f"stg{i}", [128, UNIT]) for i in range(2)]
        ring = [sb(f"ring{i}", [128, UNIT], BF16) for i in range(2)]
        PV = sb("PV", [128, NPV])
        CST = sb("CST", [128, NCST])
        identb = sb("identb", [128, 128], BF16)
        ones32 = sb("ones32", [128, 128])
        onesb = sb("onesb", [128, 128], BF16)
        epsc = sb("epsc", [128, 1])
        wsTb = sb("wsTb", [128, 8 * 128], BF16)
        bsbh = sb("bsbh", [128, 256])
        t1 = sb("t1", [128, TP])
        t2 = sb("t2", [128, TP])
        t3 = sb("t3", [128, TP])
        acc1 = sb("acc1", [128, TP])
        acc2 = sb("acc2", [128, TP])
        rstd = acc2
        hx = sb("hx", [128, 32 + TP])
        vT = hx[:, 0:576].bitcast(BF16)
        hxb = hx[:, 0:612].bitcast(BF16)
        gtail = hx[:, 700:746]
        MISC = sb("MISC", [128, 1760])
        hsA = MISC[:, 0:256]
        hsP = [MISC[:, 256:496], MISC[:, 496:736]]
        oA = MISC[:, 736:1008]
        oP = [MISC[:, 1008:1263]]
        hsC = [MISC[:, 0:480], MISC[:, 480:960]]
        oC = [MISC[:, 960:1470]]
        oVh = [MISC[:, 1470:1614], MISC[:, 1614:1758]]
        oY = [t1[:, 0:NOUT], t3[:, 0:NOUT], t2[:, 0:NOUT], hx[:, 0:NOUT]]
        oYk = ["t1", "t3", "t2", "hx"]
        st16 = sb("st16", [128, 64])
        PS = [ps("psA", [128, 1536]), ps("psB", [128, 1536])]
        AUXB = ps("auxb", [128, 2048], BF16)

        def x_(k, a=0, b=TP):
            return xT[:, k * TP + a:k * TP + b]

        def h_(k, a=0, b=TP):
            return hT[:, k * TP + a:k * TP + b]

        def at_(k, a=0, b=TP):
            return AT[:, k * TP + a:k * TP + b]

        def ln_(k, a=0, b=TP):
            return LNB[:, k * TP + a:k * TP + b]

        def v3(ap):
            return ap.rearrange("p (n c) -> p n c", c=NT)

        def ps3(s):
            return PS[s].rearrange("p (n c) -> p n c", c=512)[:, :, 0:NT]

        def pvc(name, j=0):
            o = PVO[name] + j
            return PV[:, o:o + 1]

        class WStream:
            def __init__(self):
                self.units = []
                self.nd = 0
                self.ncast = 0
                self.nu = 0

            def get(self, ap, dest=None):
                if S.dry:
                    self.units.append((ap, dest))
                    return ring[0], ("wr", 0)
                u = self.nu
                self.nu += 1
                n = len(self.units)
                tgt = min(u + 1, n - 1)
                while self.ncast <= tgt:
                    while self.nd <= self.ncast:
                        self._dma(self.nd)
                    self._cast(self.ncast)
                while self.nd < n and self.nd - 2 < self.ncast:
                    self._dma(self.nd)
                if self.units[u][1] is not None:
                    return self.units[u][1], "wpb"
                return ring[u % 2], ("wr", u % 2)

            def _dma(self, v):
                ap = self.units[v][0]
                sl = v % 2
                shp = ap.shape
                if len(shp) == 3:
                    o = stg[sl].rearrange("p (a b) -> p a b", b=shp[2])
                else:
                    o = stg[sl]
                S.op("sp", lambda e, o=o, ap=ap: e.dma_start(out=o, in_=ap),
                     writes=[("stg", sl)], dma=f"w{sl}")
                self.nd += 1

            def _cast(self, v):
                sl = v % 2
                dest = self.units[v][1]
                if dest is not None:
                    S.op("act", lambda e, sl=sl, dest=dest: e.activation(out=dest[:], in_=stg[sl][:], func=AF.Copy),
                         reads=[("stg", sl)], writes=["wpb"])
                else:
                    S.op("act", lambda e, sl=sl: e.activation(out=ring[sl][:], in_=stg[sl][:], func=AF.Copy),
                         reads=[("stg", sl)], writes=[("wr", sl)])
                self.ncast += 1

        _even_misc = ["hsA", ("hsP", 0), ("hsP", 1), "oA", ("oP", 0)]
        _odd_misc = [("hsC", 0), ("hsC", 1), ("oC", 0), ("oV", 0), ("oV", 1)]
        for k_ in _even_misc:
            S.alias[k_] = _odd_misc
        for k_ in _odd_misc:
            S.alias[k_] = _even_misc
        _even_rg = ["sa", "sbf", "plb", "wpb"]
        for k_ in _even_rg:
            S.alias[k_] = ["LNB"]
        S.alias["LNB"] = _even_rg
        S.alias["hx"] = ["vT", "gtail"]
        S.alias["vT"] = ["hx"]
        S.alias["gtail"] = ["hx"]
        W = WStream()
        C0 = [0]
        CIN = [0, 0, 96, 112]
        COUT = [0, 96, 112, 144]
        slot_ctr = [0]
        deferred = []

        def defer(fn, **kw):
            deferred.append((fn, kw))

        def flush():
            while deferred:
                fn, kw = deferred.pop(0)
                S.op("sp", fn, **kw)

        def next_slot():
            s = slot_ctr[0] % 2
            slot_ctr[0] += 1
            return s

        def mm_group(lhs, rhs, reads):
            s = next_slot()

            def fn(pe, lhs=lhs, rhs=rhs, s=s, c0=C0[0]):
                last = None
                nk = len(lhs)
                for k in range(nk):
                    for n in range(3):
                        a = c0 if n == 0 else 0
                        last = pe.matmul(PS[s][:, n * 512 + a:n * 512 + NT], lhsT=lhs[k],
                                         rhs=rhs[k][:, n * NT + a:(n + 1) * NT],
                                         start=(k == 0), stop=(k == nk - 1))
                return last
            S.op("pe", fn, reads=reads, writes=[("ps", s)])
            return s

        first_after_norm = [False]

        def panel_mm(wap, col0):
            u, key = W.get(wap[:, col0:col0 + 128].rearrange("(k p) c -> p k c", p=128))
            uv = u.rearrange("p (k c) -> p k c", c=128)
            if first_after_norm[0]:
                first_after_norm[0] = False
                s = next_slot()
                for k in range(KC):
                    def fn(pe, k=k, s=s, uv=uv, c0=C0[0]):
                        last = None
                        for n in range(3):
                            a = c0 if n == 0 else 0
                            last = pe.matmul(PS[s][:, n * 512 + a:n * 512 + NT], lhsT=uv[:, k, :],
                                             rhs=h_(k, n * NT + a, (n + 1) * NT), start=(k == 0), stop=(k == KC - 1))
                        return last
                    S.op("pe", fn, reads=[key, ("hT", k)], writes=[("ps", s)])
                return s
            return mm_group([uv[:, k, :] for k in range(KC)], [h_(k) for k in range(KC)],
                            reads=[key] + [("hT", k) for k in range(KC)])

        cur_layer = [0]

        def rowproj(wap, row0, nk=4):
            saved = C0[0]
            C0[0] = COUT[cur_layer[0]]
            _rowproj(wap, row0, nk)
            C0[0] = saved

        def _rowproj(wap, row0, nk=4):
            for q in range(4):
                u, key = W.get(wap[row0:row0 + nk * 128, q * 512:(q + 1) * 512]
                               .rearrange("(j p) c -> p j c", p=128))
                uv = u.rearrange("p (j c) -> p j c", c=512)
                for mm in range(4):
                    m = q * 4 + mm
                    s = mm_group([uv[:, j, mm * 128:(mm + 1) * 128] for j in range(nk)],
                                 [at_(j) for j in range(nk)], reads=[key, "AT"])
                    S.op("dve", lambda e, m=m, s=s: e.tensor_tensor(
                        out=v3(x_(m)), in0=ps3(s), in1=v3(x_(m)), op=ALU.add),
                        reads=[("ps", s), ("x", m)], writes=[("x", m)])

        S.op("sp", lambda e: e.dma_start(out=PV[:], in_=pv_in), writes=["PV"], dma="pv")
        S.op("sp", lambda e: e.dma_start(out=CST[:], in_=cst_in), writes=["CST"], dma="cst")
        for q in range(4):
            S.op("sp", lambda e, q=q: e.dma_start(
                out=xT.rearrange("p (k t) -> p k t", t=TP)[:, 4 * q:4 * q + 4, 0:T],
                in_=xT_in[512 * q:512 * (q + 1), :].rearrange("(k p) t -> p k t", p=128)),
                writes=[("x", k) for k in range(4 * q, 4 * q + 4)], dma=f"x{q}")
        S.op("dve", lambda e: e.tensor_copy(out=identb[:], in_=CST[:, C_ID:C_ID + 128]),
             reads=["CST"], writes=["identb"])
        S.op("dve", lambda e: e.memset(ones32[:], 1.0), writes=["ones32"])
        S.op("dve", lambda e: e.memset(onesb[:], 1.0), writes=["onesb"])
        S.op("dve", lambda e: e.memset(epsc[:], EPS), writes=["epsc"])
        S.op("pool", lambda e: e.memset(xT.rearrange("p (k t) -> p k t", t=TP)[:, :, T:TP], 0.0),
             writes=[("x", k) for k in range(KC)])
        S.op("pool", lambda e: e.memset(AT[:], 0.0), writes=["AT"])
        S.op("pool", lambda e: e.memset(hx[:], 0.0), writes=["hx"])
        S.op("pool", lambda e: e.memset(RG[:], 0.0), writes=["sa", "sbf", "LNB", "plb", "wpb"])

        def stats_broadcast(acc_ap, acc_key):
            s = next_slot()

            def fn(pe, s=s):
                last = None
                for n in range(3):
                    last = pe.matmul(PS[s][:, n * 512:n * 512 + NT], lhsT=ones32[:],
                                     rhs=acc_ap[:, n * NT:(n + 1) * NT], start=True, stop=True)
                return last
            S.op("pe", fn, reads=[acc_key, "ones32"], writes=[("ps", s)])
            return s

        def rmsnorm(gname, final=False):
            sqb = [t2[:, 0:TP // 2].bitcast(BF16), t2[:, TP // 2:TP].bitcast(BF16),
                   t3[:, 0:TP // 2].bitcast(BF16), t3[:, TP // 2:TP].bitcast(BF16)]
            s = next_slot()
            for k in range(KC):
                b = k % 4
                if k % 2 == 0:
                    S.op("act", lambda e, k=k, b=b: e.activation(out=sqb[b], in_=x_(k), func=AF.Square),
                         reads=[("x", k)], writes=[("sq", b)], extra_wait=["t2" if b < 2 else "t3"])
                else:
                    S.op("dve", lambda e, k=k, b=b: e.tensor_tensor(out=sqb[b], in0=x_(k), in1=x_(k), op=ALU.mult),
                         reads=[("x", k)], writes=[("sq", b)], extra_wait=["t2" if b < 2 else "t3"])

                def sfn(pe, k=k, b=b, s=s):
                    last = None
                    for n in range(3):
                        last = pe.matmul(PS[s][:, n * 512:n * 512 + NT], lhsT=onesb[:],
                                         rhs=sqb[b][:, n * NT:(n + 1) * NT], start=(k == 0), stop=(k == KC - 1))
                    return last
                S.op("pe", sfn, reads=[("sq", b), "onesb"], writes=[("ps", s)])
            S.op("act", lambda e, s=s: e.activation(out=v3(rstd[:]), in_=ps3(s), func=AF.Ln, scale=1.0 / D, bias=epsc[:]),
                 reads=[("ps", s), "epsc"], writes=["acc2", "t2", "t3"])
            S.op("act", lambda e: e.activation(out=rstd[:], in_=rstd[:], func=AF.Exp, scale=-0.5),
                 reads=["acc2"], writes=["acc2"])
            if not final:
                for k in range(KC):
                    S.op("dve", lambda e, k=k: e.scalar_tensor_tensor(
                        out=h_(k), in0=x_(k), scalar=pvc(gname, k), in1=rstd[:],
                        op0=ALU.mult, op1=ALU.mult),
                        reads=[("x", k), "acc2", "PV"], writes=[("hT", k)])
                first_after_norm[0] = True
            else:
                for k in range(KC):
                    b = k % 4
                    S.op("dve", lambda e, k=k, b=b: e.scalar_tensor_tensor(
                        out=oY[b][:], in0=x_(k, M0, T), scalar=pvc(gname, k), in1=rstd[:, M0:T],
                        op0=ALU.mult, op1=ALU.mult),
                        reads=[("x", k), "acc2", "PV"], writes=[oYk[b]])
                    S.op("sp", lambda e, k=k, b=b: e.dma_start(out=yT_out[k * 128:(k + 1) * 128, :], in_=oY[b][:]),
                         reads=[oYk[b]], dma=f"oY{b}")

        def ffn(l):
            C0[0] = COUT[l]
            rmsnorm(f"nffn{l}")
            for blk in range(NFB):
                for c in range(4):
                    f = blk * 4 + c
                    sg = panel_mm(w_gate[l], f * 128)
                    S.op("act", lambda e, sg=sg: e.activation(out=v3(t1[:]), in_=ps3(sg), func=AF.Silu),
                         reads=[("ps", sg)], writes=["t1"])
                    su = panel_mm(w_up[l], f * 128)
                    S.op("dve", lambda e, su=su, c=c: e.tensor_tensor(
                        out=v3(at_(c)), in0=ps3(su), in1=v3(t1[:]), op=ALU.mult),
                        reads=[("ps", su), "t1"], writes=["AT"])
                rowproj(w_down[l], blk * 512)

        def even_mixer(l):
            i = l // 2
            rmsnorm(f"nmix{l}")
            wi = w_in_even[i]
            S.op("act", lambda e: e.dma_start(
                out=hsA.rearrange("p (c s) -> p c s", s=NS * 2),
                in_=sconv_in[i].rearrange("(c p) b r -> p c (b r)", p=128)),
                writes=["hsA"], dma="hsA")
            oAv = oA.rearrange("p (c s) -> p c s", s=2 + NS * 2)
            S.op("dve", lambda e: e.memset(hx[:, 0:2], 0.0), writes=["hx"])
            for c in range(8):
                sx = panel_mm(wi, c * 128)
                S.op("act", lambda e, sx=sx: e.activation(out=v3(t1[:]), in_=ps3(sx), func=AF.Copy),
                     reads=[("ps", sx)], writes=["t1"])
                sp_ = panel_mm(wi, 1024 + c * 128)
                S.op("dve", lambda e, sp_=sp_: e.tensor_tensor(
                    out=v3(hx[:, 2:2 + TP]), in0=ps3(sp_), in1=v3(t1[:]), op=ALU.mult),
                    reads=[("ps", sp_), "t1"], writes=["hx"])
                spo = panel_mm(wi, 2048 + c * 128)
                w0, w1, w2 = (pvc(f"wca{i}", c * 3 + j) for j in range(3))
                S.op("dve", lambda e, w2=w2: e.tensor_scalar(
                    out=t2[:], in0=hx[:, 2:2 + TP], scalar1=w2, scalar2=None, op0=ALU.mult),
                    reads=["hx", "PV"], writes=["t2"])
                S.op("dve", lambda e, w1=w1: e.scalar_tensor_tensor(
                    out=t2[:], in0=hx[:, 1:1 + TP], scalar=w1, in1=t2[:], op0=ALU.mult, op1=ALU.add),
                    reads=["hx", "t2"], writes=["t2"])
                S.op("dve", lambda e, w0=w0: e.scalar_tensor_tensor(
                    out=t2[:], in0=hx[:, 0:TP], scalar=w0, in1=t2[:], op0=ALU.mult, op1=ALU.add),
                    reads=["hx", "t2"], writes=["t2"])
                hs = hsA[:, c * NS * 2:(c + 1) * NS * 2].rearrange("p (b r) -> p b r", r=2)
                gs = hx[:, 2 + S0:2 + S0 + NS]
                S.op("dve", lambda e, w2=w2, gs=gs: e.tensor_scalar(
                    out=t2[:, S0:S0 + NS], in0=gs, scalar1=w2, scalar2=None, op0=ALU.mult),
                    reads=["hx", "t2"], writes=["t2"])
                S.op("dve", lambda e, w1=w1, hs=hs: e.scalar_tensor_tensor(
                    out=t2[:, S0:S0 + NS], in0=hs[:, :, 1], scalar=w1, in1=t2[:, S0:S0 + NS],
                    op0=ALU.mult, op1=ALU.add), reads=["hsA", "t2"], writes=["t2"])
                S.op("dve", lambda e, w0=w0, hs=hs: e.scalar_tensor_tensor(
                    out=t2[:, S0:S0 + NS], in0=hs[:, :, 0], scalar=w0, in1=t2[:, S0:S0 + NS],
                    op0=ALU.mult, op1=ALU.add), reads=["hsA", "t2"], writes=["t2"])
                S.op("dve", lambda e, spo=spo, c=c: e.tensor_tensor(
                    out=v3(at_(c % 4)), in0=ps3(spo), in1=v3(t2[:]), op=ALU.mult),
                    reads=[("ps", spo), "t2"], writes=["AT"])
                S.op("act", lambda e, c=c: e.activation(out=oAv[:, c, 0:2], in_=hx[:, 2 + S0 - 2:2 + S0], func=AF.Copy),
                     reads=["hx"], writes=["oA"])
                osv = oAv[:, c, 2:2 + NS * 2].rearrange("p (b r) -> p b r", r=2)
                S.op("act", lambda e, osv=osv, hs=hs: e.activation(out=osv[:, :, 0], in_=hs[:, :, 1], func=AF.Copy),
                     reads=["hsA"], writes=["oA"])
                S.op("act", lambda e, osv=osv, gs=gs: e.activation(out=osv[:, :, 1], in_=gs, func=AF.Copy),
                     reads=["hx"], writes=["oA"])
                if c % 4 == 3:
                    rowproj(w_out_even[i], (c // 4) * 512)
            S.op("act", lambda e: e.dma_start(out=conva_out[i].rearrange("(c p) s -> p c s", p=128), in_=oAv),
                 reads=["oA"], dma="oA")
            wpu = [None]
            S.op("dve", lambda e: e.memset(sa[:, 0:15], 0.0), writes=["sa"])
            S.op("dve", lambda e: e.memset(sbf[:, 0:15], 0.0), writes=["sbf"])
            pbufs = [(hx[:, 0:TP], "hx"), (t3[:, 0:TP], "t3")]

            def load_hsP(c):
                S.op("act", lambda e, c=c: e.dma_start(
                    out=hsP[c % 2], in_=spool_in[i, c * 128:(c + 1) * 128].rearrange("p b r -> p (b r)")),
                    writes=[("hsP", c % 2)], dma=f"hsP{c % 2}")
            load_hsP(0)
            for g in range(4):
                w = 2 << g
                for cc in range(2):
                    c = 2 * g + cc
                    pb_ = c % 2
                    pb, pk = pbufs[c % 2]
                    if c + 1 < 8:
                        load_hsP(c + 1)
                    spp = panel_mm(wi, 3072 + c * 128)
                    S.op("act", lambda e, spp=spp, pb=pb: e.activation(out=v3(pb), in_=ps3(spp), func=AF.Copy),
                         reads=[("ps", spp)], writes=[pk])
                    L = 15 + TP
                    S.op("dve", lambda e, pb=pb: e.tensor_tensor(
                        out=sa[:, 16:L], in0=pb[:, 1:TP], in1=pb[:, 0:TP - 1], op=ALU.add),
                        reads=[pk], writes=["sa"])
                    S.op("dve", lambda e, pb=pb: e.tensor_copy(out=sa[:, 15:16], in_=pb[:, 0:1]),
                         reads=[pk], writes=["sa"])
                    src, skey = sa, "sa"
                    bufs = [(sa, "sa"), (sbf, "sbf")]
                    step = 2
                    bi = 1
                    while step < w:
                        dst, dkey = bufs[bi]
                        lo = 2 * step - 1
                        S.op("dve", lambda e, src=src, dst=dst, lo=lo, step=step: e.tensor_tensor(
                            out=dst[:, lo:L], in0=src[:, lo:L], in1=src[:, lo - step:L - step], op=ALU.add),
                            reads=[skey], writes=[dkey])
                        src, skey = dst, dkey
                        step *= 2
                        bi ^= 1
                    S.op("dve", lambda e, src=src, w=w, pb=pb: e.scalar_tensor_tensor(
                        out=t2[:], in0=src[:, 15:15 + TP], scalar=1.0 / w, in1=pb,
                        op0=ALU.mult, op1=ALU.subtract), reads=[skey, pk], writes=["t2"])
                    S.op("dve", lambda e, src=src, g=g: e.tensor_tensor(
                        out=t2[:, M0:M0 + 16], in0=src[:, 15 + M0:15 + M0 + 16],
                        in1=CST[:, C_CORR + g * 16:C_CORR + (g + 1) * 16], op=ALU.mult),
                        reads=[skey, "CST", "t2"], writes=["t2"])
                    S.op("dve", lambda e, pb=pb: e.tensor_tensor(
                        out=t2[:, M0:M0 + 16], in0=t2[:, M0:M0 + 16], in1=pb[:, M0:M0 + 16],
                        op=ALU.subtract), reads=[pk, "t2"], writes=["t2"])
                    hp = hsP[pb_].rearrange("p (b r) -> p b r", r=15)
                    ps_s = pb[:, S0:S0 + NS]
                    S.op("dve", lambda e, hp=hp, w=w: e.tensor_reduce(
                        out=st16[:, 0:NS], in_=hp[:, :, 15 - (w - 1):15], op=ALU.add,
                        axis=mybir.AxisListType.X), reads=[("hsP", pb_)], writes=["st16"])
                    S.op("dve", lambda e, ps_s=ps_s: e.tensor_tensor(
                        out=st16[:, 0:NS], in0=st16[:, 0:NS], in1=ps_s, op=ALU.add),
                        reads=[pk, "st16"], writes=["st16"])
                    S.op("dve", lambda e, ps_s=ps_s, w=w: e.scalar_tensor_tensor(
                        out=t2[:, S0:S0 + NS], in0=st16[:, 0:NS], scalar=1.0 / w, in1=ps_s,
                        op0=ALU.mult, op1=ALU.subtract), reads=["st16", pk, "t2"], writes=["t2"])
                    S.op("act", lambda e, cc=cc: e.activation(out=plb[:, cc * TP:(cc + 1) * TP], in_=t2[:], func=AF.Copy),
                         reads=["t2"], writes=["plb"])
                    ob = 0
                    S.op("act", lambda e, ob=ob, pb=pb: e.activation(out=oP[ob][:, 0:15], in_=pb[:, S0 - 15:S0], func=AF.Copy),
                         reads=[pk], writes=[("oP", ob)])
                    opv = oP[ob][:, 15:15 + NS * 15].rearrange("p (b r) -> p b r", r=15)
                    S.op("act", lambda e, opv=opv, hp=hp: e.activation(out=opv[:, :, 0:14], in_=hp[:, :, 1:15], func=AF.Copy),
                         reads=[("hsP", pb_)], writes=[("oP", ob)])
                    S.op("act", lambda e, opv=opv, ps_s=ps_s: e.activation(out=opv[:, :, 14], in_=ps_s, func=AF.Copy),
                         reads=[pk], writes=[("oP", ob)])
                    S.op("act", lambda e, ob=ob, c=c: e.dma_start(out=pool_out[i, c * 128:(c + 1) * 128, :], in_=oP[ob]),
                         reads=[("oP", ob)], dma=f"oP{ob}")
                if g == 0:
                    wpu[0] = W.get(w_pool[i].rearrange("g (k p) d -> p (g k) d", p=128), dest=wpb)
                if g == 2:
                    pass
                u, key = wpu[0]
                uv = u.rearrange("p (a d) -> p a d", d=256)
                for mo in range(2):
                    c = 2 * g + mo
                    s = mm_group([uv[:, g * 2 + kk, mo * 128:(mo + 1) * 128] for kk in range(2)],
                                 [plb[:, kk * TP:(kk + 1) * TP] for kk in range(2)], reads=[key, "plb"])
                    S.op("act", lambda e, s=s, c=c: e.activation(
                        out=v3(at_(c % 4)), in_=ps3(s), func=AF.Identity, scale=pvc(f"psc{i}", c)),
                        reads=[("ps", s), "PV"], writes=["AT"])
                if g % 2 == 1:
                    rowproj(w_out_even[i], 1024 + (g // 2) * 512)

        def ln_accum(c):
            if c == 0:
                S.op("dve", lambda e: e.tensor_copy(out=acc1[:], in_=ln_(0)), reads=["LNB"], writes=["acc1"])
                S.op("dve", lambda e: e.tensor_tensor(out=acc2[:], in0=ln_(0), in1=ln_(0), op=ALU.mult),
                     reads=["LNB"], writes=["acc2"])
            else:
                S.op("pool", lambda e, c=c: e.tensor_tensor(out=acc1[:], in0=acc1[:], in1=ln_(c), op=ALU.add),
                     reads=["LNB", "acc1"], writes=["acc1"])
                S.op("dve", lambda e, c=c: e.tensor_tensor(out=t3[:], in0=ln_(c), in1=ln_(c), op=ALU.mult),
                     reads=["LNB"], writes=["t3"])
                S.op("dve", lambda e: e.tensor_tensor(out=acc2[:], in0=acc2[:], in1=t3[:], op=ALU.add),
                     reads=["t3", "acc2"], writes=["acc2"])

        def ln_stats():
            s1 = stats_broadcast(acc1, "acc1")
            S.op("dve", lambda e, s1=s1: e.tensor_scalar(out=v3(acc1[:]), in0=ps3(s1), scalar1=1.0 / 1024,
                                                         scalar2=None, op0=ALU.mult),
                 reads=[("ps", s1)], writes=["acc1"])
            s2 = stats_broadcast(acc2, "acc2")
            S.op("dve", lambda e: e.tensor_tensor(out=t3[:], in0=acc1[:], in1=acc1[:], op=ALU.mult),
                 reads=["acc1"], writes=["t3"])
            S.op("dve", lambda e, s2=s2: e.scalar_tensor_tensor(
                out=v3(rstd[:]), in0=ps3(s2), scalar=1.0 / 1024, in1=v3(t3[:]),
                op0=ALU.mult, op1=ALU.subtract), reads=[("ps", s2), "t3"], writes=["acc2"])
            S.op("dve", lambda e: e.tensor_scalar(out=rstd[:], in0=rstd[:], scalar1=0.0, scalar2=EPS,
                                                  op0=ALU.max, op1=ALU.add), reads=["acc2"], writes=["acc2"])
            S.op("act", lambda e: e.activation(out=rstd[:], in_=rstd[:], func=AF.Ln),
                 reads=["acc2"], writes=["acc2"])
            S.op("act", lambda e: e.activation(out=rstd[:], in_=rstd[:], func=AF.Exp, scale=-0.5),
                 reads=["acc2"], writes=["acc2"])

        def odd_mixer(l):
            i = l // 2
            rmsnorm(f"nmix{l}")
            wi = w_in_odd[i]
            S.op("sp", lambda e: e.dma_start(out=t3[:, 0:1024], in_=wsT_in[i].rearrange("s h t -> s (h t)")),
                 writes=["t3"], dma="t3")
            for h in range(8):
                S.op("dve", lambda e, h=h: e.tensor_tensor(
                    out=wsTb[:, h * 128:(h + 1) * 128], in0=t3[:, h * 128:(h + 1) * 128],
                    in1=CST[:, C_MASK:C_MASK + 128], op=ALU.mult),
                    reads=["t3", "CST"], writes=["wsTb"])
            for c in range(8):
                sv = panel_mm(wi, 1024 + c * 128)
                S.op("act", lambda e, sv=sv, c=c: e.activation(out=v3(ln_(c)), in_=ps3(sv), func=AF.Gelu),
                     reads=[("ps", sv)], writes=["LNB"])
                ln_accum(c)
            ln_stats()

            def load_bsb(h):
                S.op("act", lambda e, h=h: e.dma_start(out=bsbh[:, (h % 2) * 128:(h % 2 + 1) * 128],
                                                       in_=bsb_in[i, :, h * 128:(h + 1) * 128]),
                     writes=[("bsbh", h % 2)], dma=f"bsbh{h % 2}")
            load_bsb(0)
            def v_normalize(h):
                S.op("dve", lambda e, h=h: e.tensor_tensor(out=t2[:], in0=ln_(h), in1=acc1[:], op=ALU.subtract),
                     reads=["LNB", "acc1"], writes=["t2"])
                S.op("dve", lambda e: e.tensor_tensor(out=t2[:], in0=t2[:], in1=rstd[:], op=ALU.mult),
                     reads=["t2", "acc2"], writes=["t2"])
                S.op("act", lambda e, h=h: e.activation(out=ln_(h), in_=t2[:], func=AF.Identity,
                                                        scale=pvc(f"vg{i}", h), bias=pvc(f"vb{i}", h)),
                     reads=["t2", "PV"], writes=["LNB"])
                S.op("act", lambda e, h=h: e.activation(out=oVh[h % 2], in_=t2[:, S0 - 128:S0 + NS], func=AF.Identity,
                                                        scale=pvc(f"vg{i}", h), bias=pvc(f"vb{i}", h)),
                     reads=["t2", "PV"], writes=[("oV", h % 2)])
                S.op("act", lambda e, h=h: e.dma_start(out=chunkv_out[i, h * 128:(h + 1) * 128, :], in_=oVh[h % 2]),
                     reads=[("oV", h % 2)], dma=f"oV{h % 2}")

            v_normalize(0)
            for h in range(8):
                if h + 1 < 8:
                    load_bsb(h + 1)
                def tfn(pe, h=h):
                    last = None
                    for j in range(NCH):
                        last = pe.transpose(AUXB[:, j * 128:(j + 1) * 128],
                                            ln_(h, G0 + j * 128, G0 + (j + 1) * 128), identb[:])
                    return last
                S.op("pe", tfn, reads=["LNB", "identb"], writes=["auxb"])
                S.op("act", lambda e: e.activation(out=vT[:], in_=AUXB[:, 0:NCH * 128], func=AF.Copy),
                     reads=["auxb"], writes=["vT"])
                su = panel_mm(wi, h * 128)
                S.op("act", lambda e, su=su: e.activation(out=v3(t1[:]), in_=ps3(su), func=AF.Gelu),
                     reads=[("ps", su)], writes=["t1"])
                sgt = next_slot()

                def gfn(pe, h=h, sgt=sgt):
                    last = None
                    for j in range(NCH):
                        o = PS[sgt][:, (j // 3) * 512 + (j % 3) * 128:(j // 3) * 512 + (j % 3) * 128 + 128]
                        last = pe.matmul(o, lhsT=vT[:, j * 128:(j + 1) * 128],
                                         rhs=wsTb[:, h * 128:(h + 1) * 128],
                                         start=True, stop=True)
                    return last
                S.op("pe", gfn, reads=["vT", "wsTb"], writes=[("ps", sgt)])
                if h + 1 < 8:
                    v_normalize(h + 1)
                bs_h = bsbh[:, (h % 2) * 128:(h % 2 + 1) * 128]
                for jb in range(3):
                    gv_ = PS[sgt][:, jb * 512:jb * 512 + 384].rearrange("p (j t) -> p j t", t=128)
                    for jj in range(3):
                        j = jb * 3 + jj
                        c0 = G0 + j * 128
                        S.op("dve", lambda e, gv_=gv_, jj=jj, bs_h=bs_h, c0=c0: e.tensor_tensor(
                            out=t3[:, c0:c0 + 128], in0=gv_[:, jj, :], in1=bs_h, op=ALU.add),
                            reads=[("ps", sgt), ("bsbh", h % 2)], writes=["t3"])
                S.op("dve", lambda e, h=h: e.tensor_tensor(
                    out=at_(h % 4, G0, S0), in0=t3[:, G0:S0], in1=t1[:, G0:S0], op=ALU.mult),
                    reads=["t3", "t1"], writes=["AT"])
                S.op("dve", lambda e, h=h: e.tensor_scalar(
                    out=st16[:, 16:16 + NS], in0=ln_(h, S0, S0 + NS), scalar1=pvc(f"ws00{i}", h),
                    scalar2=pvc(f"bs0{i}", h), op0=ALU.mult, op1=ALU.add),
                    reads=["LNB", "PV"], writes=["st16"])
                S.op("dve", lambda e, h=h: e.tensor_tensor(
                    out=at_(h % 4, S0, S0 + NS), in0=st16[:, 16:16 + NS], in1=t1[:, S0:S0 + NS], op=ALU.mult),
                    reads=["st16", "t1"], writes=["AT"])
                if h % 4 == 3:
                    rowproj(w_out_odd[i], (h // 4) * 512)
            S.op("dve", lambda e: e.memset(hxb[:, 0:30], 0.0), writes=["hx"])

            def load_hsC(c):
                S.op("act", lambda e, c=c: e.dma_start(
                    out=hsC[c % 2], in_=sconf_in[i, c * 128:(c + 1) * 128].rearrange("p b r -> p (b r)")),
                    writes=[("hsC", c % 2)], dma=f"hsC{c % 2}")
            load_hsC(0)
            T2C0 = 2 * NT

            def d_gate(c):
                sgg = panel_mm(wi, 3072 + c * 128)
                S.op("act", lambda e, sgg=sgg: e.activation(out=v3(t1[:]), in_=ps3(sgg), func=AF.Sigmoid),
                     reads=[("ps", sgg)], writes=["t1"])

            def d_glu(c):
                sa_ = panel_mm(wi, 2048 + c * 128)
                S.op("dve", lambda e, sa_=sa_: e.tensor_tensor(
                    out=v3(hxb[:, 30:30 + TP]), in0=ps3(sa_), in1=v3(t1[:]), op=ALU.mult),
                    reads=[("ps", sa_), "t1"], writes=["hx"])
                S.op("dve", lambda e, sa_=sa_: e.tensor_tensor(
                    out=gtail, in0=PS[sa_][:, 1024 + (S0 - 30 - T2C0):1024 + (S0 + NS - T2C0)],
                    in1=t1[:, S0 - 30:S0 + NS], op=ALU.mult),
                    reads=[("ps", sa_), "t1"], writes=["gtail"])

            d_gate(0)
            d_glu(0)
            for c in range(8):
                hb = c % 2
                if c + 1 < 8:
                    load_hsC(c + 1)
                    d_gate(c + 1)
                wc = lambda j, c=c: pvc(f"wcd{i}", c * 31 + j)
                for j in range(31):
                    S.op("dve", lambda e, j=j, wc=wc: e.tensor_scalar(
                        out=AT[:, j * 128:(j + 1) * 128], in0=identb[:], scalar1=wc(j), scalar2=None,
                        op0=ALU.mult), reads=["identb", "PV"], writes=[("dg", j)], extra_wait=["AT"])
                sc = next_slot()

                def cfn(pe, sc=sc, c0=C0[0]):
                    last = None
                    for j in range(31):
                        for n in range(3):
                            a = c0 if n == 0 else 0
                            last = pe.matmul(PS[sc][:, n * 512 + a:n * 512 + NT], lhsT=AT[:, j * 128:(j + 1) * 128],
                                             rhs=hxb[:, j + n * NT + a:j + (n + 1) * NT],
                                             start=(j == 0), stop=(j == 30))
                    return last
                S.op("pe", cfn, reads=["AT", "hx"] + [("dg", j) for j in range(31)], writes=[("ps", sc)])
                S.op("act", lambda e, sc=sc, c=c: e.activation(out=v3(ln_(c)), in_=ps3(sc), func=AF.Identity,
                                                              bias=pvc(f"bcd{i}", c)),
                     reads=[("ps", sc), "PV"], writes=["LNB"])
                hc = hsC[hb].rearrange("p (b r) -> p b r", r=30)
                gl_s = gtail[:, 30:30 + NS]
                wrow = PV[:, PVO[f"wcd{i}"] + c * 31:PVO[f"wcd{i}"] + c * 31 + 30]
                for b in range(NS):
                    S.op("dve", lambda e, b=b, hc=hc, wrow=wrow: e.tensor_tensor(
                        out=t3[:, b * 30:(b + 1) * 30], in0=hc[:, b, :], in1=wrow, op=ALU.mult),
                        reads=[("hsC", hb), "PV"], writes=["t3"])
                S.op("dve", lambda e: e.tensor_reduce(
                    out=st16[:, 32:32 + NS], in_=t3[:, 0:NS * 30].rearrange("p (b r) -> p b r", r=30),
                    op=ALU.add, axis=mybir.AxisListType.X), reads=["t3"], writes=["st16"])
                S.op("dve", lambda e, c=c, wc=wc, gl_s=gl_s: e.tensor_scalar(
                    out=st16[:, 48:48 + NS], in0=gl_s, scalar1=wc(30), scalar2=pvc(f"bcd{i}", c),
                    op0=ALU.mult, op1=ALU.add), reads=["gtail", "PV"], writes=["st16b"])
                S.op("dve", lambda e, c=c: e.tensor_tensor(
                    out=ln_(c, S0, S0 + NS), in0=st16[:, 48:48 + NS], in1=st16[:, 32:32 + NS], op=ALU.add),
                    reads=["st16", "st16b"], writes=["LNB"])
                ln_accum(c)
                ob = 0
                S.op("act", lambda e, ob=ob: e.activation(out=oC[ob][:, 0:30], in_=gtail[:, 0:30], func=AF.Copy),
                     reads=["gtail"], writes=[("oC", ob)])
                ocv = oC[ob][:, 30:30 + NS * 30].rearrange("p (b r) -> p b r", r=30)
                S.op("act", lambda e, ocv=ocv, hc=hc: e.activation(out=ocv[:, :, 0:29], in_=hc[:, :, 1:30], func=AF.Copy),
                     reads=[("hsC", hb)], writes=[("oC", ob)])
                S.op("act", lambda e, ocv=ocv, gl_s=gl_s: e.activation(out=ocv[:, :, 29], in_=gl_s, func=AF.Copy),
                     reads=["gtail"], writes=[("oC", ob)])
                S.op("act", lambda e, ob=ob, c=c: e.dma_start(out=conf_out[i, c * 128:(c + 1) * 128, :], in_=oC[ob]),
                     reads=[("oC", ob)], dma=f"oC{ob}")
                if c + 1 < 8:
                    d_glu(c + 1)
            ln_stats()
            tb = [(t2, "t2"), (t3, "t3"), (t2, "t2"), (t3, "t3"), (t1, "t1"), (hx[:, 0:TP], "hx"), (t2, "t2"), (t3, "t3")]

            def yd_dve(c):
                tt, tk = tb[c]
                S.op("dve", lambda e, c=c, tt=tt: e.tensor_tensor(out=tt[:], in0=ln_(c), in1=acc1[:], op=ALU.subtract),
                     reads=["LNB", "acc1"], writes=[tk])
                S.op("dve", lambda e, tt=tt: e.tensor_tensor(out=tt[:], in0=tt[:], in1=rstd[:], op=ALU.mult),
                     reads=[tk, "acc2"], writes=[tk])

            def yd_act(c):
                tt, tk = tb[c]
                S.op("act", lambda e, c=c, tt=tt: e.activation(out=at_(c % 4), in_=tt[:], func=AF.Silu,
                                                               scale=pvc(f"cg{i}", c), bias=pvc(f"cb{i}", c)),
                     reads=[tk, "PV"], writes=["AT"])
            for c in range(4):
                yd_dve(c)
                yd_act(c)
            for c in range(4, 8):
                yd_dve(c)
            rowproj(w_out_odd[i], 1024)
            for c in range(4, 8):
                yd_act(c)
            rowproj(w_out_odd[i], 1024 + 512)
            for k in range(KC):
                S.op("dve", lambda e, k=k: e.tensor_scalar(
                    out=x_(k, 0, HALO), in0=x_(k, 0, HALO), scalar1=CST[:, C_HM:C_HM + 1], scalar2=None,
                    op0=ALU.mult), reads=[("x", k), "CST"], writes=[("x", k)])

        def program():
            for l in range(n_layers):
                cur_layer[0] = l
                C0[0] = CIN[l]
                if l % 2 == 0:
                    even_mixer(l)
                else:
                    odd_mixer(l)
                ffn(l)
            rmsnorm("nfin", final=True)

        S.dry = True
        program()
        S.dry = False
        slot_ctr[0] = 0
        program()
        S.emit(nc)
    return nc


def _host_inputs(inp):
    f = np.float32
    g = {k: np.asarray(v) for k, v in inp.items()}
    pv = np.zeros((128, NPV), f)

    def put(name, rows):
        rows = np.asarray(rows, f)
        pv[:, PVO[name]:PVO[name] + rows.shape[0]] = rows.T
    for l in range(4):
        put(f"nmix{l}", g["norm_mix"][l].reshape(16, 128))
        put(f"nffn{l}", g["norm_ffn"][l].reshape(16, 128))
    put("nfin", g["norm_final"].reshape(16, 128))
    for i in range(2):
        put(f"wca{i}", g["w_conv_a"][i].reshape(3, 8, 128).transpose(1, 0, 2).reshape(24, 128))
        put(f"psc{i}", g["pool_scale"][i].reshape(8, 128))
        put(f"vg{i}", g["v_norm_g"][i].reshape(8, 128))
        put(f"vb{i}", g["v_norm_b"][i].reshape(8, 128))
        put(f"bcd{i}", g["b_conv_d"][i].reshape(8, 128))
        put(f"cg{i}", g["conf_norm_g"][i].reshape(8, 128))
        put(f"cb{i}", g["conf_norm_b"][i].reshape(8, 128))
        put(f"wcd{i}", g["w_conv_d"][i].reshape(31, 8, 128).transpose(1, 0, 2).reshape(248, 128))
        put(f"ws00{i}", np.broadcast_to(g["w_spatial"][i, :, 0, 0][:, None], (8, 128)))
        put(f"bs0{i}", np.broadcast_to(g["b_spatial"][i, :, 0][:, None], (8, 128)))
    wsT = np.ascontiguousarray(g["w_spatial"].transpose(0, 3, 1, 2)).astype(f)
    bsb = np.ascontiguousarray(np.broadcast_to(g["b_spatial"].reshape(2, 1, 1024), (2, 128, 1024))).astype(f)
    shared = dict(pv=pv, wsT=wsT, bsb=bsb)
    for k in ("w_in_even", "w_pool", "w_out_even", "w_in_odd", "w_out_odd",
              "w_ffn_gate", "w_ffn_up", "w_ffn_down"):
        shared[k] = np.ascontiguousarray(g[k], dtype=f)
    xp, xs = g["x_prompt"], g["x_sample"]
    maps = []
    ss = np.arange(128)
    for c in range(8):
        b, half = c // 2, c % 2
        xT = np.zeros((D, T), f)
        if half:
            xT[:, 0:HALO] = xp[b, MAIN - HALO:MAIN].T
        xT[:, M0:S0] = xp[b, half * MAIN:(half + 1) * MAIN].T
        xT[:, S0:T] = xs[c * NS:(c + 1) * NS, 0].T
        cst = np.zeros((128, NCST), f)
        cst[:, C_ID:C_ID + 128] = np.eye(128, dtype=f)
        cst[:, C_MASK:C_MASK + 128] = (ss[:, None] <= ss[None, :]).astype(f)
        cst[:, C_HM] = float(half)
        for gi, w in enumerate((2, 4, 8, 16)):
            pos = half * MAIN + np.arange(16)
            cst[:, C_CORR + gi * 16:C_CORR + (gi + 1) * 16] = (1.0 / np.minimum(w, pos + 1))[None, :]
        m = dict(shared)
        m["xT"] = xT
        m["cst"] = cst
        sl = slice(c * NS, (c + 1) * NS)
        m["sconv"] = np.ascontiguousarray(g["state_conv_a"][:, sl].transpose(0, 3, 1, 2)).astype(f)
        m["spool"] = np.ascontiguousarray(g["state_pool"][:, sl].transpose(0, 3, 1, 2)).astype(f)
        m["sconf"] = np.ascontiguousarray(g["state_conformer"][:, sl].transpose(0, 3, 1, 2)).astype(f)
        maps.append(m)
    return maps


def _assemble(res):
    f = np.float32
    y_prompt = np.zeros((4, 2048, D), f)
    y_sample = np.zeros((128, 1, D), f)
    conv_a_prompt = np.zeros((2, 4, 2, 1024), f)
    conv_a_sample = np.zeros((2, 128, 2, 1024), f)
    pool_prompt = np.zeros((2, 4, 15, 1024), f)
    pool_sample = np.zeros((2, 128, 15, 1024), f)
    chunk_v_prompt = np.zeros((2, 4, 128, 1024), f)
    chunk_v_sample = np.zeros((2, 128, 1, 1024), f)
    conformer_prompt = np.zeros((2, 4, 30, 1024), f)
    conformer_sample = np.zeros((2, 128, 30, 1024), f)
    for c in range(8):
        r = res[c]
        b, half = c // 2, c % 2
        sl = slice(c * NS, (c + 1) * NS)
        yT = np.asarray(r["yT"])
        y_prompt[b, half * MAIN:(half + 1) * MAIN] = yT[:, :MAIN].T
        y_sample[sl, 0] = yT[:, MAIN:].T
        ca, po, cv, cf = (np.asarray(r[k]) for k in ("conva", "pool", "chunkv", "conf"))
        conv_a_sample[:, sl] = ca[:, :, 2:].reshape(2, 1024, NS, 2).transpose(0, 2, 3, 1)
        pool_sample[:, sl] = po[:, :, 15:].reshape(2, 1024, NS, 15).transpose(0, 2, 3, 1)
        chunk_v_sample[:, sl, 0] = cv[:, :, 128:].transpose(0, 2, 1)
        conformer_sample[:, sl] = cf[:, :, 30:].reshape(2, 1024, NS, 30).transpose(0, 2, 3, 1)
        if half:
            conv_a_prompt[:, b] = ca[:, :, 0:2].transpose(0, 2, 1)
            pool_prompt[:, b] = po[:, :, 0:15].transpose(0, 2, 1)
            chunk_v_prompt[:, b] = cv[:, :, 0:128].transpose(0, 2, 1)
            conformer_prompt[:, b] = cf[:, :, 0:30].transpose(0, 2, 1)
    return (y_prompt, y_sample, conv_a_prompt, conv_a_sample, pool_prompt, pool_sample,
            chunk_v_prompt, chunk_v_sample, conformer_prompt, conformer_sample)


_NC_CACHE = {}


def kernel(**inputs):
    maps = _host_inputs(inputs)
    if "nc" not in _NC_CACHE:
        _NC_CACHE["nc"] = build_program()
    nc = _NC_CACHE["nc"]
    res = run_bass_kernel_spmd(nc, maps, core_ids=list(range(8)))
    return _assemble(res.results)
```

```python
import contextlib
import numpy as np
import concourse.bass as bass
import concourse.mybir as mybir
from concourse.bass_utils import run_bass_kernel_spmd

F32 = mybir.dt.float32
BF16 = mybir.dt.bfloat16
AF = mybir.ActivationFunctionType
ALU = mybir.AluOpType

D = 2048
KC = 16
HALO = 144
MAIN = 1024
NS = 16
T = HALO + MAIN + NS
TP = 1188
NT = 396
M0 = HALO
S0 = HALO + MAIN
G0 = 16
NCH = 9
DFF = 5632
NFB = DFF // 512
EPS = 1e-6
UNIT = 2048
NOUT = MAIN + NS
DEPTH = 4


def _pv_layout():
    off = {}
    n = 0

    def add(name, cols):
        nonlocal n
        off[name] = n
        n += cols
    for l in range(4):
        add(f"nmix{l}", 16)
    for l in range(4):
        add(f"nffn{l}", 16)
    add("nfin", 16)
    for i in range(2):
        add(f"wca{i}", 24)
        add(f"psc{i}", 8)
        add(f"vg{i}", 8)
        add(f"vb{i}", 8)
        add(f"bcd{i}", 8)
        add(f"cg{i}", 8)
        add(f"cb{i}", 8)
        add(f"wcd{i}", 248)
        add(f"ws00{i}", 8)
        add(f"bs0{i}", 8)
    return off, n


PVO, NPV = _pv_layout()
C_ID = 0
C_MASK = 128
C_HM = 256
C_CORR = 257
NCST = 257 + 64


class Sched:
    def __init__(self):
        self.prog = {e: [] for e in ("pe", "act", "dve", "pool", "sp")}
        self.cnt = {}
        self.known = {e: {} for e in self.prog}
        self.lw = {}
        self.rd = {}
        self.dry = False
        self.alias = {}

    def op(self, e, fn, reads=(), writes=(), dma=None, extra_wait=()):
        if self.dry:
            return
        deps = {}
        extra_wait = list(extra_wait)
        for b in writes:
            extra_wait.extend(self.alias.get(b, ()))

        def add(tok):
            if tok is None:
                return
            s, v = tok
            if deps.get(s, 0) < v:
                deps[s] = v
        for b in reads:
            add(self.lw.get(b))
        for b in list(writes) + list(extra_wait):
            add(self.lw.get(b))
            for s, v in self.rd.get(b, {}).items():
                add((s, v))
        for s, v in deps.items():
            if e == "pe" and s == "pe":
                continue
            if self.known[e].get(s, 0) >= v:
                continue
            self.known[e][s] = v
            self.prog[e].append(("w", s, v))
        if dma is None:
            s, amt = e, 1
        else:
            s, amt = "dma_" + dma, 16
        self.cnt[s] = self.cnt.get(s, 0) + amt
        tok = (s, self.cnt[s])
        self.prog[e].append(("o", fn, s, amt))
        for b in reads:
            r = self.rd.setdefault(b, {})
            if r.get(tok[0], 0) < tok[1]:
                r[tok[0]] = tok[1]
        for b in writes:
            self.lw[b] = tok
            self.rd[b] = {}

    def emit(self, nc):
        sems = {s: nc.alloc_semaphore(name=s) for s in self.cnt}
        for s, v in self.cnt.items():
            self.prog["sp"].append(("w", s, v))

        def run(e, eng):
            for it in self.prog[e]:
                if it[0] == "w":
                    eng.wait_ge(sems[it[1]], it[2])
                else:
                    ins = it[1](eng)
                    ins.then_inc(sems[it[2]], it[3])
        with nc.Block() as block:
            @block.tensor
            def _(eng):
                run("pe", eng)

            @block.scalar
            def _(eng):
                run("act", eng)

            @block.vector
            def _(eng):
                run("dve", eng)

            @block.gpsimd
            def _(eng):
                run("pool", eng)

            @block.sync
            def _(eng):
                run("sp", eng)


def build_program(n_layers=DEPTH):
    nc = bass.Bass("TRN2", target_bir_lowering=False)
    S = Sched()

    def din(name, shape):
        return nc.dram_tensor(name, list(shape), F32, kind="ExternalInput").ap()

    def dout(name, shape):
        return nc.dram_tensor(name, list(shape), F32, kind="ExternalOutput").ap()

    xT_in = din("xT", [D, T])
    pv_in = din("pv", [128, NPV])
    cst_in = din("cst", [128, NCST])
    sconv_in = din("sconv", [2, 1024, NS, 2])
    spool_in = din("spool", [2, 1024, NS, 15])
    sconf_in = din("sconf", [2, 1024, NS, 30])
    wsT_in = din("wsT", [2, 128, 8, 128])
    bsb_in = din("bsb", [2, 128, 1024])
    w_in_even = din("w_in_even", [2, D, 4096])
    w_pool = din("w_pool", [2, 4, 256, 256])
    w_out_even = din("w_out_even", [2, D, D])
    w_in_odd = din("w_in_odd", [2, D, 4096])
    w_out_odd = din("w_out_odd", [2, D, D])
    w_gate = din("w_ffn_gate", [4, D, DFF])
    w_up = din("w_ffn_up", [4, D, DFF])
    w_down = din("w_ffn_down", [4, DFF, D])

    yT_out = dout("yT", [D, NOUT])
    conva_out = dout("conva", [2, 1024, 2 + NS * 2])
    pool_out = dout("pool", [2, 1024, 15 + NS * 15])
    chunkv_out = dout("chunkv", [2, 1024, 128 + NS])
    conf_out = dout("conf", [2, 1024, 30 + NS * 30])

    es = contextlib.ExitStack()

    def sb(name, shape, dt=F32):
        return es.enter_context(nc.sbuf_tensor("s_" + name, list(shape), dt))

    def ps(name, shape, dt=F32):
        return es.enter_context(nc.psum_tensor("p_" + name, list(shape), dt))

    with es:
        xT = sb("xT", [128, KC * TP])
        hT = sb("hT", [128, KC * TP], BF16)
        AT = sb("AT", [128, 4 * TP], BF16)
        RG = sb("RG", [128, 4 * TP])
        LNB = RG[:].bitcast(BF16)
        sa = RG[:, 0:1204]
        sbf = RG[:, 1204:2408]
        plb = RG[:, 2408:2408 + TP].bitcast(BF16)
        wpb = RG[:, 3596:3596 + 1024].bitcast(BF16)
        stg = [sb(f"stg{i}", [128, UNIT]) for i in range(2)]
        ring = [sb(f"ring{i}", [128, UNIT], BF16) for i in range(2)]
        PV = sb("PV", [128, NPV])
        CST = sb("CST", [128, NCST])
        identb = sb("identb", [128, 128], BF16)
        ones32 = sb("ones32", [128, 128])
        onesb = sb("onesb", [128, 128], BF16)
        epsc = sb("epsc", [128, 1])
        wsTb = sb("wsTb", [128, 8 * 128], BF16)
        bsbh = sb("bsbh", [128, 256])
        t1 = sb("t1", [128, TP])
        t2 = sb("t2", [128, TP])
        t3 = sb("t3", [128, TP])
        acc1 = sb("acc1", [128, TP])
        acc2 = sb("acc2", [128, TP])
        rstd = acc2
        hx = sb("hx", [128, 32 + TP])
        vT = hx[:, 0:576].bitcast(BF16)
        hxb = hx[:, 0:612].bitcast(BF16)
        gtail = hx[:, 700:746]
        MISC = sb("MISC", [128, 1760])
        hsA = MISC[:, 0:256]
        hsP = [MISC[:, 256:496], MISC[:, 496:736]]
        oA = MISC[:, 736:1008]
        oP = [MISC[:, 1008:1263]]
        hsC = [MISC[:, 0:480], MISC[:, 480:960]]
        oC = [MISC[:, 960:1470]]
        oVh = [MISC[:, 1470:1614], MISC[:, 1614:1758]]
        oY = [t1[:, 0:NOUT], t3[:, 0:NOUT], t2[:, 0:NOUT], hx[:, 0:NOUT]]
        oYk = ["t1", "t3", "t2", "hx"]
        st16 = sb("st16", [128, 64])
        PS = [ps("psA", [128, 1536]), ps("psB", [128, 1536])]
        AUXB = ps("auxb", [128, 2048], BF16)

        def x_(k, a=0, b=TP):
            return xT[:, k * TP + a:k * TP + b]

        def h_(k, a=0, b=TP):
            return hT[:, k * TP + a:k * TP + b]

        def at_(k, a=0, b=TP):
            return AT[:, k * TP + a:k * TP + b]

        def ln_(k, a=0, b=TP):
            return LNB[:, k * TP + a:k * TP + b]

        def v3(ap):
            return ap.rearrange("p (n c) -> p n c", c=NT)

        def ps3(s):
            return PS[s].rearrange("p (n c) -> p n c", c=512)[:, :, 0:NT]

        def pvc(name, j=0):
            o = PVO[name] + j
            return PV[:, o:o + 1]

        class WStream:
            def __init__(self):
                self.units = []
                self.nd = 0
                self.ncast = 0
                self.nu = 0

            def get(self, ap, dest=None):
                if S.dry:
                    self.units.append((ap, dest))
                    return ring[0], ("wr", 0)
                u = self.nu
                self.nu += 1
                n = len(self.units)
                tgt = min(u + 1, n - 1)
                while self.ncast <= tgt:
                    while self.nd <= self.ncast:
                        self._dma(self.nd)
                    self._cast(self.ncast)
                while self.nd < n and self.nd - 2 < self.ncast:
                    self._dma(self.nd)
                if self.units[u][1] is not None:
                    return self.units[u][1], "wpb"
                return ring[u % 2], ("wr", u % 2)

            def _dma(self, v):
                ap = self.units[v][0]
                sl = v % 2
                shp = ap.shape
                if len(shp) == 3:
                    o = stg[sl].rearrange("p (a b) -> p a b", b=shp[2])
                else:
                    o = stg[sl]
                S.op("sp", lambda e, o=o, ap=ap: e.dma_start(out=o, in_=ap),
                     writes=[("stg", sl)], dma=f"w{sl}")
                self.nd += 1

            def _cast(self, v):
                sl = v % 2
                dest = self.units[v][1]
                if dest is not None:
                    S.op("act", lambda e, sl=sl, dest=dest: e.activation(out=dest[:], in_=stg[sl][:], func=AF.Copy),
                         reads=[("stg", sl)], writes=["wpb"])
                else:
                    S.op("act", lambda e, sl=sl: e.activation(out=ring[sl][:], in_=stg[sl][:], func=AF.Copy),
                         reads=[("stg", sl)], writes=[("wr", sl)])
                self.ncast += 1

        _even_misc = ["hsA", ("hsP", 0), ("hsP", 1), "oA", ("oP", 0)]
        _odd_misc = [("hsC", 0), ("hsC", 1), ("oC", 0), ("oV", 0), ("oV", 1)]
        for k_ in _even_misc:
            S.alias[k_] = _odd_misc
        for k_ in _odd_misc:
            S.alias[k_] = _even_misc
        _even_rg = ["sa", "sbf", "plb", "wpb"]
        for k_ in _even_rg:
            S.alias[k_] = ["LNB"]
        S.alias["LNB"] = _even_rg
        S.alias["hx"] = ["vT", "gtail"]
        S.alias["vT"] = ["hx"]
        S.alias["gtail"] = ["hx"]
        W = WStream()
        C0 = [0]
        CIN = [0, 0, 96, 112]
        COUT = [0, 96, 112, 144]
        slot_ctr = [0]
        deferred = []

        def defer(fn, **kw):
            deferred.append((fn, kw))

        def flush():
            while deferred:
                fn, kw = deferred.pop(0)
                S.op("sp", fn, **kw)

        def next_slot():
            s = slot_ctr[0] % 2
            slot_ctr[0] += 1
            return s

        def mm_group(lhs, rhs, reads):
            s = next_slot()

            def fn(pe, lhs=lhs, rhs=rhs, s=s, c0=C0[0]):
                last = None
                nk = len(lhs)
                for k in range(nk):
                    for n in range(3):
                        a = c0 if n == 0 else 0
                        last = pe.matmul(PS[s][:, n * 512 + a:n * 512 + NT], lhsT=lhs[k],
                                         rhs=rhs[k][:, n * NT + a:(n + 1) * NT],
                                         start=(k == 0), stop=(k == nk - 1))
                return last
            S.op("pe", fn, reads=reads, writes=[("ps", s)])
            return s

        first_after_norm = [False]

        def panel_mm(wap, col0):
            u, key = W.get(wap[:, col0:col0 + 128].rearrange("(k p) c -> p k c", p=128))
            uv = u.rearrange("p (k c) -> p k c", c=128)
            if first_after_norm[0]:
                first_after_norm[0] = False
                s = next_slot()
                for k in range(KC):
                    def fn(pe, k=k, s=s, uv=uv, c0=C0[0]):
                        last = None
                        for n in range(3):
                            a = c0 if n == 0 else 0
                            last = pe.matmul(PS[s][:, n * 512 + a:n * 512 + NT], lhsT=uv[:, k, :],
                                             rhs=h_(k, n * NT + a, (n + 1) * NT), start=(k == 0), stop=(k == KC - 1))
                        return last
                    S.op("pe", fn, reads=[key, ("hT", k)], writes=[("ps", s)])
                return s
            return mm_group([uv[:, k, :] for k in range(KC)], [h_(k) for k in range(KC)],
                            reads=[key] + [("hT", k) for k in range(KC)])

        cur_layer = [0]

        def rowproj(wap, row0, nk=4):
            saved = C0[0]
            C0[0] = COUT[cur_layer[0]]
            _rowproj(wap, row0, nk)
            C0[0] = saved

        def _rowproj(wap, row0, nk=4):
            for q in range(4):
                u, key = W.get(wap[row0:row0 + nk * 128, q * 512:(q + 1) * 512]
                               .rearrange("(j p) c -> p j c", p=128))
                uv = u.rearrange("p (j c) -> p j c", c=512)
                for mm in range(4):
                    m = q * 4 + mm
                    s = mm_group([uv[:, j, mm * 128:(mm + 1) * 128] for j in range(nk)],
                                 [at_(j) for j in range(nk)], reads=[key, "AT"])
                    S.op("dve", lambda e, m=m, s=s: e.tensor_tensor(
                        out=v3(x_(m)), in0=ps3(s), in1=v3(x_(m)), op=ALU.add),
                        reads=[("ps", s), ("x", m)], writes=[("x", m)])

        S.op("sp", lambda e: e.dma_start(out=PV[:], in_=pv_in), writes=["PV"], dma="pv")
        S.op("sp", lambda e: e.dma_start(out=CST[:], in_=cst_in), writes=["CST"], dma="cst")
        for q in range(4):
            S.op("sp", lambda e, q=q: e.dma_start(
                out=xT.rearrange("p (k t) -> p k t", t=TP)[:, 4 * q:4 * q + 4, 0:T],
                in_=xT_in[512 * q:512 * (q + 1), :].rearrange("(k p) t -> p k t", p=128)),
                writes=[("x", k) for k in range(4 * q, 4 * q + 4)], dma=f"x{q}")
        S.op("dve", lambda e: e.tensor_copy(out=identb[:], in_=CST[:, C_ID:C_ID + 128]),
             reads=["CST"], writes=["identb"])
        S.op("dve", lambda e: e.memset(ones32[:], 1.0), writes=["ones32"])
        S.op("dve", lambda e: e.memset(onesb[:], 1.0), writes=["onesb"])
        S.op("dve", lambda e: e.memset(epsc[:], EPS), writes=["epsc"])
        S.op("pool", lambda e: e.memset(xT.rearrange("p (k t) -> p k t", t=TP)[:, :, T:TP], 0.0),
             writes=[("x", k) for k in range(KC)])
        S.op("pool", lambda e: e.memset(AT[:], 0.0), writes=["AT"])
        S.op("pool", lambda e: e.memset(hx[:], 0.0), writes=["hx"])
        S.op("pool", lambda e: e.memset(RG[:], 0.0), writes=["sa", "sbf", "LNB", "plb", "wpb"])

        def stats_broadcast(acc_ap, acc_key):
            s = next_slot()

            def fn(pe, s=s):
                last = None
                for n in range(3):
                    last = pe.matmul(PS[s][:, n * 512:n * 512 + NT], lhsT=ones32[:],
                                     rhs=acc_ap[:, n * NT:(n + 1) * NT], start=True, stop=True)
                return last
            S.op("pe", fn, reads=[acc_key, "ones32"], writes=[("ps", s)])
            return s

        def rmsnorm(gname, final=False):
            sqb = [t2[:, 0:TP // 2].bitcast(BF16), t2[:, TP // 2:TP].bitcast(BF16),
                   t3[:, 0:TP // 2].bitcast(BF16), t3[:, TP // 2:TP].bitcast(BF16)]
            s = next_slot()
            for k in range(KC):
                b = k % 4
                if k % 2 == 0:
                    S.op("act", lambda e, k=k, b=b: e.activation(out=sqb[b], in_=x_(k), func=AF.Square),
                         reads=[("x", k)], writes=[("sq", b)], extra_wait=["t2" if b < 2 else "t3"])
                else:
                    S.op("dve", lambda e, k=k, b=b: e.tensor_tensor(out=sqb[b], in0=x_(k), in1=x_(k), op=ALU.mult),
                         reads=[("x", k)], writes=[("sq", b)], extra_wait=["t2" if b < 2 else "t3"])

                def sfn(pe, k=k, b=b, s=s):
                    last = None
                    for n in range(3):
                        last = pe.matmul(PS[s][:, n * 512:n * 512 + NT], lhsT=onesb[:],
                                         rhs=sqb[b][:, n * NT:(n + 1) * NT], start=(k == 0), stop=(k == KC - 1))
                    return last
                S.op("pe", sfn, reads=[("sq", b), "onesb"], writes=[("ps", s)])
            S.op("act", lambda e, s=s: e.activation(out=v3(rstd[:]), in_=ps3(s), func=AF.Ln, scale=1.0 / D, bias=epsc[:]),
                 reads=[("ps", s), "epsc"], writes=["acc2", "t2", "t3"])
            S.op("act", lambda e: e.activation(out=rstd[:], in_=rstd[:], func=AF.Exp, scale=-0.5),
                 reads=["acc2"], writes=["acc2"])
            if not final:
                for k in range(KC):
                    S.op("dve", lambda e, k=k: e.scalar_tensor_tensor(
                        out=h_(k), in0=x_(k), scalar=pvc(gname, k), in1=rstd[:],
                        op0=ALU.mult, op1=ALU.mult),
                        reads=[("x", k), "acc2", "PV"], writes=[("hT", k)])
                first_after_norm[0] = True
            else:
                for k in range(KC):
                    b = k % 4
                    S.op("dve", lambda e, k=k, b=b: e.scalar_tensor_tensor(
                        out=oY[b][:], in0=x_(k, M0, T), scalar=pvc(gname, k), in1=rstd[:, M0:T],
                        op0=ALU.mult, op1=ALU.mult),
                        reads=[("x", k), "acc2", "PV"], writes=[oYk[b]])
                    S.op("sp", lambda e, k=k, b=b: e.dma_start(out=yT_out[k * 128:(k + 1) * 128, :], in_=oY[b][:]),
                         reads=[oYk[b]], dma=f"oY{b}")

        def ffn(l):
            C0[0] = COUT[l]
            rmsnorm(f"nffn{l}")
            for blk in range(NFB):
                for c in range(4):
                    f = blk * 4 + c
                    sg = panel_mm(w_gate[l], f * 128)
                    S.op("act", lambda e, sg=sg: e.activation(out=v3(t1[:]), in_=ps3(sg), func=AF.Silu),
                         reads=[("ps", sg)], writes=["t1"])
                    su = panel_mm(w_up[l], f * 128)
                    S.op("dve", lambda e, su=su, c=c: e.tensor_tensor(
                        out=v3(at_(c)), in0=ps3(su), in1=v3(t1[:]), op=ALU.mult),
                        reads=[("ps", su), "t1"], writes=["AT"])
                rowproj(w_down[l], blk * 512)

        def even_mixer(l):
            i = l // 2
            rmsnorm(f"nmix{l}")
            wi = w_in_even[i]
            S.op("act", lambda e: e.dma_start(
                out=hsA.rearrange("p (c s) -> p c s", s=NS * 2),
                in_=sconv_in[i].rearrange("(c p) b r -> p c (b r)", p=128)),
                writes=["hsA"], dma="hsA")
            oAv = oA.rearrange("p (c s) -> p c s", s=2 + NS * 2)
            S.op("dve", lambda e: e.memset(hx[:, 0:2], 0.0), writes=["hx"])
            for c in range(8):
                sx = panel_mm(wi, c * 128)
                S.op("act", lambda e, sx=sx: e.activation(out=v3(t1[:]), in_=ps3(sx), func=AF.Copy),
                     reads=[("ps", sx)], writes=["t1"])
                sp_ = panel_mm(wi, 1024 + c * 128)
                S.op("dve", lambda e, sp_=sp_: e.tensor_tensor(
                    out=v3(hx[:, 2:2 + TP]), in0=ps3(sp_), in1=v3(t1[:]), op=ALU.mult),
                    reads=[("ps", sp_), "t1"], writes=["hx"])
                spo = panel_mm(wi, 2048 + c * 128)
                w0, w1, w2 = (pvc(f"wca{i}", c * 3 + j) for j in range(3))
                S.op("dve", lambda e, w2=w2: e.tensor_scalar(
                    out=t2[:], in0=hx[:, 2:2 + TP], scalar1=w2, scalar2=None, op0=ALU.mult),
                    reads=["hx", "PV"], writes=["t2"])
                S.op("dve", lambda e, w1=w1: e.scalar_tensor_tensor(
                    out=t2[:], in0=hx[:, 1:1 + TP], scalar=w1, in1=t2[:], op0=ALU.mult, op1=ALU.add),
                    reads=["hx", "t2"], writes=["t2"])
                S.op("dve", lambda e, w0=w0: e.scalar_tensor_tensor(
                    out=t2[:], in0=hx[:, 0:TP], scalar=w0, in1=t2[:], op0=ALU.mult, op1=ALU.add),
                    reads=["hx", "t2"], writes=["t2"])
                hs = hsA[:, c * NS * 2:(c + 1) * NS * 2].rearrange("p (b r) -> p b r", r=2)
                gs = hx[:, 2 + S0:2 + S0 + NS]
                S.op("dve", lambda e, w2=w2, gs=gs: e.tensor_scalar(
                    out=t2[:, S0:S0 + NS], in0=gs, scalar1=w2, scalar2=None, op0=ALU.mult),
                    reads=["hx", "t2"], writes=["t2"])
                S.op("dve", lambda e, w1=w1, hs=hs: e.scalar_tensor_tensor(
                    out=t2[:, S0:S0 + NS], in0=hs[:, :, 1], scalar=w1, in1=t2[:, S0:S0 + NS],
                    op0=ALU.mult, op1=ALU.add), reads=["hsA", "t2"], writes=["t2"])
                S.op("dve", lambda e, w0=w0, hs=hs: e.scalar_tensor_tensor(
                    out=t2[:, S0:S0 + NS], in0=hs[:, :, 0], scalar=w0, in1=t2[:, S0:S0 + NS],
                    op0=ALU.mult, op1=ALU.add), reads=["hsA", "t2"], writes=["t2"])
                S.op("dve", lambda e, spo=spo, c=c: e.tensor_tensor(
                    out=v3(at_(c % 4)), in0=ps3(spo), in1=v3(t2[:]), op=ALU.mult),
                    reads=[("ps", spo), "t2"], writes=["AT"])
                S.op("act", lambda e, c=c: e.activation(out=oAv[:, c, 0:2], in_=hx[:, 2 + S0 - 2:2 + S0], func=AF.Copy),
                     reads=["hx"], writes=["oA"])
                osv = oAv[:, c, 2:2 + NS * 2].rearrange("p (b r) -> p b r", r=2)
                S.op("act", lambda e, osv=osv, hs=hs: e.activation(out=osv[:, :, 0], in_=hs[:, :, 1], func=AF.Copy),
                     reads=["hsA"], writes=["oA"])
                S.op("act", lambda e, osv=osv, gs=gs: e.activation(out=osv[:, :, 1], in_=gs, func=AF.Copy),
                     reads=["hx"], writes=["oA"])
                if c % 4 == 3:
                    rowproj(w_out_even[i], (c // 4) * 512)
            S.op("act", lambda e: e.dma_start(out=conva_out[i].rearrange("(c p) s -> p c s", p=128), in_=oAv),
                 reads=["oA"], dma="oA")
            wpu = [None]
            S.op("dve", lambda e: e.memset(sa[:, 0:15], 0.0), writes=["sa"])
            S.op("dve", lambda e: e.memset(sbf[:, 0:15], 0.0), writes=["sbf"])
            pbufs = [(hx[:, 0:TP], "hx"), (t3[:, 0:TP], "t3")]

            def load_hsP(c):
                S.op("act", lambda e, c=c: e.dma_start(
                    out=hsP[c % 2], in_=spool_in[i, c * 128:(c + 1) * 128].rearrange("p b r -> p (b r)")),
                    writes=[("hsP", c % 2)], dma=f"hsP{c % 2}")
            gorder = [3, 2, 1, 0]
            chunks = [(2 * g + cc, g, cc) for g in gorder for cc in range(2)]
            spps = {}

            def b_front(pos):
                c, g, cc = chunks[pos]
                pb, pk = pbufs[pos % 2]
                spp = spps[pos] = panel_mm(wi, 3072 + c * 128)
                S.op("act", lambda e, spp=spp, pb=pb: e.activation(out=v3(pb), in_=ps3(spp), func=AF.Copy),
                     reads=[("ps", spp)], writes=[pk])

            def b_back(pos):
                c, g, cc = chunks[pos]
                w = 2 << g
                pb_ = pos % 2
                pb, pk = pbufs[pos % 2]
                L = 15 + TP
                S.op("dve", lambda e, pb=pb: e.tensor_tensor(
                    out=sa[:, 16:L], in0=pb[:, 1:TP], in1=pb[:, 0:TP - 1], op=ALU.add),
                    reads=[pk], writes=["sa"])
                S.op("dve", lambda e, pb=pb: e.tensor_copy(out=sa[:, 15:16], in_=pb[:, 0:1]),
                     reads=[pk], writes=["sa"])
                src, skey = sa, "sa"
                bufs = [(sa, "sa"), (sbf, "sbf")]
                step = 2
                bi = 1
                while step < w:
                    dst, dkey = bufs[bi]
                    lo = 2 * step - 1
                    S.op("dve", lambda e, src=src, dst=dst, lo=lo, step=step: e.tensor_tensor(
                        out=dst[:, lo:L], in0=src[:, lo:L], in1=src[:, lo - step:L - step], op=ALU.add),
                        reads=[skey], writes=[dkey])
                    src, skey = dst, dkey
                    step *= 2
                    bi ^= 1
                S.op("dve", lambda e, src=src, w=w, pb=pb: e.scalar_tensor_tensor(
                    out=t2[:], in0=src[:, 15:15 + TP], scalar=1.0 / w, in1=pb,
                    op0=ALU.mult, op1=ALU.subtract), reads=[skey, pk], writes=["t2"])
                S.op("dve", lambda e, src=src, g=g: e.tensor_tensor(
                    out=t2[:, M0:M0 + 16], in0=src[:, 15 + M0:15 + M0 + 16],
                    in1=CST[:, C_CORR + g * 16:C_CORR + (g + 1) * 16], op=ALU.mult),
                    reads=[skey, "CST", "t2"], writes=["t2"])
                S.op("dve", lambda e, pb=pb: e.tensor_tensor(
                    out=t2[:, M0:M0 + 16], in0=t2[:, M0:M0 + 16], in1=pb[:, M0:M0 + 16],
                    op=ALU.subtract), reads=[pk, "t2"], writes=["t2"])
                hp = hsP[pb_].rearrange("p (b r) -> p b r", r=15)
                ps_s = pb[:, S0:S0 + NS]
                S.op("dve", lambda e, hp=hp, w=w: e.tensor_reduce(
                    out=st16[:, 0:NS], in_=hp[:, :, 15 - (w - 1):15], op=ALU.add,
                    axis=mybir.AxisListType.X), reads=[("hsP", pb_)], writes=["st16"])
                S.op("dve", lambda e, ps_s=ps_s: e.tensor_tensor(
                    out=st16[:, 0:NS], in0=st16[:, 0:NS], in1=ps_s, op=ALU.add),
                    reads=[pk, "st16"], writes=["st16"])
                S.op("dve", lambda e, ps_s=ps_s, w=w: e.scalar_tensor_tensor(
                    out=t2[:, S0:S0 + NS], in0=st16[:, 0:NS], scalar=1.0 / w, in1=ps_s,
                    op0=ALU.mult, op1=ALU.subtract), reads=["st16", pk, "t2"], writes=["t2"])
                S.op("act", lambda e, cc=cc: e.activation(out=plb[:, cc * TP:(cc + 1) * TP], in_=t2[:], func=AF.Copy),
                     reads=["t2"], writes=["plb"])
                ob = 0
                S.op("act", lambda e, ob=ob, pb=pb: e.activation(out=oP[ob][:, 0:15], in_=pb[:, S0 - 15:S0], func=AF.Copy),
                     reads=[pk], writes=[("oP", ob)])
                opv = oP[ob][:, 15:15 + NS * 15].rearrange("p (b r) -> p b r", r=15)
                S.op("act", lambda e, opv=opv, hp=hp: e.activation(out=opv[:, :, 0:14], in_=hp[:, :, 1:15], func=AF.Copy),
                     reads=[("hsP", pb_)], writes=[("oP", ob)])
                S.op("act", lambda e, opv=opv, ps_s=ps_s: e.activation(out=opv[:, :, 14], in_=ps_s, func=AF.Copy),
                     reads=[pk], writes=[("oP", ob)])
                S.op("act", lambda e, ob=ob, c=c: e.dma_start(out=pool_out[i, c * 128:(c + 1) * 128, :], in_=oP[ob]),
                     reads=[("oP", ob)], dma=f"oP{ob}")
                if pos + 2 < 8:
                    load_hsP(pos + 2)

            def b_yb(q):
                g = gorder[q]
                if wpu[0] is None:
                    wpu[0] = W.get(w_pool[i].rearrange("g (k p) d -> p (g k) d", p=128), dest=wpb)
                u, key = wpu[0]
                uv = u.rearrange("p (a d) -> p a d", d=256)
                for mo in range(2):
                    c = 2 * g + mo
                    s_ = mm_group([uv[:, g * 2 + kk, mo * 128:(mo + 1) * 128] for kk in range(2)],
                                  [plb[:, kk * TP:(kk + 1) * TP] for kk in range(2)], reads=[key, "plb"])
                    S.op("act", lambda e, s_=s_, c=c: e.activation(
                        out=v3(at_(c % 4)), in_=ps3(s_), func=AF.Identity, scale=pvc(f"psc{i}", c)),
                        reads=[("ps", s_), "PV"], writes=["AT"])
                if q % 2 == 1:
                    rowproj(w_out_even[i], 1024 + (g // 2) * 512)

            def load_hsP(pos):
                c = chunks[pos][0]
                S.op("act", lambda e, c=c, pos=pos: e.dma_start(
                    out=hsP[pos % 2], in_=spool_in[i, c * 128:(c + 1) * 128].rearrange("p b r -> p (b r)")),
                    writes=[("hsP", pos % 2)], dma=f"hsP{pos % 2}")
            load_hsP(0)
            load_hsP(1)
            b_front(0)
            b_front(1)
            b_back(0)
            b_back(1)
            for q in range(1, 4):
                b_front(2 * q)
                b_front(2 * q + 1)
                b_yb(q - 1)
                b_back(2 * q)
                b_back(2 * q + 1)
            b_yb(3)

        def ln_accum(c):
            if c == 0:
                S.op("dve", lambda e: e.tensor_copy(out=acc1[:], in_=ln_(0)), reads=["LNB"], writes=["acc1"])
                S.op("dve", lambda e: e.tensor_tensor(out=acc2[:], in0=ln_(0), in1=ln_(0), op=ALU.mult),
                     reads=["LNB"], writes=["acc2"])
            else:
                S.op("pool", lambda e, c=c: e.tensor_tensor(out=acc1[:], in0=acc1[:], in1=ln_(c), op=ALU.add),
                     reads=["LNB", "acc1"], writes=["acc1"])
                S.op("dve", lambda e, c=c: e.tensor_tensor(out=t3[:], in0=ln_(c), in1=ln_(c), op=ALU.mult),
                     reads=["LNB"], writes=["t3"])
                S.op("dve", lambda e: e.tensor_tensor(out=acc2[:], in0=acc2[:], in1=t3[:], op=ALU.add),
                     reads=["t3", "acc2"], writes=["acc2"])

        def ln_stats():
            s1 = stats_broadcast(acc1, "acc1")
            S.op("dve", lambda e, s1=s1: e.tensor_scalar(out=v3(acc1[:]), in0=ps3(s1), scalar1=1.0 / 1024,
                                                         scalar2=None, op0=ALU.mult),
                 reads=[("ps", s1)], writes=["acc1"])
            s2 = stats_broadcast(acc2, "acc2")
            S.op("dve", lambda e: e.tensor_tensor(out=t3[:], in0=acc1[:], in1=acc1[:], op=ALU.mult),
                 reads=["acc1"], writes=["t3"])
            S.op("dve", lambda e, s2=s2: e.scalar_tensor_tensor(
                out=v3(rstd[:]), in0=ps3(s2), scalar=1.0 / 1024, in1=v3(t3[:]),
                op0=ALU.mult, op1=ALU.subtract), reads=[("ps", s2), "t3"], writes=["acc2"])
            S.op("dve", lambda e: e.tensor_scalar(out=rstd[:], in0=rstd[:], scalar1=0.0, scalar2=EPS,
                                                  op0=ALU.max, op1=ALU.add), reads=["acc2"], writes=["acc2"])
            S.op("act", lambda e: e.activation(out=rstd[:], in_=rstd[:], func=AF.Ln),
                 reads=["acc2"], writes=["acc2"])
            S.op("act", lambda e: e.activation(out=rstd[:], in_=rstd[:], func=AF.Exp, scale=-0.5),
                 reads=["acc2"], writes=["acc2"])

        def odd_mixer(l):
            i = l // 2
            rmsnorm(f"nmix{l}")
            wi = w_in_odd[i]
            S.op("sp", lambda e: e.dma_start(out=t3[:, 0:1024], in_=wsT_in[i].rearrange("s h t -> s (h t)")),
                 writes=["t3"], dma="t3")
            for h in range(8):
                S.op("dve", lambda e, h=h: e.tensor_tensor(
                    out=wsTb[:, h * 128:(h + 1) * 128], in0=t3[:, h * 128:(h + 1) * 128],
                    in1=CST[:, C_MASK:C_MASK + 128], op=ALU.mult),
                    reads=["t3", "CST"], writes=["wsTb"])
            for c in range(8):
                sv = panel_mm(wi, 1024 + c * 128)
                S.op("act", lambda e, sv=sv, c=c: e.activation(out=v3(ln_(c)), in_=ps3(sv), func=AF.Gelu),
                     reads=[("ps", sv)], writes=["LNB"])
                ln_accum(c)
            ln_stats()

            def load_bsb(h):
                S.op("act", lambda e, h=h: e.dma_start(out=bsbh[:, (h % 2) * 128:(h % 2 + 1) * 128],
                                                       in_=bsb_in[i, :, h * 128:(h + 1) * 128]),
                     writes=[("bsbh", h % 2)], dma=f"bsbh{h % 2}")
            load_bsb(0)
            def v_normalize(h):
                S.op("dve", lambda e, h=h: e.tensor_tensor(out=t2[:], in0=ln_(h), in1=acc1[:], op=ALU.subtract),
                     reads=["LNB", "acc1"], writes=["t2"])
                S.op("dve", lambda e: e.tensor_tensor(out=t2[:], in0=t2[:], in1=rstd[:], op=ALU.mult),
                     reads=["t2", "acc2"], writes=["t2"])
                S.op("act", lambda e, h=h: e.activation(out=ln_(h), in_=t2[:], func=AF.Identity,
                                                        scale=pvc(f"vg{i}", h), bias=pvc(f"vb{i}", h)),
                     reads=["t2", "PV"], writes=["LNB"])
                S.op("act", lambda e, h=h: e.activation(out=oVh[h % 2], in_=t2[:, S0 - 128:S0 + NS], func=AF.Identity,
                                                        scale=pvc(f"vg{i}", h), bias=pvc(f"vb{i}", h)),
                     reads=["t2", "PV"], writes=[("oV", h % 2)])
                S.op("act", lambda e, h=h: e.dma_start(out=chunkv_out[i, h * 128:(h + 1) * 128, :], in_=oVh[h % 2]),
                     reads=[("oV", h % 2)], dma=f"oV{h % 2}")

            v_normalize(0)
            for h in range(8):
                if h + 1 < 8:
                    load_bsb(h + 1)
                def tfn(pe, h=h):
                    last = None
                    for j in range(NCH):
                        last = pe.transpose(AUXB[:, j * 128:(j + 1) * 128],
                                            ln_(h, G0 + j * 128, G0 + (j + 1) * 128), identb[:])
                    return last
                S.op("pe", tfn, reads=["LNB", "identb"], writes=["auxb"])
                S.op("act", lambda e: e.activation(out=vT[:], in_=AUXB[:, 0:NCH * 128], func=AF.Copy),
                     reads=["auxb"], writes=["vT"])
                su = panel_mm(wi, h * 128)
                S.op("act", lambda e, su=su: e.activation(out=v3(t1[:]), in_=ps3(su), func=AF.Gelu),
                     reads=[("ps", su)], writes=["t1"])
                sgt = next_slot()

                def gfn(pe, h=h, sgt=sgt):
                    last = None
                    for j in range(NCH):
                        o = PS[sgt][:, (j // 3) * 512 + (j % 3) * 128:(j // 3) * 512 + (j % 3) * 128 + 128]
                        last = pe.matmul(o, lhsT=vT[:, j * 128:(j + 1) * 128],
                                         rhs=wsTb[:, h * 128:(h + 1) * 128],
                                         start=True, stop=True)
                    return last
                S.op("pe", gfn, reads=["vT", "wsTb"], writes=[("ps", sgt)])
                if h + 1 < 8:
                    v_normalize(h + 1)
                bs_h = bsbh[:, (h % 2) * 128:(h % 2 + 1) * 128]
                for jb in range(3):
                    gv_ = PS[sgt][:, jb * 512:jb * 512 + 384].rearrange("p (j t) -> p j t", t=128)
                    for jj in range(3):
                        j = jb * 3 + jj
                        c0 = G0 + j * 128
                        S.op("dve", lambda e, gv_=gv_, jj=jj, bs_h=bs_h, c0=c0: e.tensor_tensor(
                            out=t3[:, c0:c0 + 128], in0=gv_[:, jj, :], in1=bs_h, op=ALU.add),
                            reads=[("ps", sgt), ("bsbh", h % 2)], writes=["t3"])
                S.op("dve", lambda e, h=h: e.tensor_tensor(
                    out=at_(h % 4, G0, S0), in0=t3[:, G0:S0], in1=t1[:, G0:S0], op=ALU.mult),
                    reads=["t3", "t1"], writes=["AT"])
                S.op("dve", lambda e, h=h: e.tensor_scalar(
                    out=st16[:, 16:16 + NS], in0=ln_(h, S0, S0 + NS), scalar1=pvc(f"ws00{i}", h),
                    scalar2=pvc(f"bs0{i}", h), op0=ALU.mult, op1=ALU.add),
                    reads=["LNB", "PV"], writes=["st16"])
                S.op("dve", lambda e, h=h: e.tensor_tensor(
                    out=at_(h % 4, S0, S0 + NS), in0=st16[:, 16:16 + NS], in1=t1[:, S0:S0 + NS], op=ALU.mult),
                    reads=["st16", "t1"], writes=["AT"])
                if h % 4 == 3:
                    rowproj(w_out_odd[i], (h // 4) * 512)
            S.op("dve", lambda e: e.memset(hxb[:, 0:30], 0.0), writes=["hx"])

            def load_hsC(c):
                S.op("act", lambda e, c=c: e.dma_start(
                    out=hsC[c % 2], in_=sconf_in[i, c * 128:(c + 1) * 128].rearrange("p b r -> p (b r)")),
                    writes=[("hsC", c % 2)], dma=f"hsC{c % 2}")
            load_hsC(0)
            T2C0 = 2 * NT

            def d_gate(c):
                sgg = panel_mm(wi, 3072 + c * 128)
                S.op("act", lambda e, sgg=sgg: e.activation(out=v3(t1[:]), in_=ps3(sgg), func=AF.Sigmoid),
                     reads=[("ps", sgg)], writes=["t1"])

            def d_glu(c):
                sa_ = panel_mm(wi, 2048 + c * 128)
                S.op("dve", lambda e, sa_=sa_: e.tensor_tensor(
                    out=v3(hxb[:, 30:30 + TP]), in0=ps3(sa_), in1=v3(t1[:]), op=ALU.mult),
                    reads=[("ps", sa_), "t1"], writes=["hx"])
                S.op("dve", lambda e, sa_=sa_: e.tensor_tensor(
                    out=gtail, in0=PS[sa_][:, 1024 + (S0 - 30 - T2C0):1024 + (S0 + NS - T2C0)],
                    in1=t1[:, S0 - 30:S0 + NS], op=ALU.mult),
                    reads=[("ps", sa_), "t1"], writes=["gtail"])

            d_gate(0)
            d_glu(0)
            for c in range(8):
                hb = c % 2
                if c + 1 < 8:
                    load_hsC(c + 1)
                    d_gate(c + 1)
                wc = lambda j, c=c: pvc(f"wcd{i}", c * 31 + j)
                for j in range(31):
                    S.op("dve", lambda e, j=j, wc=wc: e.tensor_scalar(
                        out=AT[:, j * 128:(j + 1) * 128], in0=identb[:], scalar1=wc(j), scalar2=None,
                        op0=ALU.mult), reads=["identb", "PV"], writes=[("dg", j)], extra_wait=["AT"])
                sc = next_slot()

                def cfn(pe, sc=sc, c0=C0[0]):
                    last = None
                    for j in range(31):
                        for n in range(3):
                            a = c0 if n == 0 else 0
                            last = pe.matmul(PS[sc][:, n * 512 + a:n * 512 + NT], lhsT=AT[:, j * 128:(j + 1) * 128],
                                             rhs=hxb[:, j + n * NT + a:j + (n + 1) * NT],
                                             start=(j == 0), stop=(j == 30))
                    return last
                S.op("pe", cfn, reads=["AT", "hx"] + [("dg", j) for j in range(31)], writes=[("ps", sc)])
                S.op("act", lambda e, sc=sc, c=c: e.activation(out=v3(ln_(c)), in_=ps3(sc), func=AF.Identity,
                                                              bias=pvc(f"bcd{i}", c)),
                     reads=[("ps", sc), "PV"], writes=["LNB"])
                hc = hsC[hb].rearrange("p (b r) -> p b r", r=30)
                gl_s = gtail[:, 30:30 + NS]
                wrow = PV[:, PVO[f"wcd{i}"] + c * 31:PVO[f"wcd{i}"] + c * 31 + 30]
                for b in range(NS):
                    S.op("dve", lambda e, b=b, hc=hc, wrow=wrow: e.tensor_tensor(
                        out=t3[:, b * 30:(b + 1) * 30], in0=hc[:, b, :], in1=wrow, op=ALU.mult),
                        reads=[("hsC", hb), "PV"], writes=["t3"])
                S.op("dve", lambda e: e.tensor_reduce(
                    out=st16[:, 32:32 + NS], in_=t3[:, 0:NS * 30].rearrange("p (b r) -> p b r", r=30),
                    op=ALU.add, axis=mybir.AxisListType.X), reads=["t3"], writes=["st16"])
                S.op("dve", lambda e, c=c, wc=wc, gl_s=gl_s: e.tensor_scalar(
                    out=st16[:, 48:48 + NS], in0=gl_s, scalar1=wc(30), scalar2=pvc(f"bcd{i}", c),
                    op0=ALU.mult, op1=ALU.add), reads=["gtail", "PV"], writes=["st16b"])
                S.op("dve", lambda e, c=c: e.tensor_tensor(
                    out=ln_(c, S0, S0 + NS), in0=st16[:, 48:48 + NS], in1=st16[:, 32:32 + NS], op=ALU.add),
                    reads=["st16", "st16b"], writes=["LNB"])
                ln_accum(c)
                ob = 0
                S.op("act", lambda e, ob=ob: e.activation(out=oC[ob][:, 0:30], in_=gtail[:, 0:30], func=AF.Copy),
                     reads=["gtail"], writes=[("oC", ob)])
                ocv = oC[ob][:, 30:30 + NS * 30].rearrange("p (b r) -> p b r", r=30)
                S.op("act", lambda e, ocv=ocv, hc=hc: e.activation(out=ocv[:, :, 0:29], in_=hc[:, :, 1:30], func=AF.Copy),
                     reads=[("hsC", hb)], writes=[("oC", ob)])
                S.op("act", lambda e, ocv=ocv, gl_s=gl_s: e.activation(out=ocv[:, :, 29], in_=gl_s, func=AF.Copy),
                     reads=["gtail"], writes=[("oC", ob)])
                S.op("act", lambda e, ob=ob, c=c: e.dma_start(out=conf_out[i, c * 128:(c + 1) * 128, :], in_=oC[ob]),
                     reads=[("oC", ob)], dma=f"oC{ob}")
                if c + 1 < 8:
                    d_glu(c + 1)
            ln_stats()
            tb = [(t2, "t2"), (t3, "t3"), (t2, "t2"), (t3, "t3"), (t1, "t1"), (hx[:, 0:TP], "hx"), (t2, "t2"), (t3, "t3")]

            def yd_dve(c):
                tt, tk = tb[c]
                S.op("dve", lambda e, c=c, tt=tt: e.tensor_tensor(out=tt[:], in0=ln_(c), in1=acc1[:], op=ALU.subtract),
                     reads=["LNB", "acc1"], writes=[tk])
                S.op("dve", lambda e, tt=tt: e.tensor_tensor(out=tt[:], in0=tt[:], in1=rstd[:], op=ALU.mult),
                     reads=[tk, "acc2"], writes=[tk])

            def yd_act(c):
                tt, tk = tb[c]
                S.op("act", lambda e, c=c, tt=tt: e.activation(out=at_(c % 4), in_=tt[:], func=AF.Silu,
                                                               scale=pvc(f"cg{i}", c), bias=pvc(f"cb{i}", c)),
                     reads=[tk, "PV"], writes=["AT"])
            for c in range(4):
                yd_dve(c)
                yd_act(c)
            for c in range(4, 8):
                yd_dve(c)
            rowproj(w_out_odd[i], 1024)
            for c in range(4, 8):
                yd_act(c)
            rowproj(w_out_odd[i], 1024 + 512)
            for k in range(KC):
                S.op("dve", lambda e, k=k: e.tensor_scalar(
                    out=x_(k, 0, HALO), in0=x_(k, 0, HALO), scalar1=CST[:, C_HM:C_HM + 1], scalar2=None,
                    op0=ALU.mult), reads=[("x", k), "CST"], writes=[("x", k)])

        def program():
            for l in range(n_layers):
                cur_layer[0] = l
                C0[0] = CIN[l]
                if l % 2 == 0:
                    even_mixer(l)
                else:
                    odd_mixer(l)
                ffn(l)
            rmsnorm("nfin", final=True)

        S.dry = True
        program()
        S.dry = False
        slot_ctr[0] = 0
        program()
        S.emit(nc)
    return nc


def _host_inputs(inp):
    f = np.float32
    g = {k: np.asarray(v) for k, v in inp.items()}
    pv = np.zeros((128, NPV), f)

    def put(name, rows):
        rows = np.asarray(rows, f)
        pv[:, PVO[name]:PVO[name] + rows.shape[0]] = rows.T
    for l in range(4):
        put(f"nmix{l}", g["norm_mix"][l].reshape(16, 128))
        put(f"nffn{l}", g["norm_ffn"][l].reshape(16, 128))
    put("nfin", g["norm_final"].reshape(16, 128))
    for i in range(2):
        put(f"wca{i}", g["w_conv_a"][i].reshape(3, 8, 128).transpose(1, 0, 2).reshape(24, 128))
        put(f"psc{i}", g["pool_scale"][i].reshape(8, 128))
        put(f"vg{i}", g["v_norm_g"][i].reshape(8, 128))
        put(f"vb{i}", g["v_norm_b"][i].reshape(8, 128))
        put(f"bcd{i}", g["b_conv_d"][i].reshape(8, 128))
        put(f"cg{i}", g["conf_norm_g"][i].reshape(8, 128))
        put(f"cb{i}", g["conf_norm_b"][i].reshape(8, 128))
        put(f"wcd{i}", g["w_conv_d"][i].reshape(31, 8, 128).transpose(1, 0, 2).reshape(248, 128))
        put(f"ws00{i}", np.broadcast_to(g["w_spatial"][i, :, 0, 0][:, None], (8, 128)))
        put(f"bs0{i}", np.broadcast_to(g["b_spatial"][i, :, 0][:, None], (8, 128)))
    wsT = np.ascontiguousarray(g["w_spatial"].transpose(0, 3, 1, 2)).astype(f)
    bsb = np.ascontiguousarray(np.broadcast_to(g["b_spatial"].reshape(2, 1, 1024), (2, 128, 1024))).astype(f)
    shared = dict(pv=pv, wsT=wsT, bsb=bsb)
    for k in ("w_in_even", "w_pool", "w_out_even", "w_in_odd", "w_out_odd",
              "w_ffn_gate", "w_ffn_up", "w_ffn_down"):
        shared[k] = np.ascontiguousarray(g[k], dtype=f)
    xp, xs = g["x_prompt"], g["x_sample"]
    maps = []
    ss = np.arange(128)
    for c in range(8):
        b, half = c // 2, c % 2
        xT = np.zeros((D, T), f)
        if half:
            xT[:, 0:HALO] = xp[b, MAIN - HALO:MAIN].T
        xT[:, M0:S0] = xp[b, half * MAIN:(half + 1) * MAIN].T
        xT[:, S0:T] = xs[c * NS:(c + 1) * NS, 0].T
        cst = np.zeros((128, NCST), f)
        cst[:, C_ID:C_ID + 128] = np.eye(128, dtype=f)
        cst[:, C_MASK:C_MASK + 128] = (ss[:, None] <= ss[None, :]).astype(f)
        cst[:, C_HM] = float(half)
        for gi, w in enumerate((2, 4, 8, 16)):
            pos = half * MAIN + np.arange(16)
            cst[:, C_CORR + gi * 16:C_CORR + (gi + 1) * 16] = (1.0 / np.minimum(w, pos + 1))[None, :]
        m = dict(shared)
        m["xT"] = xT
        m["cst"] = cst
        sl = slice(c * NS, (c + 1) * NS)
        m["sconv"] = np.ascontiguousarray(g["state_conv_a"][:, sl].transpose(0, 3, 1, 2)).astype(f)
        m["spool"] = np.ascontiguousarray(g["state_pool"][:, sl].transpose(0, 3, 1, 2)).astype(f)
        m["sconf"] = np.ascontiguousarray(g["state_conformer"][:, sl].transpose(0, 3, 1, 2)).astype(f)
        maps.append(m)
    return maps


def _assemble(res):
    f = np.float32
    y_prompt = np.zeros((4, 2048, D), f)
    y_sample = np.zeros((128, 1, D), f)
    conv_a_prompt = np.zeros((2, 4, 2, 1024), f)
    conv_a_sample = np.zeros((2, 128, 2, 1024), f)
    pool_prompt = np.zeros((2, 4, 15, 1024), f)
    pool_sample = np.zeros((2, 128, 15, 1024), f)
    chunk_v_prompt = np.zeros((2, 4, 128, 1024), f)
    chunk_v_sample = np.zeros((2, 128, 1, 1024), f)
    conformer_prompt = np.zeros((2, 4, 30, 1024), f)
    conformer_sample = np.zeros((2, 128, 30, 1024), f)
    for c in range(8):
        r = res[c]
        b, half = c // 2, c % 2
        sl = slice(c * NS, (c + 1) * NS)
        yT = np.asarray(r["yT"])
        y_prompt[b, half * MAIN:(half + 1) * MAIN] = yT[:, :MAIN].T
        y_sample[sl, 0] = yT[:, MAIN:].T
        ca, po, cv, cf = (np.asarray(r[k]) for k in ("conva", "pool", "chunkv", "conf"))
        conv_a_sample[:, sl] = ca[:, :, 2:].reshape(2, 1024, NS, 2).transpose(0, 2, 3, 1)
        pool_sample[:, sl] = po[:, :, 15:].reshape(2, 1024, NS, 15).transpose(0, 2, 3, 1)
        chunk_v_sample[:, sl, 0] = cv[:, :, 128:].transpose(0, 2, 1)
        conformer_sample[:, sl] = cf[:, :, 30:].reshape(2, 1024, NS, 30).transpose(0, 2, 3, 1)
        if half:
            conv_a_prompt[:, b] = ca[:, :, 0:2].transpose(0, 2, 1)
            pool_prompt[:, b] = po[:, :, 0:15].transpose(0, 2, 1)
            chunk_v_prompt[:, b] = cv[:, :, 0:128].transpose(0, 2, 1)
            conformer_prompt[:, b] = cf[:, :, 0:30].transpose(0, 2, 1)
    return (y_prompt, y_sample, conv_a_prompt, conv_a_sample, pool_prompt, pool_sample,
            chunk_v_prompt, chunk_v_sample, conformer_prompt, conformer_sample)


_NC_CACHE = {}


def kernel(**inputs):
    maps = _host_inputs(inputs)
    if "nc" not in _NC_CACHE:
        _NC_CACHE["nc"] = build_program()
    nc = _NC_CACHE["nc"]
    res = run_bass_kernel_spmd(nc, maps, core_ids=list(range(8)))
    return _assemble(res.results)
```

```python
import contextlib
import numpy as np
import concourse.bass as bass
import concourse.mybir as mybir
from concourse.bass_utils import run_bass_kernel_spmd

F32 = mybir.dt.float32
BF16 = mybir.dt.bfloat16
AF = mybir.ActivationFunctionType
ALU = mybir.AluOpType

D = 2048
KC = 16
HALO = 144
MAIN = 1024
NS = 16
T = HALO + MAIN + NS
TP = 1188
NT = 396
PADC = TP - T
M0 = HALO
S0 = HALO + MAIN
G0 = 16
NCH = 9
DFF = 5632
NFB = DFF // 512
EPS = 1e-6
UNIT = 2048
NOUT = MAIN + NS
DEPTH = 4


def _pv_layout():
    off = {}
    n = 0

    def add(name, cols):
        nonlocal n
        off[name] = n
        n += cols
    for l in range(4):
        add(f"nmix{l}", 16)
    for l in range(4):
        add(f"nffn{l}", 16)
    add("nfin", 16)
    for i in range(2):
        add(f"wca{i}", 24)
        add(f"psc{i}", 8)
        add(f"vg{i}", 8)
        add(f"vb{i}", 8)
        add(f"bcd{i}", 8)
        add(f"cg{i}", 8)
        add(f"cb{i}", 8)
        add(f"wcd{i}", 248)
        add(f"ws00{i}", 8)
        add(f"bs0{i}", 8)
    return off, n


PVO, NPV = _pv_layout()
C_ID = 0
C_MASK = 128
C_HM = 256
C_CORR = 257
NCST = 257 + 64


class Sched:
    def __init__(self):
        self.prog = {e: [] for e in ("pe", "act", "dve", "pool", "sp")}
        self.cnt = {}
        self.known = {e: {} for e in self.prog}
        self.lw = {}
        self.rd = {}
        self.dry = False
        self.alias = {}

    def op(self, e, fn, reads=(), writes=(), dma=None, extra_wait=()):
        if self.dry:
            return
        deps = {}
        extra_wait = list(extra_wait)
        for b in writes:
            extra_wait.extend(self.alias.get(b, ()))

        def add(tok):
            if tok is None:
                return
            s, v = tok
            if deps.get(s, 0) < v:
                deps[s] = v
        for b in reads:
            add(self.lw.get(b))
        for b in list(writes) + list(extra_wait):
            add(self.lw.get(b))
            for s, v in self.rd.get(b, {}).items():
                add((s, v))
        for s, v in deps.items():
            if e == "pe" and s == "pe":
                continue
            if self.known[e].get(s, 0) >= v:
                continue
            self.known[e][s] = v
            self.prog[e].append(("w", s, v))
        if dma is None:
            s, amt = e, 1
        else:
            s, amt = "dma_" + dma, 16
        self.cnt[s] = self.cnt.get(s, 0) + amt
        tok = (s, self.cnt[s])
        self.prog[e].append(("o", fn, s, amt))
        for b in reads:
            r = self.rd.setdefault(b, {})
            if r.get(tok[0], 0) < tok[1]:
                r[tok[0]] = tok[1]
        for b in writes:
            self.lw[b] = tok
            self.rd[b] = {}

    def emit(self, nc):
        sems = {s: nc.alloc_semaphore(name=s) for s in self.cnt}
        for s, v in self.cnt.items():
            self.prog["sp"].append(("w", s, v))

        def run(e, eng):
            for it in self.prog[e]:
                if it[0] == "w":
                    eng.wait_ge(sems[it[1]], it[2])
                else:
                    ins = it[1](eng)
                    ins.then_inc(sems[it[2]], it[3])
        with nc.Block() as block:
            @block.tensor
            def _(eng):
                run("pe", eng)

            @block.scalar
            def _(eng):
                run("act", eng)

            @block.vector
            def _(eng):
                run("dve", eng)

            @block.gpsimd
            def _(eng):
                run("pool", eng)

            @block.sync
            def _(eng):
                run("sp", eng)


def build_program(n_layers=DEPTH):
    nc = bass.Bass("TRN2", target_bir_lowering=False)
    S = Sched()

    def din(name, shape):
        return nc.dram_tensor(name, list(shape), F32, kind="ExternalInput").ap()

    def dout(name, shape):
        return nc.dram_tensor(name, list(shape), F32, kind="ExternalOutput").ap()

    xT_in = din("xT", [D, T])
    pv_in = din("pv", [128, NPV])
    cst_in = din("cst", [128, NCST])
    sconv_in = din("sconv", [2, 1024, NS, 2])
    spool_in = din("spool", [2, 1024, NS, 15])
    sconf_in = din("sconf", [2, 1024, NS, 30])
    wsT_in = din("wsT", [2, 128, 8, 128])
    bsb_in = din("bsb", [2, 128, 1024])
    w_in_even = din("w_in_even", [2, D, 4096])
    w_pool = din("w_pool", [2, 4, 256, 256])
    w_out_even = din("w_out_even", [2, D, D])
    w_in_odd = din("w_in_odd", [2, D, 4096])
    w_out_odd = din("w_out_odd", [2, D, D])
    w_gate = din("w_ffn_gate", [4, D, DFF])
    w_up = din("w_ffn_up", [4, D, DFF])
    w_down = din("w_ffn_down", [4, DFF, D])

    yT_out = dout("yT", [D, NOUT])
    conva_out = dout("conva", [2, 1024, 2 + NS * 2])
    pool_out = dout("pool", [2, 1024, 15 + NS * 15])
    chunkv_out = dout("chunkv", [2, 1024, 128 + NS])
    conf_out = dout("conf", [2, 1024, 30 + NS * 30])

    es = contextlib.ExitStack()

    def sb(name, shape, dt=F32):
        return es.enter_context(nc.sbuf_tensor("s_" + name, list(shape), dt))

    def ps(name, shape, dt=F32):
        return es.enter_context(nc.psum_tensor("p_" + name, list(shape), dt))

    with es:
        xT = sb("xT", [128, KC * TP])
        hT = sb("hT", [128, KC * TP], BF16)
        AT = sb("AT", [128, 4 * TP], BF16)
        RG = sb("RG", [128, 4 * TP])
        LNB = RG[:].bitcast(BF16)
        sa = RG[:, 0:1204]
        sbf = RG[:, 1204:2408]
        plb = RG[:, 2408:2408 + TP].bitcast(BF16)
        wpb = RG[:, 3596:3596 + 1024].bitcast(BF16)
        stg = [sb(f"stg{i}", [128, UNIT]) for i in range(2)]
        ring = [sb(f"ring{i}", [128, UNIT], BF16) for i in range(2)]
        PV = sb("PV", [128, NPV])
        CST = sb("CST", [128, NCST])
        identb = sb("identb", [128, 128], BF16)
        ones32 = sb("ones32", [128, 128])
        onesb = sb("onesb", [128, 128], BF16)
        epsc = sb("epsc", [128, 1])
        wsTb = sb("wsTb", [128, 8 * 128], BF16)
        bsbh = sb("bsbh", [128, 256])
        t1 = sb("t1", [128, TP])
        t2 = sb("t2", [128, TP])
        t3 = sb("t3", [128, TP])
        acc1 = sb("acc1", [128, TP])
        acc2 = sb("acc2", [128, TP])
        rstd = acc2
        hx = sb("hx", [128, 32 + TP])
        vT = hx[:, 0:576].bitcast(BF16)
        hxb = hx[:, 0:612].bitcast(BF16)
        gtail = hx[:, 700:746]
        MISC = sb("MISC", [128, 1760])
        hsA = MISC[:, 0:256]
        hsP = [MISC[:, 256:496], MISC[:, 496:736]]
        oA = MISC[:, 736:1008]
        oP = [MISC[:, 1008:1263]]
        hsC = [MISC[:, 0:480], MISC[:, 480:960]]
        oC = [MISC[:, 960:1470]]
        oVh = [MISC[:, 1470:1614], MISC[:, 1614:1758]]
        oY = [t1[:, 0:NOUT], t3[:, 0:NOUT], t2[:, 0:NOUT], hx[:, 0:NOUT]]
        oYk = ["t1", "t3", "t2", "hx"]
        st16 = sb("st16", [128, 64])
        PS = [ps("psA", [128, 1536]), ps("psB", [128, 1536])]
        AUXB = ps("auxb", [128, 2048], BF16)

        def x_(k, a=0, b=TP):
            return xT[:, k * TP + a:k * TP + b]

        def h_(k, a=0, b=TP):
            return hT[:, k * TP + a:k * TP + b]

        def at_(k, a=0, b=TP):
            return AT[:, k * TP + a:k * TP + b]

        def ln_(k, a=0, b=TP):
            return LNB[:, k * TP + a:k * TP + b]

        def v3(ap):
            return ap.rearrange("p (n c) -> p n c", c=NT)

        def ps3(s):
            return PS[s].rearrange("p (n c) -> p n c", c=512)[:, :, 0:NT]

        def pvc(name, j=0):
            o = PVO[name] + j
            return PV[:, o:o + 1]

        class WStream:
            def __init__(self):
                self.units = []
                self.nd = 0
                self.ncast = 0
                self.nu = 0

            def get(self, ap, dest=None):
                if S.dry:
                    self.units.append((ap, dest))
                    return ring[0], ("wr", 0)
                u = self.nu
                self.nu += 1
                n = len(self.units)
                tgt = min(u + 1, n - 1)
                while self.ncast <= tgt:
                    while self.nd <= self.ncast:
                        self._dma(self.nd)
                    self._cast(self.ncast)
                while self.nd < n and self.nd - 2 < self.ncast:
                    self._dma(self.nd)
                if self.units[u][1] is not None:
                    return self.units[u][1], "wpb"
                return ring[u % 2], ("wr", u % 2)

            def _dma(self, v):
                ap = self.units[v][0]
                sl = v % 2
                shp = ap.shape
                if len(shp) == 3:
                    o = stg[sl].rearrange("p (a b) -> p a b", b=shp[2])
                else:
                    o = stg[sl]
                S.op("sp", lambda e, o=o, ap=ap: e.dma_start(out=o, in_=ap),
                     writes=[("stg", sl)], dma=f"w{sl}")
                self.nd += 1

            def _cast(self, v):
                sl = v % 2
                dest = self.units[v][1]
                if dest is not None:
                    S.op("act", lambda e, sl=sl, dest=dest: e.activation(out=dest[:], in_=stg[sl][:], func=AF.Copy),
                         reads=[("stg", sl)], writes=["wpb"])
                else:
                    S.op("act", lambda e, sl=sl: e.activation(out=ring[sl][:], in_=stg[sl][:], func=AF.Copy),
                         reads=[("stg", sl)], writes=[("wr", sl)])
                self.ncast += 1

        _even_misc = ["hsA", ("hsP", 0), ("hsP", 1), "oA", ("oP", 0)]
        _odd_misc = [("hsC", 0), ("hsC", 1), ("oC", 0), ("oV", 0), ("oV", 1)]
        for k_ in _even_misc:
            S.alias[k_] = _odd_misc
        for k_ in _odd_misc:
            S.alias[k_] = _even_misc
        _even_rg = ["sa", "sbf", "plb", "wpb"]
        for k_ in _even_rg:
            S.alias[k_] = ["LNB"]
        S.alias["LNB"] = _even_rg
        S.alias["hx"] = ["vT", "gtail"]
        S.alias["vT"] = ["hx"]
        S.alias["gtail"] = ["hx"]
        W = WStream()
        C0 = [0]
        CIN = [0, 16, 96, 112]
        COUT = [16, 96, 112, 144]
        slot_ctr = [0]
        deferred = []

        def defer(fn, **kw):
            deferred.append((fn, kw))

        def flush():
            while deferred:
                fn, kw = deferred.pop(0)
                S.op("sp", fn, **kw)

        def next_slot():
            s = slot_ctr[0] % 2
            slot_ctr[0] += 1
            return s

        def mm_group(lhs, rhs, reads):
            s = next_slot()

            def fn(pe, lhs=lhs, rhs=rhs, s=s, c0=C0[0]):
                last = None
                nk = len(lhs)
                for k in range(nk):
                    for n in range(3):
                        a = c0 if n == 0 else 0
                        e = PADC if n == 2 else 0
                        last = pe.matmul(PS[s][:, n * 512 + a:n * 512 + NT - e], lhsT=lhs[k],
                                         rhs=rhs[k][:, n * NT + a:(n + 1) * NT - e],
                                         start=(k == 0), stop=(k == nk - 1))
                return last
            S.op("pe", fn, reads=reads, writes=[("ps", s)])
            return s

        first_after_norm = [False]

        def panel_mm(wap, col0):
            u, key = W.get(wap[:, col0:col0 + 128].rearrange("(k p) c -> p k c", p=128))
            uv = u.rearrange("p (k c) -> p k c", c=128)
            if first_after_norm[0]:
                first_after_norm[0] = False
                s = next_slot()
                for k in range(KC):
                    def fn(pe, k=k, s=s, uv=uv, c0=C0[0]):
                        last = None
                        for n in range(3):
                            a = c0 if n == 0 else 0
                            e = PADC if n == 2 else 0
                            last = pe.matmul(PS[s][:, n * 512 + a:n * 512 + NT - e], lhsT=uv[:, k, :],
                                             rhs=h_(k, n * NT + a, (n + 1) * NT - e), start=(k == 0), stop=(k == KC - 1))
                        return last
                    S.op("pe", fn, reads=[key, ("hT", k)], writes=[("ps", s)])
                return s
            return mm_group([uv[:, k, :] for k in range(KC)], [h_(k) for k in range(KC)],
                            reads=[key] + [("hT", k) for k in range(KC)])

        cur_layer = [0]

        def rowproj(wap, row0, nk=4):
            saved = C0[0]
            C0[0] = COUT[cur_layer[0]]
            _rowproj(wap, row0, nk)
            C0[0] = saved

        def _rowproj(wap, row0, nk=4):
            for q in range(4):
                u, key = W.get(wap[row0:row0 + nk * 128, q * 512:(q + 1) * 512]
                               .rearrange("(j p) c -> p j c", p=128))
                uv = u.rearrange("p (j c) -> p j c", c=512)
                for mm in range(4):
                    m = q * 4 + mm
                    s = mm_group([uv[:, j, mm * 128:(mm + 1) * 128] for j in range(nk)],
                                 [at_(j) for j in range(nk)], reads=[key, "AT"])
                    S.op("dve", lambda e, m=m, s=s: e.tensor_tensor(
                        out=v3(x_(m)), in0=ps3(s), in1=v3(x_(m)), op=ALU.add),
                        reads=[("ps", s), ("x", m)], writes=[("x", m)])

        S.op("sp", lambda e: e.dma_start(out=PV[:], in_=pv_in), writes=["PV"], dma="pv")
        S.op("sp", lambda e: e.dma_start(out=CST[:], in_=cst_in), writes=["CST"], dma="cst")
        for q in range(4):
            S.op("sp", lambda e, q=q: e.dma_start(
                out=xT.rearrange("p (k t) -> p k t", t=TP)[:, 4 * q:4 * q + 4, 0:T],
                in_=xT_in[512 * q:512 * (q + 1), :].rearrange("(k p) t -> p k t", p=128)),
                writes=[("x", k) for k in range(4 * q, 4 * q + 4)], dma=f"x{q}")
        S.op("dve", lambda e: e.tensor_copy(out=identb[:], in_=CST[:, C_ID:C_ID + 128]),
             reads=["CST"], writes=["identb"])
        S.op("dve", lambda e: e.memset(ones32[:], 1.0), writes=["ones32"])
        S.op("dve", lambda e: e.memset(onesb[:], 1.0), writes=["onesb"])
        S.op("dve", lambda e: e.memset(epsc[:], EPS), writes=["epsc"])
        S.op("pool", lambda e: e.memset(xT.rearrange("p (k t) -> p k t", t=TP)[:, :, T:TP], 0.0),
             writes=[("x", k) for k in range(KC)])
        S.op("pool", lambda e: e.memset(AT[:], 0.0), writes=["AT"])
        S.op("pool", lambda e: e.memset(hx[:], 0.0), writes=["hx"])
        S.op("pool", lambda e: e.memset(RG[:], 0.0), writes=["sa", "sbf", "LNB", "plb", "wpb"])

        def stats_broadcast(acc_ap, acc_key):
            s = next_slot()

            def fn(pe, s=s):
                last = None
                for n in range(3):
                    last = pe.matmul(PS[s][:, n * 512:n * 512 + NT], lhsT=ones32[:],
                                     rhs=acc_ap[:, n * NT:(n + 1) * NT], start=True, stop=True)
                return last
            S.op("pe", fn, reads=[acc_key, "ones32"], writes=[("ps", s)])
            return s

        def rmsnorm(gname, final=False):
            sqb = [t2[:, 0:TP // 2].bitcast(BF16), t2[:, TP // 2:TP].bitcast(BF16),
                   t3[:, 0:TP // 2].bitcast(BF16), t3[:, TP // 2:TP].bitcast(BF16)]
            s = next_slot()
            for k in range(KC):
                b = k % 4
                if k % 2 == 0:
                    S.op("act", lambda e, k=k, b=b: e.activation(out=sqb[b], in_=x_(k), func=AF.Square),
                         reads=[("x", k)], writes=[("sq", b)], extra_wait=["t2" if b < 2 else "t3"])
                else:
                    S.op("dve", lambda e, k=k, b=b: e.tensor_tensor(out=sqb[b], in0=x_(k), in1=x_(k), op=ALU.mult),
                         reads=[("x", k)], writes=[("sq", b)], extra_wait=["t2" if b < 2 else "t3"])

                def sfn(pe, k=k, b=b, s=s):
                    last = None
                    for n in range(3):
                        last = pe.matmul(PS[s][:, n * 512:n * 512 + NT], lhsT=onesb[:],
                                         rhs=sqb[b][:, n * NT:(n + 1) * NT], start=(k == 0), stop=(k == KC - 1))
                    return last
                S.op("pe", sfn, reads=[("sq", b), "onesb"], writes=[("ps", s)])
            S.op("act", lambda e, s=s: e.activation(out=v3(rstd[:]), in_=ps3(s), func=AF.Ln, scale=1.0 / D, bias=epsc[:]),
                 reads=[("ps", s), "epsc"], writes=["acc2", "t2", "t3"])
            S.op("act", lambda e: e.activation(out=rstd[:], in_=rstd[:], func=AF.Exp, scale=-0.5),
                 reads=["acc2"], writes=["acc2"])
            if not final:
                for k in range(KC):
                    S.op("dve", lambda e, k=k: e.scalar_tensor_tensor(
                        out=h_(k), in0=x_(k), scalar=pvc(gname, k), in1=rstd[:],
                        op0=ALU.mult, op1=ALU.mult),
                        reads=[("x", k), "acc2", "PV"], writes=[("hT", k)])
                first_after_norm[0] = True
            else:
                for k in range(KC):
                    b = k % 4
                    S.op("dve", lambda e, k=k, b=b: e.scalar_tensor_tensor(
                        out=oY[b][:], in0=x_(k, M0, T), scalar=pvc(gname, k), in1=rstd[:, M0:T],
                        op0=ALU.mult, op1=ALU.mult),
                        reads=[("x", k), "acc2", "PV"], writes=[oYk[b]])
                    S.op("sp", lambda e, k=k, b=b: e.dma_start(out=yT_out[k * 128:(k + 1) * 128, :], in_=oY[b][:]),
                         reads=[oYk[b]], dma=f"oY{b}")

        def ffn(l):
            C0[0] = COUT[l]
            rmsnorm(f"nffn{l}")
            for blk in range(NFB):
                for c in range(4):
                    f = blk * 4 + c
                    sg = panel_mm(w_gate[l], f * 128)
                    S.op("act", lambda e, sg=sg: e.activation(out=v3(t1[:]), in_=ps3(sg), func=AF.Silu),
                         reads=[("ps", sg)], writes=["t1"])
                    su = panel_mm(w_up[l], f * 128)
                    S.op("dve", lambda e, su=su, c=c: e.tensor_tensor(
                        out=v3(at_(c)), in0=ps3(su), in1=v3(t1[:]), op=ALU.mult),
                        reads=[("ps", su), "t1"], writes=["AT"])
                rowproj(w_down[l], blk * 512)

        def even_mixer(l):
            i = l // 2
            rmsnorm(f"nmix{l}")
            wi = w_in_even[i]
            S.op("act", lambda e: e.dma_start(
                out=hsA.rearrange("p (c s) -> p c s", s=NS * 2),
                in_=sconv_in[i].rearrange("(c p) b r -> p c (b r)", p=128)),
                writes=["hsA"], dma="hsA")
            oAv = oA.rearrange("p (c s) -> p c s", s=2 + NS * 2)
            S.op("dve", lambda e: e.memset(hx[:, 0:2], 0.0), writes=["hx"])
            for c in range(8):
                sx = panel_mm(wi, c * 128)
                S.op("act", lambda e, sx=sx: e.activation(out=v3(t1[:]), in_=ps3(sx), func=AF.Copy),
                     reads=[("ps", sx)], writes=["t1"])
                sp_ = panel_mm(wi, 1024 + c * 128)
                S.op("dve", lambda e, sp_=sp_: e.tensor_tensor(
                    out=v3(hx[:, 2:2 + TP]), in0=ps3(sp_), in1=v3(t1[:]), op=ALU.mult),
                    reads=[("ps", sp_), "t1"], writes=["hx"])
                spo = panel_mm(wi, 2048 + c * 128)
                w0, w1, w2 = (pvc(f"wca{i}", c * 3 + j) for j in range(3))
                S.op("dve", lambda e, w2=w2: e.tensor_scalar(
                    out=t2[:], in0=hx[:, 2:2 + TP], scalar1=w2, scalar2=None, op0=ALU.mult),
                    reads=["hx", "PV"], writes=["t2"])
                S.op("dve", lambda e, w1=w1: e.scalar_tensor_tensor(
                    out=t2[:], in0=hx[:, 1:1 + TP], scalar=w1, in1=t2[:], op0=ALU.mult, op1=ALU.add),
                    reads=["hx", "t2"], writes=["t2"])
                S.op("dve", lambda e, w0=w0: e.scalar_tensor_tensor(
                    out=t2[:], in0=hx[:, 0:TP], scalar=w0, in1=t2[:], op0=ALU.mult, op1=ALU.add),
                    reads=["hx", "t2"], writes=["t2"])
                hs = hsA[:, c * NS * 2:(c + 1) * NS * 2].rearrange("p (b r) -> p b r", r=2)
                gs = hx[:, 2 + S0:2 + S0 + NS]
                S.op("dve", lambda e, w2=w2, gs=gs: e.tensor_scalar(
                    out=t2[:, S0:S0 + NS], in0=gs, scalar1=w2, scalar2=None, op0=ALU.mult),
                    reads=["hx", "t2"], writes=["t2"])
                S.op("dve", lambda e, w1=w1, hs=hs: e.scalar_tensor_tensor(
                    out=t2[:, S0:S0 + NS], in0=hs[:, :, 1], scalar=w1, in1=t2[:, S0:S0 + NS],
                    op0=ALU.mult, op1=ALU.add), reads=["hsA", "t2"], writes=["t2"])
                S.op("dve", lambda e, w0=w0, hs=hs: e.scalar_tensor_tensor(
                    out=t2[:, S0:S0 + NS], in0=hs[:, :, 0], scalar=w0, in1=t2[:, S0:S0 + NS],
                    op0=ALU.mult, op1=ALU.add), reads=["hsA", "t2"], writes=["t2"])
                S.op("dve", lambda e, spo=spo, c=c: e.tensor_tensor(
                    out=v3(at_(c % 4)), in0=ps3(spo), in1=v3(t2[:]), op=ALU.mult),
                    reads=[("ps", spo), "t2"], writes=["AT"])
                S.op("act", lambda e, c=c: e.activation(out=oAv[:, c, 0:2], in_=hx[:, 2 + S0 - 2:2 + S0], func=AF.Copy),
                     reads=["hx"], writes=["oA"])
                osv = oAv[:, c, 2:2 + NS * 2].rearrange("p (b r) -> p b r", r=2)
                S.op("act", lambda e, osv=osv, hs=hs: e.activation(out=osv[:, :, 0], in_=hs[:, :, 1], func=AF.Copy),
                     reads=["hsA"], writes=["oA"])
                S.op("act", lambda e, osv=osv, gs=gs: e.activation(out=osv[:, :, 1], in_=gs, func=AF.Copy),
                     reads=["hx"], writes=["oA"])
                if c % 4 == 3:
                    rowproj(w_out_even[i], (c // 4) * 512)
            S.op("act", lambda e: e.dma_start(out=conva_out[i].rearrange("(c p) s -> p c s", p=128), in_=oAv),
                 reads=["oA"], dma="oA")
            wpu = [None]
            S.op("dve", lambda e: e.memset(sa[:, 0:15], 0.0), writes=["sa"])
            S.op("dve", lambda e: e.memset(sbf[:, 0:15], 0.0), writes=["sbf"])
            pbufs = [(hx[:, 0:TP], "hx"), (t3[:, 0:TP], "t3")]

            def load_hsP(c):
                S.op("act", lambda e, c=c: e.dma_start(
                    out=hsP[c % 2], in_=spool_in[i, c * 128:(c + 1) * 128].rearrange("p b r -> p (b r)")),
                    writes=[("hsP", c % 2)], dma=f"hsP{c % 2}")
            gorder = [3, 2, 1, 0]
            chunks = [(2 * g + cc, g, cc) for g in gorder for cc in range(2)]
            spps = {}

            def b_front(pos):
                c, g, cc = chunks[pos]
                pb, pk = pbufs[pos % 2]
                spp = spps[pos] = panel_mm(wi, 3072 + c * 128)
                S.op("act", lambda e, spp=spp, pb=pb: e.activation(out=v3(pb), in_=ps3(spp), func=AF.Copy),
                     reads=[("ps", spp)], writes=[pk])

            def b_back(pos):
                c, g, cc = chunks[pos]
                w = 2 << g
                pb_ = pos % 2
                pb, pk = pbufs[pos % 2]
                L = 15 + TP
                S.op("dve", lambda e, pb=pb: e.tensor_tensor(
                    out=sa[:, 16:L], in0=pb[:, 1:TP], in1=pb[:, 0:TP - 1], op=ALU.add),
                    reads=[pk], writes=["sa"])
                S.op("dve", lambda e, pb=pb: e.tensor_copy(out=sa[:, 15:16], in_=pb[:, 0:1]),
                     reads=[pk], writes=["sa"])
                src, skey = sa, "sa"
                bufs = [(sa, "sa"), (sbf, "sbf")]
                step = 2
                bi = 1
                while step < w:
                    dst, dkey = bufs[bi]
                    lo = 2 * step - 1
                    S.op("dve", lambda e, src=src, dst=dst, lo=lo, step=step: e.tensor_tensor(
                        out=dst[:, lo:L], in0=src[:, lo:L], in1=src[:, lo - step:L - step], op=ALU.add),
                        reads=[skey], writes=[dkey])
                    src, skey = dst, dkey
                    step *= 2
                    bi ^= 1
                S.op("dve", lambda e, src=src, w=w, pb=pb: e.scalar_tensor_tensor(
                    out=t2[:], in0=src[:, 15:15 + TP], scalar=1.0 / w, in1=pb,
                    op0=ALU.mult, op1=ALU.subtract), reads=[skey, pk], writes=["t2"])
                S.op("dve", lambda e, src=src, g=g: e.tensor_tensor(
                    out=t2[:, M0:M0 + 16], in0=src[:, 15 + M0:15 + M0 + 16],
                    in1=CST[:, C_CORR + g * 16:C_CORR + (g + 1) * 16], op=ALU.mult),
                    reads=[skey, "CST", "t2"], writes=["t2"])
                S.op("dve", lambda e, pb=pb: e.tensor_tensor(
                    out=t2[:, M0:M0 + 16], in0=t2[:, M0:M0 + 16], in1=pb[:, M0:M0 + 16],
                    op=ALU.subtract), reads=[pk, "t2"], writes=["t2"])
                hp = hsP[pb_].rearrange("p (b r) -> p b r", r=15)
                ps_s = pb[:, S0:S0 + NS]
                S.op("dve", lambda e, hp=hp, w=w: e.tensor_reduce(
                    out=st16[:, 0:NS], in_=hp[:, :, 15 - (w - 1):15], op=ALU.add,
                    axis=mybir.AxisListType.X), reads=[("hsP", pb_)], writes=["st16"])
                S.op("dve", lambda e, ps_s=ps_s: e.tensor_tensor(
                    out=st16[:, 0:NS], in0=st16[:, 0:NS], in1=ps_s, op=ALU.add),
                    reads=[pk, "st16"], writes=["st16"])
                S.op("dve", lambda e, ps_s=ps_s, w=w: e.scalar_tensor_tensor(
                    out=t2[:, S0:S0 + NS], in0=st16[:, 0:NS], scalar=1.0 / w, in1=ps_s,
                    op0=ALU.mult, op1=ALU.subtract), reads=["st16", pk, "t2"], writes=["t2"])
                S.op("act", lambda e, cc=cc: e.activation(out=plb[:, cc * TP:(cc + 1) * TP], in_=t2[:], func=AF.Copy),
                     reads=["t2"], writes=["plb"])
                ob = 0
                S.op("act", lambda e, ob=ob, pb=pb: e.activation(out=oP[ob][:, 0:15], in_=pb[:, S0 - 15:S0], func=AF.Copy),
                     reads=[pk], writes=[("oP", ob)])
                opv = oP[ob][:, 15:15 + NS * 15].rearrange("p (b r) -> p b r", r=15)
                S.op("act", lambda e, opv=opv, hp=hp: e.activation(out=opv[:, :, 0:14], in_=hp[:, :, 1:15], func=AF.Copy),
                     reads=[("hsP", pb_)], writes=[("oP", ob)])
                S.op("act", lambda e, opv=opv, ps_s=ps_s: e.activation(out=opv[:, :, 14], in_=ps_s, func=AF.Copy),
                     reads=[pk], writes=[("oP", ob)])
                S.op("act", lambda e, ob=ob, c=c: e.dma_start(out=pool_out[i, c * 128:(c + 1) * 128, :], in_=oP[ob]),
                     reads=[("oP", ob)], dma=f"oP{ob}")
                if pos + 2 < 8:
                    load_hsP(pos + 2)

            def b_yb(q):
                g = gorder[q]
                if wpu[0] is None:
                    wpu[0] = W.get(w_pool[i].rearrange("g (k p) d -> p (g k) d", p=128), dest=wpb)
                u, key = wpu[0]
                uv = u.rearrange("p (a d) -> p a d", d=256)
                for mo in range(2):
                    c = 2 * g + mo
                    s_ = mm_group([uv[:, g * 2 + kk, mo * 128:(mo + 1) * 128] for kk in range(2)],
                                  [plb[:, kk * TP:(kk + 1) * TP] for kk in range(2)], reads=[key, "plb"])
                    S.op("act", lambda e, s_=s_, c=c: e.activation(
                        out=v3(at_(c % 4)), in_=ps3(s_), func=AF.Identity, scale=pvc(f"psc{i}", c)),
                        reads=[("ps", s_), "PV"], writes=["AT"])
                if q % 2 == 1:
                    rowproj(w_out_even[i], 1024 + (g // 2) * 512)

            def load_hsP(pos):
                c = chunks[pos][0]
                S.op("act", lambda e, c=c, pos=pos: e.dma_start(
                    out=hsP[pos % 2], in_=spool_in[i, c * 128:(c + 1) * 128].rearrange("p b r -> p (b r)")),
                    writes=[("hsP", pos % 2)], dma=f"hsP{pos % 2}")
            load_hsP(0)
            load_hsP(1)
            b_front(0)
            b_front(1)
            b_back(0)
            b_back(1)
            for q in range(1, 4):
                b_front(2 * q)
                b_front(2 * q + 1)
                b_yb(q - 1)
                b_back(2 * q)
                b_back(2 * q + 1)
            b_yb(3)

        def ln_accum(c):
            if c == 0:
                S.op("dve", lambda e: e.tensor_copy(out=acc1[:], in_=ln_(0)), reads=["LNB"], writes=["acc1"])
                S.op("dve", lambda e: e.tensor_tensor(out=acc2[:], in0=ln_(0), in1=ln_(0), op=ALU.mult),
                     reads=["LNB"], writes=["acc2"])
            else:
                S.op("pool", lambda e, c=c: e.tensor_tensor(out=acc1[:], in0=acc1[:], in1=ln_(c), op=ALU.add),
                     reads=["LNB", "acc1"], writes=["acc1"])
                S.op("dve", lambda e, c=c: e.tensor_tensor(out=t3[:], in0=ln_(c), in1=ln_(c), op=ALU.mult),
                     reads=["LNB"], writes=["t3"])
                S.op("dve", lambda e: e.tensor_tensor(out=acc2[:], in0=acc2[:], in1=t3[:], op=ALU.add),
                     reads=["t3", "acc2"], writes=["acc2"])

        def ln_stats():
            s1 = stats_broadcast(acc1, "acc1")
            S.op("dve", lambda e, s1=s1: e.tensor_scalar(out=v3(acc1[:]), in0=ps3(s1), scalar1=1.0 / 1024,
                                                         scalar2=None, op0=ALU.mult),
                 reads=[("ps", s1)], writes=["acc1"])
            s2 = stats_broadcast(acc2, "acc2")
            S.op("dve", lambda e: e.tensor_tensor(out=t3[:], in0=acc1[:], in1=acc1[:], op=ALU.mult),
                 reads=["acc1"], writes=["t3"])
            S.op("dve", lambda e, s2=s2: e.scalar_tensor_tensor(
                out=v3(rstd[:]), in0=ps3(s2), scalar=1.0 / 1024, in1=v3(t3[:]),
                op0=ALU.mult, op1=ALU.subtract), reads=[("ps", s2), "t3"], writes=["acc2"])
            S.op("dve", lambda e: e.tensor_scalar(out=rstd[:], in0=rstd[:], scalar1=0.0, scalar2=EPS,
                                                  op0=ALU.max, op1=ALU.add), reads=["acc2"], writes=["acc2"])
            S.op("act", lambda e: e.activation(out=rstd[:], in_=rstd[:], func=AF.Ln),
                 reads=["acc2"], writes=["acc2"])
            S.op("act", lambda e: e.activation(out=rstd[:], in_=rstd[:], func=AF.Exp, scale=-0.5),
                 reads=["acc2"], writes=["acc2"])

        def odd_mixer(l):
            i = l // 2
            rmsnorm(f"nmix{l}")
            wi = w_in_odd[i]
            S.op("sp", lambda e: e.dma_start(out=t3[:, 0:1024], in_=wsT_in[i].rearrange("s h t -> s (h t)")),
                 writes=["t3"], dma="t3")
            for h in range(8):
                S.op("dve", lambda e, h=h: e.tensor_tensor(
                    out=wsTb[:, h * 128:(h + 1) * 128], in0=t3[:, h * 128:(h + 1) * 128],
                    in1=CST[:, C_MASK:C_MASK + 128], op=ALU.mult),
                    reads=["t3", "CST"], writes=["wsTb"])
            for c in range(8):
                sv = panel_mm(wi, 1024 + c * 128)
                S.op("act", lambda e, sv=sv, c=c: e.activation(out=v3(ln_(c)), in_=ps3(sv), func=AF.Gelu),
                     reads=[("ps", sv)], writes=["LNB"])
                ln_accum(c)
            ln_stats()

            def load_bsb(h):
                S.op("act", lambda e, h=h: e.dma_start(out=bsbh[:, (h % 2) * 128:(h % 2 + 1) * 128],
                                                       in_=bsb_in[i, :, h * 128:(h + 1) * 128]),
                     writes=[("bsbh", h % 2)], dma=f"bsbh{h % 2}")
            load_bsb(0)
            def v_normalize(h):
                S.op("dve", lambda e, h=h: e.tensor_tensor(out=t2[:], in0=ln_(h), in1=acc1[:], op=ALU.subtract),
                     reads=["LNB", "acc1"], writes=["t2"])
                S.op("dve", lambda e: e.tensor_tensor(out=t2[:], in0=t2[:], in1=rstd[:], op=ALU.mult),
                     reads=["t2", "acc2"], writes=["t2"])
                S.op("act", lambda e, h=h: e.activation(out=ln_(h), in_=t2[:], func=AF.Identity,
                                                        scale=pvc(f"vg{i}", h), bias=pvc(f"vb{i}", h)),
                     reads=["t2", "PV"], writes=["LNB"])
                S.op("act", lambda e, h=h: e.activation(out=oVh[h % 2], in_=t2[:, S0 - 128:S0 + NS], func=AF.Identity,
                                                        scale=pvc(f"vg{i}", h), bias=pvc(f"vb{i}", h)),
                     reads=["t2", "PV"], writes=[("oV", h % 2)])
                S.op("act", lambda e, h=h: e.dma_start(out=chunkv_out[i, h * 128:(h + 1) * 128, :], in_=oVh[h % 2]),
                     reads=[("oV", h % 2)], dma=f"oV{h % 2}")

            v_normalize(0)
            for h in range(8):
                if h + 1 < 8:
                    load_bsb(h + 1)
                def tfn(pe, h=h):
                    last = None
                    for j in range(NCH):
                        last = pe.transpose(AUXB[:, j * 128:(j + 1) * 128],
                                            ln_(h, G0 + j * 128, G0 + (j + 1) * 128), identb[:])
                    return last
                S.op("pe", tfn, reads=["LNB", "identb"], writes=["auxb"])
                S.op("act", lambda e: e.activation(out=vT[:], in_=AUXB[:, 0:NCH * 128], func=AF.Copy),
                     reads=["auxb"], writes=["vT"])
                su = panel_mm(wi, h * 128)
                S.op("act", lambda e, su=su: e.activation(out=v3(t1[:]), in_=ps3(su), func=AF.Gelu),
                     reads=[("ps", su)], writes=["t1"])
                sgt = next_slot()

                def gfn(pe, h=h, sgt=sgt):
                    last = None
                    for j in range(NCH):
                        o = PS[sgt][:, (j // 3) * 512 + (j % 3) * 128:(j // 3) * 512 + (j % 3) * 128 + 128]
                        last = pe.matmul(o, lhsT=vT[:, j * 128:(j + 1) * 128],
                                         rhs=wsTb[:, h * 128:(h + 1) * 128],
                                         start=True, stop=True)
                    return last
                S.op("pe", gfn, reads=["vT", "wsTb"], writes=[("ps", sgt)])
                if h + 1 < 8:
                    v_normalize(h + 1)
                bs_h = bsbh[:, (h % 2) * 128:(h % 2 + 1) * 128]
                for jb in range(3):
                    gv_ = PS[sgt][:, jb * 512:jb * 512 + 384].rearrange("p (j t) -> p j t", t=128)
                    for jj in range(3):
                        j = jb * 3 + jj
                        c0 = G0 + j * 128
                        S.op("dve", lambda e, gv_=gv_, jj=jj, bs_h=bs_h, c0=c0: e.tensor_tensor(
                            out=t3[:, c0:c0 + 128], in0=gv_[:, jj, :], in1=bs_h, op=ALU.add),
                            reads=[("ps", sgt), ("bsbh", h % 2)], writes=["t3"])
                S.op("dve", lambda e, h=h: e.tensor_tensor(
                    out=at_(h % 4, G0, S0), in0=t3[:, G0:S0], in1=t1[:, G0:S0], op=ALU.mult),
                    reads=["t3", "t1"], writes=["AT"])
                S.op("dve", lambda e, h=h: e.tensor_scalar(
                    out=st16[:, 16:16 + NS], in0=ln_(h, S0, S0 + NS), scalar1=pvc(f"ws00{i}", h),
                    scalar2=pvc(f"bs0{i}", h), op0=ALU.mult, op1=ALU.add),
                    reads=["LNB", "PV"], writes=["st16"])
                S.op("dve", lambda e, h=h: e.tensor_tensor(
                    out=at_(h % 4, S0, S0 + NS), in0=st16[:, 16:16 + NS], in1=t1[:, S0:S0 + NS], op=ALU.mult),
                    reads=["st16", "t1"], writes=["AT"])
                if h % 4 == 3:
                    rowproj(w_out_odd[i], (h // 4) * 512)
            S.op("dve", lambda e: e.memset(hxb[:, 0:30], 0.0), writes=["hx"])

            def load_hsC(c):
                S.op("act", lambda e, c=c: e.dma_start(
                    out=hsC[c % 2], in_=sconf_in[i, c * 128:(c + 1) * 128].rearrange("p b r -> p (b r)")),
                    writes=[("hsC", c % 2)], dma=f"hsC{c % 2}")
            load_hsC(0)
            T2C0 = 2 * NT

            def d_gate(c):
                sgg = panel_mm(wi, 3072 + c * 128)
                S.op("act", lambda e, sgg=sgg: e.activation(out=v3(t1[:]), in_=ps3(sgg), func=AF.Sigmoid),
                     reads=[("ps", sgg)], writes=["t1"])

            def d_glu(c):
                sa_ = panel_mm(wi, 2048 + c * 128)
                S.op("dve", lambda e, sa_=sa_: e.tensor_tensor(
                    out=v3(hxb[:, 30:30 + TP]), in0=ps3(sa_), in1=v3(t1[:]), op=ALU.mult),
                    reads=[("ps", sa_), "t1"], writes=["hx"])
                S.op("dve", lambda e, sa_=sa_: e.tensor_tensor(
                    out=gtail, in0=PS[sa_][:, 1024 + (S0 - 30 - T2C0):1024 + (S0 + NS - T2C0)],
                    in1=t1[:, S0 - 30:S0 + NS], op=ALU.mult),
                    reads=[("ps", sa_), "t1"], writes=["gtail"])

            d_gate(0)
            d_glu(0)
            for c in range(8):
                hb = c % 2
                if c + 1 < 8:
                    load_hsC(c + 1)
                    d_gate(c + 1)
                wc = lambda j, c=c: pvc(f"wcd{i}", c * 31 + j)
                for j in range(31):
                    S.op("dve", lambda e, j=j, wc=wc: e.tensor_scalar(
                        out=AT[:, j * 128:(j + 1) * 128], in0=identb[:], scalar1=wc(j), scalar2=None,
                        op0=ALU.mult), reads=["identb", "PV"], writes=[("dg", j)], extra_wait=["AT"])
                sc = next_slot()

                def cfn(pe, sc=sc, c0=C0[0]):
                    last = None
                    for j in range(31):
                        for n in range(3):
                            a = c0 if n == 0 else 0
                            e = PADC if n == 2 else 0
                            last = pe.matmul(PS[sc][:, n * 512 + a:n * 512 + NT - e], lhsT=AT[:, j * 128:(j + 1) * 128],
                                             rhs=hxb[:, j + n * NT + a:j + (n + 1) * NT - e],
                                             start=(j == 0), stop=(j == 30))
                    return last
                S.op("pe", cfn, reads=["AT", "hx"] + [("dg", j) for j in range(31)], writes=[("ps", sc)])
                S.op("act", lambda e, sc=sc, c=c: e.activation(out=v3(ln_(c)), in_=ps3(sc), func=AF.Identity,
                                                              bias=pvc(f"bcd{i}", c)),
                     reads=[("ps", sc), "PV"], writes=["LNB"])
                hc = hsC[hb].rearrange("p (b r) -> p b r", r=30)
                gl_s = gtail[:, 30:30 + NS]
                wrow = PV[:, PVO[f"wcd{i}"] + c * 31:PVO[f"wcd{i}"] + c * 31 + 30]
                for b in range(NS):
                    S.op("dve", lambda e, b=b, hc=hc, wrow=wrow: e.tensor_tensor(
                        out=t3[:, b * 30:(b + 1) * 30], in0=hc[:, b, :], in1=wrow, op=ALU.mult),
                        reads=[("hsC", hb), "PV"], writes=["t3"])
                S.op("dve", lambda e: e.tensor_reduce(
                    out=st16[:, 32:32 + NS], in_=t3[:, 0:NS * 30].rearrange("p (b r) -> p b r", r=30),
                    op=ALU.add, axis=mybir.AxisListType.X), reads=["t3"], writes=["st16"])
                S.op("dve", lambda e, c=c, wc=wc, gl_s=gl_s: e.tensor_scalar(
                    out=st16[:, 48:48 + NS], in0=gl_s, scalar1=wc(30), scalar2=pvc(f"bcd{i}", c),
                    op0=ALU.mult, op1=ALU.add), reads=["gtail", "PV"], writes=["st16b"])
                S.op("dve", lambda e, c=c: e.tensor_tensor(
                    out=ln_(c, S0, S0 + NS), in0=st16[:, 48:48 + NS], in1=st16[:, 32:32 + NS], op=ALU.add),
                    reads=["st16", "st16b"], writes=["LNB"])
                ln_accum(c)
                ob = 0
                S.op("act", lambda e, ob=ob: e.activation(out=oC[ob][:, 0:30], in_=gtail[:, 0:30], func=AF.Copy),
                     reads=["gtail"], writes=[("oC", ob)])
                ocv = oC[ob][:, 30:30 + NS * 30].rearrange("p (b r) -> p b r", r=30)
                S.op("act", lambda e, ocv=ocv, hc=hc: e.activation(out=ocv[:, :, 0:29], in_=hc[:, :, 1:30], func=AF.Copy),
                     reads=[("hsC", hb)], writes=[("oC", ob)])
                S.op("act", lambda e, ocv=ocv, gl_s=gl_s: e.activation(out=ocv[:, :, 29], in_=gl_s, func=AF.Copy),
                     reads=["gtail"], writes=[("oC", ob)])
                S.op("act", lambda e, ob=ob, c=c: e.dma_start(out=conf_out[i, c * 128:(c + 1) * 128, :], in_=oC[ob]),
                     reads=[("oC", ob)], dma=f"oC{ob}")
                if c + 1 < 8:
                    d_glu(c + 1)
            ln_stats()
            tb = [(t2, "t2"), (t3, "t3"), (t2, "t2"), (t3, "t3"), (t1, "t1"), (hx[:, 0:TP], "hx"), (t2, "t2"), (t3, "t3")]

            def yd_dve(c):
                tt, tk = tb[c]
                S.op("dve", lambda e, c=c, tt=tt: e.tensor_tensor(out=tt[:], in0=ln_(c), in1=acc1[:], op=ALU.subtract),
                     reads=["LNB", "acc1"], writes=[tk])
                S.op("dve", lambda e, tt=tt: e.tensor_tensor(out=tt[:], in0=tt[:], in1=rstd[:], op=ALU.mult),
                     reads=[tk, "acc2"], writes=[tk])

            def yd_act(c):
                tt, tk = tb[c]
                S.op("act", lambda e, c=c, tt=tt: e.activation(out=at_(c % 4), in_=tt[:], func=AF.Silu,
                                                               scale=pvc(f"cg{i}", c), bias=pvc(f"cb{i}", c)),
                     reads=[tk, "PV"], writes=["AT"])
            for c in range(4):
                yd_dve(c)
                yd_act(c)
            for c in range(4, 8):
                yd_dve(c)
            rowproj(w_out_odd[i], 1024)
            for c in range(4, 8):
                yd_act(c)
            rowproj(w_out_odd[i], 1024 + 512)
            for k in range(KC):
                S.op("dve", lambda e, k=k: e.tensor_scalar(
                    out=x_(k, 0, HALO), in0=x_(k, 0, HALO), scalar1=CST[:, C_HM:C_HM + 1], scalar2=None,
                    op0=ALU.mult), reads=[("x", k), "CST"], writes=[("x", k)])

        def program():
            for l in range(n_layers):
                cur_layer[0] = l
                C0[0] = CIN[l]
                if l % 2 == 0:
                    even_mixer(l)
                else:
                    odd_mixer(l)
                ffn(l)
            rmsnorm("nfin", final=True)

        S.dry = True
        program()
        S.dry = False
        slot_ctr[0] = 0
        program()
        S.emit(nc)
    return nc


def _host_inputs(inp):
    f = np.float32
    g = {k: np.asarray(v) for k, v in inp.items()}
    pv = np.zeros((128, NPV), f)

    def put(name, rows):
        rows = np.asarray(rows, f)
        pv[:, PVO[name]:PVO[name] + rows.shape[0]] = rows.T
    for l in range(4):
        put(f"nmix{l}", g["norm_mix"][l].reshape(16, 128))
        put(f"nffn{l}", g["norm_ffn"][l].reshape(16, 128))
    put("nfin", g["norm_final"].reshape(16, 128))
    for i in range(2):
        put(f"wca{i}", g["w_conv_a"][i].reshape(3, 8, 128).transpose(1, 0, 2).reshape(24, 128))
        put(f"psc{i}", g["pool_scale"][i].reshape(8, 128))
        put(f"vg{i}", g["v_norm_g"][i].reshape(8, 128))
        put(f"vb{i}", g["v_norm_b"][i].reshape(8, 128))
        put(f"bcd{i}", g["b_conv_d"][i].reshape(8, 128))
        put(f"cg{i}", g["conf_norm_g"][i].reshape(8, 128))
        put(f"cb{i}", g["conf_norm_b"][i].reshape(8, 128))
        put(f"wcd{i}", g["w_conv_d"][i].reshape(31, 8, 128).transpose(1, 0, 2).reshape(248, 128))
        put(f"ws00{i}", np.broadcast_to(g["w_spatial"][i, :, 0, 0][:, None], (8, 128)))
        put(f"bs0{i}", np.broadcast_to(g["b_spatial"][i, :, 0][:, None], (8, 128)))
    wsT = np.ascontiguousarray(g["w_spatial"].transpose(0, 3, 1, 2)).astype(f)
    bsb = np.ascontiguousarray(np.broadcast_to(g["b_spatial"].reshape(2, 1, 1024), (2, 128, 1024))).astype(f)
    shared = dict(pv=pv, wsT=wsT, bsb=bsb)
    for k in ("w_in_even", "w_pool", "w_out_even", "w_in_odd", "w_out_odd",
              "w_ffn_gate", "w_ffn_up", "w_ffn_down"):
        shared[k] = np.ascontiguousarray(g[k], dtype=f)
    xp, xs = g["x_prompt"], g["x_sample"]
    maps = []
    ss = np.arange(128)
    for c in range(8):
        b, half = c // 2, c % 2
        xT = np.zeros((D, T), f)
        if half:
            xT[:, 0:HALO] = xp[b, MAIN - HALO:MAIN].T
        xT[:, M0:S0] = xp[b, half * MAIN:(half + 1) * MAIN].T
        xT[:, S0:T] = xs[c * NS:(c + 1) * NS, 0].T
        cst = np.zeros((128, NCST), f)
        cst[:, C_ID:C_ID + 128] = np.eye(128, dtype=f)
        cst[:, C_MASK:C_MASK + 128] = (ss[:, None] <= ss[None, :]).astype(f)
        cst[:, C_HM] = float(half)
        for gi, w in enumerate((2, 4, 8, 16)):
            pos = half * MAIN + np.arange(16)
            cst[:, C_CORR + gi * 16:C_CORR + (gi + 1) * 16] = (1.0 / np.minimum(w, pos + 1))[None, :]
        m = dict(shared)
        m["xT"] = xT
        m["cst"] = cst
        sl = slice(c * NS, (c + 1) * NS)
        m["sconv"] = np.ascontiguousarray(g["state_conv_a"][:, sl].transpose(0, 3, 1, 2)).astype(f)
        m["spool"] = np.ascontiguousarray(g["state_pool"][:, sl].transpose(0, 3, 1, 2)).astype(f)
        m["sconf"] = np.ascontiguousarray(g["state_conformer"][:, sl].transpose(0, 3, 1, 2)).astype(f)
        maps.append(m)
    return maps


def _assemble(res):
    f = np.float32
    y_prompt = np.zeros((4, 2048, D), f)
    y_sample = np.zeros((128, 1, D), f)
    conv_a_prompt = np.zeros((2, 4, 2, 1024), f)
    conv_a_sample = np.zeros((2, 128, 2, 1024), f)
    pool_prompt = np.zeros((2, 4, 15, 1024), f)
    pool_sample = np.zeros((2, 128, 15, 1024), f)
    chunk_v_prompt = np.zeros((2, 4, 128, 1024), f)
    chunk_v_sample = np.zeros((2, 128, 1, 1024), f)
    conformer_prompt = np.zeros((2, 4, 30, 1024), f)
    conformer_sample = np.zeros((2, 128, 30, 1024), f)
    for c in range(8):
        r = res[c]
        b, half = c // 2, c % 2
        sl = slice(c * NS, (c + 1) * NS)
        yT = np.asarray(r["yT"])
        y_prompt[b, half * MAIN:(half + 1) * MAIN] = yT[:, :MAIN].T
        y_sample[sl, 0] = yT[:, MAIN:].T
        ca, po, cv, cf = (np.asarray(r[k]) for k in ("conva", "pool", "chunkv", "conf"))
        conv_a_sample[:, sl] = ca[:, :, 2:].reshape(2, 1024, NS, 2).transpose(0, 2, 3, 1)
        pool_sample[:, sl] = po[:, :, 15:].reshape(2, 1024, NS, 15).transpose(0, 2, 3, 1)
        chunk_v_sample[:, sl, 0] = cv[:, :, 128:].transpose(0, 2, 1)
        conformer_sample[:, sl] = cf[:, :, 30:].reshape(2, 1024, NS, 30).transpose(0, 2, 3, 1)
        if half:
            conv_a_prompt[:, b] = ca[:, :, 0:2].transpose(0, 2, 1)
            pool_prompt[:, b] = po[:, :, 0:15].transpose(0, 2, 1)
            chunk_v_prompt[:, b] = cv[:, :, 0:128].transpose(0, 2, 1)
            conformer_prompt[:, b] = cf[:, :, 0:30].transpose(0, 2, 1)
    return (y_prompt, y_sample, conv_a_prompt, conv_a_sample, pool_prompt, pool_sample,
            chunk_v_prompt, chunk_v_sample, conformer_prompt, conformer_sample)


_NC_CACHE = {}


def kernel(**inputs):
    maps = _host_inputs(inputs)
    if "nc" not in _NC_CACHE:
        _NC_CACHE["nc"] = build_program()
    nc = _NC_CACHE["nc"]
    res = run_bass_kernel_spmd(nc, maps, core_ids=list(range(8)))
    return _assemble(res.results)
```
